# Optimizing a Trainium2 kernel written in Bass

```python
import jax, jax.numpy as jnp
from jax import lax
import numpy as np

D_MODEL = 2048
BATCH = 4
SEQ = 2048
DEPTH = 1
DEC_BATCH = 128
DEC_SEQ = 4
PAST_LEN = 16384
PAGE_SIZE = 128

D_MIX = D_MODEL
D_CONV = D_MIX // 2
N_CONV_GROUPS = 8
CONV_A_WIDTH = 3
DN_HEADS = 8
DN_DK = 128
DN_DV = (D_MIX - D_CONV) // DN_HEADS
DN_QK = DN_HEADS * DN_DK
DN_V = DN_HEADS * DN_DV
DN_CONV_WIDTH = 4
DN_QKV = 2 * DN_QK + DN_V
CHUNK = 64
D_FF = 5632
D_PLE = 256
EPS = 1e-6
COL_SIZES = [D_CONV, D_CONV, D_CONV, DN_QK, DN_QK, DN_V, DN_V, DN_HEADS, DN_HEADS]
COL_OFFSETS = [int(o) for o in np.cumsum(COL_SIZES)[:-1]]
IN_COLS = int(sum(COL_SIZES))

kernel_name = "hymba_conv_gdn_macaron_step"


def rmsnorm(x, g):
    xf = x.astype(jnp.float32)
    y = xf * lax.rsqrt(jnp.mean(xf * xf, axis=-1, keepdims=True) + EPS)
    return (y * g.astype(jnp.float32)).astype(x.dtype)


def l2norm(x):
    return x * lax.rsqrt(jnp.sum(x * x, axis=-1, keepdims=True) + EPS)


def swiglu(x, wg, wu, wd):
    return (jax.nn.silu(x @ wg) * (x @ wu)) @ wd


def causal_dwconv(u, buf, w):
    T = u.shape[1]
    W = w.shape[0]
    full = jnp.concatenate([buf.astype(u.dtype), u], axis=1)
    y = full[:, 0:T] * w[0]
    for j in range(1, W):
        y = y + full[:, j:j + T] * w[j]
    return y, full[:, -(W - 1):]


def gated_delta_chunked(q, k, v, g, beta, s0):
    Bn, T, H, Dk = q.shape
    Dv = v.shape[-1]
    C = CHUNK if T >= CHUNK else T
    pad = (-T) % C
    if pad:
        pw = ((0, 0), (0, pad), (0, 0), (0, 0))
        q, k, v = jnp.pad(q, pw), jnp.pad(k, pw), jnp.pad(v, pw)
        g, beta = jnp.pad(g, pw[:3]), jnp.pad(beta, pw[:3])
    N = (T + pad) // C

    def chunks(a):
        a = a.reshape((Bn, N, C, H) + a.shape[3:])
        return jnp.moveaxis(a, (1, 3), (0, 2))

    qc, kc, vc = chunks(q), chunks(k), chunks(v)
    gc = jnp.cumsum(chunks(g), axis=-1)
    bc = chunks(beta)
    causal = jnp.tril(jnp.ones((C, C), dtype=bool))
    strict = jnp.tril(jnp.ones((C, C), dtype=bool), -1)
    decay = jnp.exp(jnp.where(causal, gc[..., :, None] - gc[..., None, :], -jnp.inf))
    kb = kc * bc[..., None]
    a_mat = jnp.where(strict, jnp.einsum('nbhid,nbhjd->nbhij', kb, kc) * decay, 0.0)
    eye = jnp.eye(C, dtype=jnp.float32)
    t_mat = lax.linalg.triangular_solve(a_mat + eye, jnp.broadcast_to(eye, a_mat.shape),
                                        left_side=True, lower=True, unit_diagonal=True)
    w = jnp.einsum('nbhij,nbhjd->nbhid', t_mat, kb * jnp.exp(gc)[..., None])
    u = jnp.einsum('nbhij,nbhjd->nbhid', t_mat, vc * bc[..., None])
    qk = jnp.where(causal, jnp.einsum('nbhid,nbhjd->nbhij', qc, kc) * decay, 0.0)

    def step(S, inp):
        q_i, k_i, u_i, w_i, qk_i, g_i = inp
        v_new = u_i - jnp.einsum('bhck,bhkv->bhcv', w_i, S)
        o_i = (jnp.einsum('bhck,bhkv->bhcv', q_i * jnp.exp(g_i)[..., None], S)
               + jnp.einsum('bhij,bhjv->bhiv', qk_i, v_new))
        g_last = g_i[..., -1:]
        S = (S * jnp.exp(g_last)[..., None]
             + jnp.einsum('bhck,bhcv->bhkv', k_i * jnp.exp(g_last - g_i)[..., None], v_new))
        return S, o_i

    s_fin, o = lax.scan(step, s0, (qc, kc, u, w, qk, gc))
    o = jnp.moveaxis(o, (0, 2), (1, 3)).reshape(Bn, N * C, H, Dv)[:, :T]
    return o, s_fin


def mixing(h, conv_a_buf, qkv_buf, s0, w_in, conv_a_w, conv_qkv_w, a_log, dt_bias, dn_norm, w_out):
    Bn, T, _ = h.shape
    proj = h @ w_in
    gb, gcv, hc, q, k, v, z, a, b = jnp.split(proj, COL_OFFSETS, axis=-1)
    cu, conv_a_new = causal_dwconv(gcv * hc, conv_a_buf, conv_a_w)
    y_a = gb * cu
    cqkv, qkv_new = causal_dwconv(jnp.concatenate([q, k, v], axis=-1), qkv_buf, conv_qkv_w)
    cqkv = jax.nn.silu(cqkv).astype(jnp.float32)
    q, k, v = jnp.split(cqkv, [DN_QK, 2 * DN_QK], axis=-1)
    q = l2norm(q.reshape(Bn, T, DN_HEADS, DN_DK)) * (DN_DK ** -0.5)
    k = l2norm(k.reshape(Bn, T, DN_HEADS, DN_DK))
    v = v.reshape(Bn, T, DN_HEADS, DN_DV)
    g = -jnp.exp(a_log.astype(jnp.float32)) * jax.nn.softplus(
        a.astype(jnp.float32) + dt_bias.astype(jnp.float32))
    beta = jax.nn.sigmoid(b.astype(jnp.float32))
    o, s_new = gated_delta_chunked(q, k, v, g, beta, s0.astype(jnp.float32))
    o = rmsnorm(o, dn_norm) * jax.nn.silu(z.astype(jnp.float32).reshape(Bn, T, DN_HEADS, DN_DV))
    y_b = o.reshape(Bn, T, DN_V).astype(h.dtype)
    out = jnp.concatenate([y_a, y_b], axis=-1) @ w_out
    return out, conv_a_new, qkv_new, s_new.astype(h.dtype)


def layer_forward(x, p, conv_a_buf, qkv_buf, s0, lw):
    (f1_pre, f1_post, f1_wg, f1_wu, f1_wd,
     mix_pre, mix_post, w_in, conv_a_w, conv_qkv_w, a_log, dt_bias, dn_norm, w_out,
     f2_pre, f2_post, f2_wg, f2_wu, f2_wd,
     ple_pre, ple_post, w_ple_gate, w_ple_proj) = lw
    x = x + 0.5 * rmsnorm(swiglu(rmsnorm(x, f1_pre), f1_wg, f1_wu, f1_wd), f1_post)
    m, ca, cq, s = mixing(rmsnorm(x, mix_pre), conv_a_buf, qkv_buf, s0, w_in, conv_a_w,
                          conv_qkv_w, a_log, dt_bias, dn_norm, w_out)
    x = x + rmsnorm(m, mix_post)
    x = x + 0.5 * rmsnorm(swiglu(rmsnorm(x, f2_pre), f2_wg, f2_wu, f2_wd), f2_post)
    gate = jax.nn.sigmoid(rmsnorm(x, ple_pre) @ w_ple_gate)
    x = x + rmsnorm(gate * (p.astype(x.dtype) @ w_ple_proj), ple_post)
    return x, ca, cq, s


def setup_inputs(seed: int = 0) -> dict:
    key = jax.random.key(seed)
    ks = iter(jax.random.split(key, 64))
    nrm = lambda shape, s=1.0: jax.random.normal(next(ks), shape, jnp.float32) * s
    gain = lambda n: 1.0 + 0.02 * jax.random.normal(next(ks), (DEPTH, n), jnp.float32)
    d = {}
    d["x_prompt"] = nrm((BATCH, SEQ, D_MODEL))
    d["x_sample"] = nrm((DEC_BATCH, DEC_SEQ, D_MODEL))
    d["state_conv_a"] = nrm((DEPTH, DEC_BATCH, CONV_A_WIDTH - 1, D_CONV))
    d["state_conv_qkv"] = nrm((DEPTH, DEC_BATCH, DN_CONV_WIDTH - 1, DN_QKV))
    d["state_delta"] = nrm((DEPTH, DEC_BATCH, DN_HEADS, DN_DK, DN_DV), 0.1)
    d["p_prompt"] = nrm((DEPTH, BATCH, SEQ, D_PLE))
    d["p_sample"] = nrm((DEPTH, DEC_BATCH, DEC_SEQ, D_PLE))
    for nm in ("f1",):
        d[nm + "_pre"] = gain(D_MODEL)
        d[nm + "_post"] = gain(D_MODEL)
        d[nm + "_wg"] = nrm((DEPTH, D_MODEL, D_FF), D_MODEL ** -0.5)
        d[nm + "_wu"] = nrm((DEPTH, D_MODEL, D_FF), D_MODEL ** -0.5)
        d[nm + "_wd"] = nrm((DEPTH, D_FF, D_MODEL), D_FF ** -0.5)
    d["mix_pre"] = gain(D_MODEL)
    d["mix_post"] = gain(D_MODEL)
    d["w_in"] = nrm((DEPTH, D_MODEL, IN_COLS), D_MODEL ** -0.5)
    d["conv_a_w"] = nrm((DEPTH, CONV_A_WIDTH, D_CONV), CONV_A_WIDTH ** -0.5)
    d["conv_qkv_w"] = nrm((DEPTH, DN_CONV_WIDTH, DN_QKV), DN_CONV_WIDTH ** -0.5)
    d["a_log"] = jnp.log(jax.random.uniform(next(ks), (DEPTH, DN_HEADS), jnp.float32, 1.0, 16.0))
    d["dt_bias"] = nrm((DEPTH, DN_HEADS), 0.1)
    d["dn_norm"] = gain(DN_DV)
    d["w_out"] = nrm((DEPTH, D_MIX, D_MODEL), D_MIX ** -0.5)
    for nm in ("f2",):
        d[nm + "_pre"] = gain(D_MODEL)
        d[nm + "_post"] = gain(D_MODEL)
        d[nm + "_wg"] = nrm((DEPTH, D_MODEL, D_FF), D_MODEL ** -0.5)
        d[nm + "_wu"] = nrm((DEPTH, D_MODEL, D_FF), D_MODEL ** -0.5)
        d[nm + "_wd"] = nrm((DEPTH, D_FF, D_MODEL), D_FF ** -0.5)
    d["ple_pre"] = gain(D_MODEL)
    d["ple_post"] = gain(D_MODEL)
    d["w_ple_gate"] = nrm((DEPTH, D_MODEL, D_MODEL), D_MODEL ** -0.5)
    d["w_ple_proj"] = nrm((DEPTH, D_PLE, D_MODEL), D_PLE ** -0.5)
    return d


def reference(x_prompt, x_sample, state_conv_a, state_conv_qkv, state_delta, p_prompt, p_sample,
              f1_pre, f1_post, f1_wg, f1_wu, f1_wd,
              mix_pre, mix_post, w_in, conv_a_w, conv_qkv_w, a_log, dt_bias, dn_norm, w_out,
              f2_pre, f2_post, f2_wg, f2_wu, f2_wd,
              ple_pre, ple_post, w_ple_gate, w_ple_proj):
    yp, ys = x_prompt, x_sample
    bp = x_prompt.shape[0]
    ca_p, cq_p, s_p, ca_s, cq_s, s_s = [], [], [], [], [], []
    for i in range(DEPTH):
        lw = (f1_pre[i], f1_post[i], f1_wg[i], f1_wu[i], f1_wd[i],
              mix_pre[i], mix_post[i], w_in[i], conv_a_w[i], conv_qkv_w[i], a_log[i], dt_bias[i],
              dn_norm[i], w_out[i],
              f2_pre[i], f2_post[i], f2_wg[i], f2_wu[i], f2_wd[i],
              ple_pre[i], ple_post[i], w_ple_gate[i], w_ple_proj[i])
        zero_a = jnp.zeros((bp, CONV_A_WIDTH - 1, D_CONV), yp.dtype)
        zero_q = jnp.zeros((bp, DN_CONV_WIDTH - 1, DN_QKV), yp.dtype)
        zero_s = jnp.zeros((bp, DN_HEADS, DN_DK, DN_DV), jnp.float32)
        yp, a1, q1, s1 = layer_forward(yp, p_prompt[i], zero_a, zero_q, zero_s, lw)
        ys, a2, q2, s2 = layer_forward(ys, p_sample[i], state_conv_a[i], state_conv_qkv[i],
                                       state_delta[i], lw)
        ca_p.append(a1); cq_p.append(q1); s_p.append(s1)
        ca_s.append(a2); cq_s.append(q2); s_s.append(s2)
    return (yp, ys, jnp.stack(ca_p), jnp.stack(cq_p), jnp.stack(s_p),
            jnp.stack(ca_s), jnp.stack(cq_s), jnp.stack(s_s))
```

```python
import numpy as np
from contextlib import ExitStack
import concourse.bass as bass
import concourse.mybir as mybir
from concourse.bass_utils import run_bass_kernel_spmd

F32 = mybir.dt.float32
BF16 = mybir.dt.bfloat16
ALU = mybir.AluOpType
AF = mybir.ActivationFunctionType

D = 2048
KC = 16
DFF = 5632
JC = 44
NG = 2
NP = 1024
NSQ = 8
NSB = 2
NSEQ = NSQ * NSB
NS = NSEQ * 4
NT = NP + NS + 1
TGW = NT // 3
NCH = NP // 128
NCK = NCH + NSB
EPS = 1e-6
NEG = -1.0e5


class Buf:
    __slots__ = ("last_w", "reads", "sem", "semval", "excl")

    def __init__(self, excl=False):
        self.excl = excl
        self.last_w = None
        self.reads = []
        self.sem = None
        self.semval = 0


class Prog:
    ENGS = ("pe", "act", "dve", "pool", "sp")

    def __init__(self, nc):
        self.nc = nc
        self.ops = {e: [] for e in self.ENGS}
        self.sems = {}
        self.cnt = {e: 0 for e in self.ENGS}
        self.seen = {e: {} for e in self.ENGS}
        for e in self.ENGS:
            self.sems[e] = nc.alloc_semaphore("s_" + e)
        self.ndsem = 0
        self.dma_events = {}

    def _deps(self, eng, reads, writes):
        need = {}
        for b in reads:
            if b.last_w is not None:
                k, v = b.last_w
                if need.get(k, 0) < v:
                    need[k] = v
        for b in writes:
            if b.last_w is not None:
                k, v = b.last_w
                if need.get(k, 0) < v:
                    need[k] = v
            for (k, v) in b.reads:
                if need.get(k, 0) < v:
                    need[k] = v
        return self._prune(eng, need)

    def _prune(self, eng, need):
        waits = []
        seen = self.seen[eng]
        for k, v in need.items():
            if eng == "pe" and k == "pe":
                continue
            if seen.get(k, 0) < v:
                seen[k] = v
                waits.append((k, v))
        return waits

    def _mark(self, ev, reads, writes):
        for b in reads:
            b.reads.append(ev)
        for b in writes:
            b.last_w = ev
            b.reads = []

    @staticmethod
    def _split(reads, writes):
        ex = [b for b in reads if b.excl]
        if ex:
            return [b for b in reads if not b.excl], list(writes) + ex
        return reads, writes

    def op(self, eng, fn, reads=(), writes=()):
        reads, writes = self._split(reads, writes)
        waits = self._deps(eng, reads, writes)
        self.cnt[eng] += 1
        ev = (eng, self.cnt[eng])
        self._mark(ev, reads, writes)
        self.ops[eng].append((waits, fn, ev))
        return ev

    def group(self, eng, fns, reads=(), writes=()):
        n = len(fns)
        if n == 1:
            return self.op(eng, fns[0], reads, writes)
        reads, writes = self._split(reads, writes)
        waits = self._deps(eng, reads, writes)
        self.ops[eng].append((waits, fns[0], None))
        for fn in fns[1:-1]:
            self.ops[eng].append(((), fn, None))
        self.cnt[eng] += 1
        ev = (eng, self.cnt[eng])
        self._mark(ev, reads, writes)
        self.ops[eng].append(((), fns[-1], ev))
        return ev

    def dma(self, eng, out_ap, in_ap, reads, wbuf):
        writes = (wbuf,)
        waits = self._deps(eng, reads, writes)
        if wbuf.sem is None:
            key = "d%d" % self.ndsem
            self.ndsem += 1
            self.sems[key] = self.nc.alloc_semaphore(key)
            wbuf.sem = key
        wbuf.semval += 16
        ev = (wbuf.sem, wbuf.semval)
        self._mark(ev, reads, writes)
        if eng == "sp":
            self.dma_events[wbuf.sem] = wbuf.semval

        def fn(e, out_ap=out_ap, in_ap=in_ap):
            return e.dma_start(out=out_ap, in_=in_ap)
        self.ops[eng].append((waits, fn, ev))
        return ev

    def barrier(self):
        need = {e: self.cnt[e] for e in ("pe", "act", "dve") if self.cnt[e] > 0}
        need.update(self.dma_events)
        for e in ("pe", "act", "dve", "sp", "pool"):
            n2 = dict(need)
            if e == "pe":
                n2.pop("pe", None)
            waits = self._prune(e, n2)
            if waits:
                self.ops[e].append((waits, None, None))

    def wait_all(self, eng, bufs):
        waits = self._deps(eng, bufs, bufs)
        self.ops[eng].append((waits, None, None))

    def emit(self):
        nc = self.nc
        sems = self.sems
        ops = self.ops

        def run(e, name):
            for waits, fn, ev in ops[name]:
                for (k, v) in waits:
                    e.wait_ge(sems[k], v)
                if fn is None:
                    continue
                ins = fn(e)
                if ev is not None:
                    k, v = ev
                    ins.then_inc(sems[k], 1 if k == name else 16)

        with nc.Block() as block:
            @block.tensor
            def _(e):
                run(e, "pe")

            @block.scalar
            def _(e):
                run(e, "act")

            @block.vector
            def _(e):
                run(e, "dve")

            @block.gpsimd
            def _(e):
                run(e, "pool")

            @block.sync
            def _(e):
                run(e, "sp")


class _Stop(Exception):
    pass


def build_program(stop=None):
    nc = bass.Bass("TRN2", target_bir_lowering=False)
    dt_in = lambda name, shape: nc.dram_tensor(name, list(shape), F32, kind="ExternalInput").ap()
    dt_out = lambda name, shape: nc.dram_tensor(name, list(shape), F32, kind="ExternalOutput").ap()

    xT = dt_in("xT", (NG, D, NT))
    pT = dt_in("pT", (256, NT))
    hist_a = dt_in("hist_a", (8, 128, NSEQ, 2))
    hist_q = dt_in("hist_q", (24, 128, NSEQ, 3))
    s0 = dt_in("s0", (NSEQ, 8, 128, 128))
    carry = dt_in("carry", (128, 1))
    wgu1 = dt_in("wgu1", (JC, D, 256))
    wd1 = dt_in("wd1", (KC, 4, 128, 11 * 128))
    wgu2 = dt_in("wgu2", (JC, D, 256))
    wd2 = dt_in("wd2", (KC, 4, 128, 11 * 128))
    win = dt_in("win", (56, D, 128))
    wa = dt_in("wa", (D, 8))
    wb = dt_in("wb", (D, 8))
    wout = dt_in("wout", (KC, D, 128))
    wpg = dt_in("wpg", (KC, D, 128))
    wpp = dt_in("wpp", (KC, 256, 128))
    gains = dt_in("gains", (128, 8, KC))
    cw_a = dt_in("cw_a", (128, 8, 3))
    cw_q = dt_in("cw_q", (128, 24, 4))
    alog = dt_in("alog", (8,))
    dtb = dt_in("dtb", (8,))
    dnn = dt_in("dnn", (128, 1))
    cmask = dt_in("cmask", (6, 128, 128))
    smask = dt_in("smask", (5, 32, 32))
    seqsel = dt_in("seqsel", (32, NSQ))
    seqmb = dt_in("seqmb", (128, NSQ, 32))

    yT = dt_out("yT", (D, NT))
    o_ca_p = dt_out("o_ca_p", (8, 128, 2))
    o_cq_p = dt_out("o_cq_p", (24, 128, 3))
    o_dl_p = dt_out("o_dl_p", (8, 128, 128))
    o_ca_s = dt_out("o_ca_s", (8, 128, NSEQ, 2))
    o_cq_s = dt_out("o_cq_s", (24, 128, NSEQ, 3))
    o_dl_s = dt_out("o_dl_s", (NSEQ, 8, 128, 128))
    xs_d = nc.dram_tensor("xs_scratch", [D, NT], F32).ap()
    dbg = dt_out("dbg", (D, NT)) if stop is not None else None

    P = Prog(nc)
    es = ExitStack()
    with es:
        sb = lambda n, s, d=F32: es.enter_context(nc.sbuf_tensor(n, list(s), d))
        RA = sb("RA", (128, JC * NT // 2))
        RB = sb("RB", (128, KC * NT))
        WR = sb("WR", (128, 4224), F32)
        RA_x = RA[:, 0:KC * NT].rearrange("p (k n) -> p k n", k=KC)
        RA_act = RA[:].bitcast(BF16).rearrange("p (j n) -> p j n", j=JC)
        RB_y = RB[:].rearrange("p (k n) -> p k n", k=KC)
        RB_bf = RB[:].bitcast(BF16)
        RB_h = RB_bf[:, 0:KC * NT].rearrange("p (k n) -> p k n", k=KC)
        RB_m = RB_bf[:, KC * NT:2 * KC * NT].rearrange("p (k n) -> p k n", k=KC)
        WRb = WR[:].bitcast(BF16)
        RSTD = sb("RSTD", (128, NT))
        TMP1 = sb("TMP1", (128, NT))
        SQ = [sb("SQ%d" % i, (128, NT), BF16) for i in range(2)]
        CARRY = sb("CARRY", (128, 1))
        GAINS = sb("GAINS", (128, 8, KC))
        CWA = sb("CWA", (128, 8, 3))
        CWQ = sb("CWQ", (128, 24, 4))
        ONESB = sb("ONESB", (128, 128), BF16)
        ONES1 = sb("ONES1", (128, 128), BF16)
        ONESF = sb("ONESF", (128, 128))
        NONESF = sb("NONESF", (128, 128))
        IDB = sb("IDB", (128, 128), BF16)
        CM = sb("CM", (128, 6, 128))
        SM = sb("SM", (32, 5, 32))
        SEQSEL = sb("SEQSEL", (32, NSQ))
        SEQMB = sb("SEQMB", (128, NSQ, 32))
        DNN = sb("DNN", (128, 1))
        ALOG = sb("ALOG", (8, 1))
        DTB = sb("DTB", (8, 1))
        NEXPA = sb("NEXPA", (8, 1))
        HISTA = sb("HISTA", (128, 8, 2))
        HISTQ = sb("HISTQ", (128, 24, 3))
        SST = sb("SST", (128, 8, 128))
        ALLPS = es.enter_context(nc.psum_tensor("ALLPS", [128, 8, 512], F32))
        PTt = [ALLPS[:, 0:3, :], ALLPS[:, 3:6, :]]
        PM = [ALLPS[:, 6, :], ALLPS[:, 7, :]]

        B = {}

        def bf(name):
            if name not in B:
                B[name] = Buf()
            return B[name]

        def bk(i):
            name = "BK%d" % i
            if name not in B:
                B[name] = Buf(excl=True)
            return B[name]

        def ptb(a):
            return [bk(3 * a), bk(3 * a + 1), bk(3 * a + 2)]

        def pt3(a, m=128):
            return PTt[a][0:m, :, 0:TGW]

        def v3(ap2d):
            return ap2d.rearrange("p (t n) -> p t n", t=3)

        P.dma("sp", GAINS[:], gains, [], bf("GAINS"))
        P.dma("sp", CWA[:], cw_a, [], bf("CWA"))
        P.dma("sp", CWQ[:], cw_q, [], bf("CWQ"))
        P.dma("sp", CM[:], cmask.rearrange("m p n -> p m n"), [], bf("CM"))
        P.dma("sp", SM[:], smask.rearrange("m p n -> p m n"), [], bf("SM"))
        P.dma("sp", SEQSEL[:], seqsel, [], bf("SEQSEL"))
        P.dma("sp", SEQMB[:], seqmb, [], bf("SEQMB"))
        P.dma("sp", DNN[:], dnn, [], bf("DNN"))
        P.dma("sp", CARRY[:], carry, [], bf("CARRY"))
        P.dma("sp", ALOG[:], alog.rearrange("(h o) -> h o", o=1), [], bf("ALOG"))
        P.dma("sp", DTB[:], dtb.rearrange("(h o) -> h o", o=1), [], bf("DTB"))
        P.op("dve", lambda e: e.memset(ONESB[:], 1.0 / D), [], [bf("ONESB")])
        P.op("dve", lambda e: e.memset(ONES1[:], 1.0), [], [bf("ONES1")])
        P.op("dve", lambda e: e.memset(ONESF[:], 1.0), [], [bf("ONESF")])
        P.op("dve", lambda e: e.memset(NONESF[:], -1.0), [], [bf("NONESF")])
        P.op("dve", lambda e: e.tensor_copy(out=IDB[:], in_=CM[:, 5, :]), [bf("CM")], [bf("IDB")])
        P.op("dve", lambda e: e.memset(HISTA[:], 0.0), [], [bf("HISTA")])
        P.op("dve", lambda e: e.memset(HISTQ[:], 0.0), [], [bf("HISTQ")])
        P.op("dve", lambda e: e.memset(SST[:], 0.0), [], [bf("SST")])
        P.op("act", lambda e: e.activation(out=NEXPA[:], in_=ALOG[:], func=AF.Exp), [bf("ALOG")], [bf("NEXPA")])
        P.op("dve", lambda e: e.tensor_scalar(out=NEXPA[:], in0=NEXPA[:], scalar1=-1.0, scalar2=None, op0=ALU.mult),
             [bf("NEXPA")], [bf("NEXPA")])
        TRI, SFX, NEGI, NEGT, STRICT, IDENT = [CM[:, i, :] for i in range(6)]

        wstate = {"n": 0}

        class WTiles:
            def __init__(self, srcs, slot_elems, nslots, view):
                self.srcs = srcs
                self.view = view
                self.nslots = nslots
                self.slot_elems = slot_elems
                self.issued = 0
                wstate["n"] += 1
                self.tag = "W%d_" % wstate["n"]
                self.prefetch(nslots)

            def slot(self, i):
                s = i % self.nslots
                return self.view(WRb[:, s * self.slot_elems:(s + 1) * self.slot_elems]), bf(self.tag + str(s))

            def prefetch(self, upto):
                while self.issued < min(upto, len(self.srcs)):
                    ap, b = self.slot(self.issued)
                    P.dma("pool", ap, self.srcs[self.issued], [], b)
                    self.issued += 1

            def get(self, i):
                assert self.issued > i
                return self.slot(i)

            def done(self, i):
                self.prefetch(i + self.nslots + 1)

        def stats(src3, nk, ones, pa):
            for k in range(nk):
                q = SQ[k % 2]
                P.op("act", lambda e, k=k, q=q: e.activation(out=q[:], in_=src3[:, k, :], func=AF.Square),
                     [bf("SRC")], [bf("SQ%d" % (k % 2))])
                fns = [lambda e, t=t, k=k, q=q: e.matmul(PTt[pa][:, t, 0:TGW], lhsT=ones[:], rhs=q[:, t * TGW:(t + 1) * TGW],
                                                         start=(k == 0), stop=(k == nk - 1)) for t in range(3)]
                P.group("pe", fns, [bf("SQ%d" % (k % 2)), bf("ONES")], [*ptb(pa)])

        def rstd_from(pa, scale=1.0):
            P.op("act", lambda e: e.activation(out=v3(TMP1[:]), in_=pt3(pa), func=AF.Sqrt, bias=EPSB[:, 0:1], scale=scale),
                 [*ptb(pa), bf("EPSB")], [bf("TMP1")])
            P.op("dve", lambda e: e.reciprocal(out=RSTD[:], in_=TMP1[:]), [bf("TMP1")], [bf("RSTD")])

        EPSB = sb("EPSB", (128, 1))
        P.op("dve", lambda e: e.memset(EPSB[:], EPS), [], [bf("EPSB")])
        P.barrier()

        def prenorm(gi, have_stats=False):
            if have_stats:
                rstd_from(1)
            else:
                stats(RA_x, KC, ONESB, 0)
                rstd_from(0)
            for k in range(KC):
                P.op("dve", lambda e, k=k: e.scalar_tensor_tensor(out=RB_h[:, k, :], in0=RA_x[:, k, :],
                                                                  scalar=GAINS[:, gi, k:k + 1], in1=RSTD[:],
                                                                  op0=ALU.mult, op1=ALU.mult),
                     [bf("SRC"), bf("RSTD"), bf("GAINS")], [bf("HB")])

        def post_residual(y3, gi, coef, res_src, g, last=False, next_stats=False):
            stats(y3, KC, ONESB, 0)
            rstd_from(0)
            XS2 = [RA[:, KC * NT:(KC + 1) * NT], RA[:, (KC + 1) * NT:(KC + 2) * NT]]

            def load(m):
                P.dma("sp", XS2[m % 2], res_src[m * 128:(m + 1) * 128, :], [bf("XS%d" % m)], bf("XST%d" % (m % 2)))
            load(0)
            for m in range(KC):
                xs = XS2[m % 2]
                if m + 1 < KC:
                    load(m + 1)
                P.op("dve", lambda e, m=m: e.scalar_tensor_tensor(out=y3[:, m, :], in0=y3[:, m, :],
                                                                  scalar=GAINS[:, gi, m:m + 1], in1=RSTD[:],
                                                                  op0=ALU.mult, op1=ALU.mult),
                     [bf("SRC"), bf("RSTD"), bf("GAINS")], [bf("SRC")])
                P.op("dve", lambda e, m=m, xs=xs: e.scalar_tensor_tensor(out=RA_x[:, m, :], in0=y3[:, m, :], scalar=coef,
                                                                         in1=xs, op0=ALU.mult, op1=ALU.add),
                     [bf("SRC"), bf("XST%d" % (m % 2))], [bf("XN%d" % m)])
                if next_stats:
                    q = SQ[m % 2]
                    P.op("act", lambda e, m=m, q=q: e.activation(out=q[:], in_=RA_x[:, m, :], func=AF.Square),
                         [bf("XN%d" % m)], [bf("SQ%d" % (m % 2))])
                    fns = [lambda e, t=t, m=m, q=q: e.matmul(PTt[1][:, t, 0:TGW], lhsT=ONESB[:], rhs=q[:, t * TGW:(t + 1) * TGW],
                                                             start=(m == 0), stop=(m == KC - 1)) for t in range(3)]
                    P.group("pe", fns, [bf("SQ%d" % (m % 2))], [*ptb(1)])
                dst = yT[m * 128:(m + 1) * 128, :] if last else xs_d[m * 128:(m + 1) * 128, :]
                P.dma("sp", dst, RA_x[:, m, :], [bf("XN%d" % m)], bf("YOUT%d" % m) if last else bf("XS%d" % m))

        def ffn(wgu, wd):
            t1 = WTiles([wgu[j].rearrange("(k p) n -> p k n", p=128) for j in range(JC)], 4096, 2,
                        lambda ap: ap.rearrange("p (k n) -> p k n", k=KC))
            for j in range(JC):
                w, wbuf = t1.get(j)
                for which in range(2):
                    fns = [lambda e, t=t, k=k, w=w, which=which: e.matmul(
                        PTt[which][:, t, 0:TGW], lhsT=w[:, k, which * 128:(which + 1) * 128],
                        rhs=RB_h[:, k, t * TGW:(t + 1) * TGW], start=(k == 0), stop=(k == KC - 1))
                        for t in range(3) for k in range(KC)]
                    P.group("pe", fns, [wbuf, bf("HB")], [*ptb(which)])
                t1.done(j)
                P.op("act", lambda e: e.activation(out=v3(TMP1[:]), in_=pt3(0), func=AF.Silu), [*ptb(0)], [bf("TMP1")])
                P.op("dve", lambda e, j=j: e.tensor_tensor(out=v3(RA_act[:, j, :]), in0=v3(TMP1[:]), in1=pt3(1), op=ALU.mult),
                     [bf("TMP1"), *ptb(1)], [bf("ACT")])
            P.barrier()
            t2 = WTiles([wd[m, hf] for m in range(KC) for hf in range(4)], 1408, 6,
                        lambda ap: ap.rearrange("p (j n) -> p j n", j=11))
            for m in range(KC):
                wl = [t2.get(4 * m + hf) for hf in range(4)]
                pa = m % 2
                for hf in range(4):
                    fns = [lambda e, t=t, hf=hf, jj=jj, pa=pa, w=wl[hf][0]: e.matmul(
                        PTt[pa][:, t, 0:TGW], lhsT=w[:, jj, :], rhs=RA_act[:, hf * 11 + jj, t * TGW:(t + 1) * TGW],
                        start=(hf == 0 and jj == 0), stop=(hf == 3 and jj == 10))
                        for t in range(3) for jj in range(11)]
                    P.group("pe", fns, [wl[hf][1], bf("ACT")], [*ptb(pa)])
                    t2.done(4 * m + hf)
                P.op("act", lambda e, m=m, pa=pa: e.activation(out=v3(RB_y[:, m, :]), in_=pt3(pa), func=AF.Copy),
                     [*ptb(pa)], [bf("SRC")])
            P.barrier()

        def proj_tile(w, wbuf, pa, m=128):
            fns = [lambda e, t=t, k=k: e.matmul(PTt[pa][0:m, t, 0:TGW], lhsT=w[:, k, 0:m],
                                                rhs=RB_h[:, k, t * TGW:(t + 1) * TGW], start=(k == 0), stop=(k == KC - 1))
                   for t in range(3) for k in range(KC)]
            P.group("pe", fns, [wbuf, bf("HB")], [*ptb(pa)])


        WAs = sb("WAs", (128, KC, 8), BF16)
        WBs = sb("WBs", (128, KC, 8), BF16)
        SSQ = sb("SSQ", (128, 2))
        ZS2 = sb("ZS2", (128, NT))

        def PMq(a, q):
            return ALLPS[:, a * 4 + q, 0:128]

        def pmb(a, q):
            return bk(a * 4 + q)

        def run_lockstep(gens):
            gens = list(gens)
            while gens:
                for g_ in list(gens):
                    try:
                        next(g_)
                    except StopIteration:
                        gens.remove(g_)

        def run_pool(tasks, nlanes, stagger, extra=()):
            tasks = list(tasks)
            active = {}
            extra = list(extra)
            rnd = 0
            started = 0
            while tasks or active or extra:
                for L in range(nlanes):
                    if L not in active and tasks and rnd >= started * stagger:
                        active[L] = tasks.pop(0)(L)
                        started += 1
                for L in list(active):
                    try:
                        next(active[L])
                    except StopIteration:
                        del active[L]
                for g_ in list(extra):
                    try:
                        next(g_)
                    except StopIteration:
                        extra.remove(g_)
                rnd += 1

        def pool_gen(tasks, nlanes, stagger):
            tasks = list(tasks)
            active = {}
            rnd = 0
            started = 0
            while tasks or active:
                for L in range(nlanes):
                    if L not in active and tasks and rnd >= started * stagger:
                        active[L] = tasks.pop(0)(L)
                        started += 1
                for L in list(active):
                    try:
                        next(active[L])
                    except StopIteration:
                        del active[L]
                rnd += 1
                yield

        def run_weighted(gens_w):
            gens_w = [[g_, w_] for g_, w_ in gens_w]
            while gens_w:
                for item in list(gens_w):
                    for _ in range(item[1]):
                        try:
                            next(item[0])
                        except StopIteration:
                            gens_w.remove(item)
                            break

        def BKf(i):
            return ALLPS[:, i, :]

        def BKb(i):
            return ALLPS[:, i, :].bitcast(BF16)

        def mixer(g, so):
            off = [0]

            def ra(n, name=None):
                a = RA[:, off[0]:off[0] + n]
                off[0] += n
                assert off[0] <= JC * NT // 2, off[0]
                return a

            def rab(n, name=None):
                return ra((n + 1) // 2).bitcast(BF16)[:, 0:n]
            CIP = ra(1032)
            CIS = ra(NSEQ * 7).rearrange("p (s t) -> p s t", s=NSEQ)
            FL = ra(NT)
            T1 = ra(NT)
            QNb = [rab(NT), rab(NT), rab(NT)]
            KNb = [rab(NT), rab(NT)]; VSb = rab(NT)
            ZS = [ra(NT), ra(NT), ZS2[:]]
            GFM = FL[0:8, :]; BFM = T1[0:8, :]
            GTOK = ra(NCK * 8).rearrange("p (c h) -> p c h", c=NCK)
            BTOK = ra(NCK * 8).rearrange("p (c h) -> p c h", c=NCK)
            NBTOK = ra(NCK * 8).rearrange("p (c h) -> p c h", c=NCK)
            KTOKs = [rab(NCK * 128).rearrange("p (c n) -> p c n", c=NCK) for _ in range(2)]
            VTOKs = [rab(NCK * 128).rearrange("p (c n) -> p c n", c=NCK) for _ in range(2)]
            WUs2 = [rab(NCK * 256).rearrange("p (c j n) -> p c j n", c=NCK, j=2) for _ in range(2)]
            QKTs2 = [rab(NCK * 128).rearrange("p (c n) -> p c n", c=NCK) for _ in range(2)]
            KPs2 = [rab(NCK * 128).rearrange("p (c n) -> p c n", c=NCK) for _ in range(2)]
            SMX2 = [ra(NCK * 3).rearrange("p (c j) -> p c j", c=NCK) for _ in range(2)]
            EGLS2 = [ra(NSB * NSQ), ra(NSB * NSQ)]
            RS = ra(1)
            nck = NCH if so else NCK

            def ccols(c):
                return slice(c * 128, (c + 1) * 128) if c < NCH else slice(NP + 32 * (c - NCH), NP + 32 * (c - NCH + 1))
            lanes = []
            for L in range(4):
                lanes.append(dict(Gm=ra(128), Es=ra(128), ET=ra(128), Nm=rab(128), NTt=rab(128),
                                  PP=[rab(256).rearrange("p (j n) -> p j n", j=2), rab(256).rearrange("p (j n) -> p j n", j=2)],
                                  RT=rab(128), KBG=rab(128), VB=rab(128), BEG=ra(1), GSEL=ra(NSQ)))
            VNb = rab(128); T2 = ra(128); Oo = ra(128); ON = ra(128); Sb = rab(128)
            WTpad = rab(NSQ * 32).rearrange("p (s i) -> p s i", s=NSQ)
            QNpad = rab(NSQ * 32).rearrange("p (s i) -> p s i", s=NSQ)
            KPpad = rab(NSQ * 128).rearrange("p (s n) -> p s n", s=NSQ)
            S0h = ra(NSQ * 128).rearrange("p (s n) -> p s n", s=NSQ)
            S0b = rab(NSQ * 128).rearrange("p (s n) -> p s n", s=NSQ)

            if so:
                tl = [3 * i + j for i in range(8) for j in (0, 1)] + [24 + 4 * h + j for h in range(8) for j in (0, 1, 2)]
            else:
                tl = list(range(56))
            tpos = {t: i for i, t in enumerate(tl)}
            wt0 = WTiles([win[t].rearrange("(k p) n -> p k n", p=128) for t in tl], 2048, 4,
                         lambda ap: ap.rearrange("p (k n) -> p k n", k=KC))

            class _WT:
                def get(self, t):
                    return wt0.get(tpos[t])

                def done(self, t):
                    wt0.done(tpos[t])
            wt = _WT()
            P.dma("pool", WAs[:], wa.rearrange("(k p) n -> p k n", p=128), [], bf("WAs"))
            P.dma("pool", WBs[:], wb.rearrange("(k p) n -> p k n", p=128), [], bf("WBs"))
            T1s = T1[:, NP:NP + NS].rearrange("p (s t) -> p s t", s=NSEQ)
            FLs = FL[:, NP:NP + NS].rearrange("p (s t) -> p s t", s=NSEQ)

            def conv(wts, ci, ntap):
                for j in range(ntap):
                    wj = wts[:, ci, j:j + 1]
                    if j == 0:
                        P.op("dve", lambda e, wj=wj: e.tensor_scalar(out=T1[:, 0:NP], in0=CIP[:, 0:NP], scalar1=wj,
                                                                    scalar2=None, op0=ALU.mult), [bf("CIN")], [bf("T1")])
                        if not so:
                            P.op("dve", lambda e, wj=wj: e.tensor_scalar(out=T1s, in0=CIS[:, :, 0:4], scalar1=wj,
                                                                        scalar2=None, op0=ALU.mult), [bf("CIN")], [bf("T1")])
                    else:
                        P.op("dve", lambda e, wj=wj, j=j: e.scalar_tensor_tensor(
                            out=T1[:, 0:NP], in0=CIP[:, j:j + NP], scalar=wj, in1=T1[:, 0:NP], op0=ALU.mult, op1=ALU.add),
                            [bf("CIN"), bf("T1")], [bf("T1")])
                        if not so:
                            P.op("dve", lambda e, wj=wj, j=j: e.scalar_tensor_tensor(
                                out=T1s, in0=CIS[:, :, j:j + 4], scalar=wj, in1=T1s, op0=ALU.mult, op1=ALU.add),
                                [bf("CIN"), bf("T1")], [bf("T1")])
                    yield

            def hist_proj(t, ncol, pa):
                w, wb_ = wt.get(t)
                fns = [lambda e, k=k, w=w: e.matmul(PTt[pa][:, 0, 0:ncol], lhsT=w[:, k, :], rhs=RB_h[:, k, NP - ncol:NP],
                                                    start=(k == 0), stop=(k == KC - 1)) for k in range(KC)]
                P.group("pe", fns, [wb_, bf("HB")], [*ptb(pa)])
                wt.done(t)

            for i in range(8 if so else 0):
                hist_proj(3 * i, 2, 0)
                P.op("act", lambda e: e.activation(out=TMP1[:, 0:2], in_=PTt[0][:, 0, 0:2], func=AF.Copy), [*ptb(0)], [bf("TMP1")])
                hist_proj(3 * i + 1, 2, 1)
                P.op("dve", lambda e, i=i: e.tensor_tensor(out=HISTA[:, i, :], in0=TMP1[:, 0:2], in1=PTt[1][:, 0, 0:2], op=ALU.mult),
                     [bf("TMP1"), *ptb(1)], [bf("HISTA")])
            for i in range(0 if so else 8):
                P.op("dve", lambda e, i=i: e.tensor_copy(out=CIP[:, 0:2], in_=HISTA[:, i, :]), [bf("HISTA")], [bf("CIN")])
                P.dma("sp", CIS[:, :, 0:2], hist_a[i], [], bf("CIN"))
                w, wb_ = wt.get(3 * i)
                proj_tile(w, wb_, 0)
                wt.done(3 * i)
                P.op("act", lambda e: e.activation(out=v3(TMP1[:]), in_=pt3(0), func=AF.Copy), [*ptb(0)], [bf("TMP1")])
                w, wb_ = wt.get(3 * i + 1)
                proj_tile(w, wb_, 1)
                wt.done(3 * i + 1)
                P.op("dve", lambda e: e.tensor_tensor(out=v3(FL[:]), in0=v3(TMP1[:]), in1=pt3(1), op=ALU.mult),
                     [bf("TMP1"), *ptb(1)], [bf("FL")])
                P.op("act", lambda e: e.activation(out=CIP[:, 2:2 + NP], in_=FL[:, 0:NP], func=AF.Copy), [bf("FL")], [bf("CIN")])
                P.op("dve", lambda e: e.tensor_copy(out=CIS[:, :, 2:6], in_=FLs), [bf("FL")], [bf("CIN")])
                for _ in conv(CWA, i, 3):
                    pass
                P.op("dve", lambda e, i=i: e.tensor_copy(out=HISTA[:, i, :], in_=CIP[:, NP:NP + 2]), [bf("CIN")], [bf("HISTA")])
                P.dma("sp", o_ca_s[i], CIS[:, :, 4:6], [bf("CIN")], bf("OCAS"))
                w, wb_ = wt.get(3 * i + 2)
                proj_tile(w, wb_, 0)
                wt.done(3 * i + 2)
                P.op("act", lambda e: e.activation(out=v3(RSTD[:]), in_=pt3(0), func=AF.Copy), [*ptb(0)], [bf("RSTD")])
                P.op("dve", lambda e, i=i: e.tensor_tensor(out=RB_m[:, i, :], in0=T1[:], in1=RSTD[:], op=ALU.mult),
                     [bf("T1"), bf("RSTD")], [bf("YMIX")])
            checkpoint("mixA", RB_y[:, 8:16, :], nk=8)

            for pa, wsm, nm in ((0, WAs, "WAs"), (1, WBs, "WBs")):
                fns = [lambda e, t=t, k=k, pa=pa, wsm=wsm: e.matmul(PTt[pa][0:8, t, 0:TGW], lhsT=wsm[:, k, :],
                                                                    rhs=RB_h[:, k, t * TGW:(t + 1) * TGW],
                                                                    start=(k == 0), stop=(k == KC - 1))
                       for t in range(3) for k in range(KC)]
                P.group("pe", fns, [bf(nm), bf("HB")], [*ptb(pa)])
            P.op("act", lambda e: e.activation(out=v3(GFM), in_=pt3(0, 8), func=AF.Exp, bias=DTB[:, 0:1]),
                 [*ptb(0), bf("DTB")], [bf("FL")])
            P.op("act", lambda e: e.activation(out=GFM, in_=GFM, func=AF.Ln, bias=ONESF[0:8, 0:1]),
                 [bf("FL"), bf("ONESF")], [bf("FL")])
            P.op("dve", lambda e: e.tensor_scalar(out=GFM, in0=GFM, scalar1=NEXPA[:, 0:1], scalar2=None,
                                                  op0=ALU.mult), [bf("FL"), bf("NEXPA")], [bf("FL")])
            P.op("act", lambda e: e.activation(out=v3(BFM), in_=pt3(1, 8), func=AF.Sigmoid), [*ptb(1)], [bf("T1")])
            for a, src, nm in ((0, GFM, "FL"), (1, BFM, "T1")):
                fns = [lambda e, c=c, a=a, src=src: e.transpose(out=PM[a][:, c * 8:(c + 1) * 8],
                                                                 in_=src[:, c * 128:(c + 1) * 128], identity=CM[0:8, 5, 0:8])
                       for c in range(NCH)]
                for b_ in range(0 if so else NSB):
                    fns.append(lambda e, a=a, src=src, b_=b_: e.transpose(out=PM[a][0:32, 64 + 8 * b_:72 + 8 * b_],
                                                                          in_=src[:, NP + 32 * b_:NP + 32 * b_ + 32],
                                                                          identity=CM[0:8, 5, 0:8]))
                P.group("pe", fns, [bf(nm), bf("CM")], [bk(6 + a)])
            for a, dst, nm in ((0, GTOK, "GTOK"), (1, BTOK, "BTOK")):
                P.op("dve", lambda e, a=a, dst=dst: e.tensor_copy(out=dst[:, 0:8, :],
                                                                  in_=PM[a][:, 0:64].rearrange("p (c h) -> p c h", c=8)),
                     [bk(6 + a)], [bf(nm)])
                if not so:
                    P.op("dve", lambda e, a=a, dst=dst: e.tensor_copy(out=dst[0:32, 8:8 + NSB, :],
                                                                      in_=PM[a][0:32, 64:64 + 8 * NSB].rearrange("p (c h) -> p c h", c=NSB)),
                         [bk(6 + a)], [bf(nm)])
            P.op("dve", lambda e: e.tensor_scalar(out=NBTOK[:, 0:8, :], in0=BTOK[:, 0:8, :], scalar1=-1.0, scalar2=None,
                                                  op0=ALU.mult), [bf("BTOK")], [bf("NBTOK")])
            if not so:
                P.op("dve", lambda e: e.tensor_scalar(out=NBTOK[0:32, 8:8 + NSB, :], in0=BTOK[0:32, 8:8 + NSB, :], scalar1=-1.0, scalar2=None,
                                                      op0=ALU.mult), [bf("BTOK")], [bf("NBTOK")])
            checkpoint("gb", RB_y[:, 8:16, :], nk=8)

            def stageP(h):
                par = h % 2
                par3 = h % 3
                for qi in range(3):
                    cch = qi * 8 + h
                    if so and qi == 0:
                        hist_proj(24 + 4 * h, 3, 0)
                        P.op("dve", lambda e, cch=cch: e.tensor_copy(out=HISTQ[:, cch, :], in_=PTt[0][:, 0, 0:3]), [*ptb(0)], [bf("HISTQ")])
                        yield
                        continue
                    P.op("dve", lambda e, cch=cch: e.tensor_copy(out=CIP[:, 0:3], in_=HISTQ[:, cch, :]), [bf("HISTQ")], [bf("CIN")])
                    if not so:
                        P.dma("sp", CIS[:, :, 0:3], hist_q[cch], [], bf("CIN"))
                    w, wb_ = wt.get(24 + 4 * h + qi)
                    pa = 0
                    proj_tile(w, wb_, pa)
                    wt.done(24 + 4 * h + qi)
                    yield
                    P.op("act", lambda e, pa=pa: e.activation(out=v3(FL[:]), in_=pt3(pa), func=AF.Copy), [*ptb(pa)], [bf("FL")])
                    yield
                    P.op("act", lambda e: e.activation(out=CIP[:, 3:3 + NP], in_=FL[:, 0:NP], func=AF.Copy), [bf("FL")], [bf("CIN")])
                    if not so:
                        P.op("dve", lambda e: e.tensor_copy(out=CIS[:, :, 3:7], in_=FLs), [bf("FL")], [bf("CIN")])
                    yield
                    yield from conv(CWQ, cch, 4)
                    P.op("dve", lambda e, cch=cch: e.tensor_copy(out=HISTQ[:, cch, :], in_=CIP[:, NP:NP + 3]), [bf("CIN")], [bf("HISTQ")])
                    if not so:
                        P.dma("sp", o_cq_s[cch], CIS[:, :, 4:7], [bf("CIN")], bf("OCQS"))
                    yield
                    if qi == 2:
                        P.op("act", lambda e: e.activation(out=VSb, in_=T1[:], func=AF.Silu), [bf("T1")], [bf("VSb")])
                        yield
                        continue
                    P.op("act", lambda e: e.activation(out=FL[:], in_=T1[:], func=AF.Silu), [bf("T1")], [bf("FL")])
                    yield
                    P.op("act", lambda e: e.activation(out=SQ[0][:], in_=FL[:], func=AF.Square), [bf("FL")], [bf("SQ0")])
                    yield
                    fns = [lambda e, t=t, pa=pa: e.matmul(PTt[pa][:, t, 0:TGW], lhsT=ONES1[:], rhs=SQ[0][:, t * TGW:(t + 1) * TGW],
                                                          start=True, stop=True) for t in range(3)]
                    P.group("pe", fns, [bf("SQ0")], [*ptb(pa)])
                    yield
                    rstd_from(pa)
                    yield
                    if qi == 0:
                        P.op("dve", lambda e, par3=par3: e.scalar_tensor_tensor(out=QNb[par3], in0=FL[:], scalar=128.0 ** -0.5, in1=RSTD[:],
                                                                             op0=ALU.mult, op1=ALU.mult),
                             [bf("FL"), bf("RSTD")], [bf("QNb%d" % par3)])
                    else:
                        P.op("dve", lambda e, par=par: e.tensor_tensor(out=KNb[par], in0=FL[:], in1=RSTD[:], op=ALU.mult),
                             [bf("FL"), bf("RSTD")], [bf("KNb%d" % par)])
                    yield
                if not so:
                    w, wb_ = wt.get(24 + 4 * h + 3)
                    proj_tile(w, wb_, 0)
                    wt.done(24 + 4 * h + 3)
                    yield
                    P.op("act", lambda e, par3=par3: e.activation(out=v3(ZS[par3][:]), in_=pt3(0), func=AF.Silu), [*ptb(0)], [bf("ZS%d" % par3)])
                    yield
                n = 0
                for src, snm, dst, dnm in ((KNb[par], "KNb%d" % par, KTOKs[par], "KTOK%d" % par), (VSb, "VSb", VTOKs[par], "VTOK%d" % par)):
                    for c in range(nck):
                        C = 128 if c < NCH else 32
                        cs = ccols(c)
                        bi = n % 3
                        n += 1
                        P.op("pe", lambda e, src=src, cs=cs, bi=bi, C=C: e.transpose(out=BKb(bi)[0:C, 0:128], in_=src[:, cs], identity=IDB[:]),
                             [bf(snm), bf("IDB")], [bk(bi)])
                        if n % 2:
                            P.op("act", lambda e, dst=dst, c=c, bi=bi, C=C: e.activation(out=dst[0:C, c, :], in_=BKb(bi)[0:C, 0:128], func=AF.Copy),
                                 [bk(bi)], [bf(dnm)])
                        else:
                            P.op("dve", lambda e, dst=dst, c=c, bi=bi, C=C: e.tensor_copy(out=dst[0:C, c, :], in_=BKb(bi)[0:C, 0:128]),
                                 [bk(bi)], [bf(dnm)])
                        yield

            def prep(h, c, L):
                par = h % 2
                par3 = h % 3
                WUs, QKTs, KPs, SMX, EGLS = WUs2[par], QKTs2[par], KPs2[par], SMX2[par], EGLS2[par]
                WTs, Us = WUs[:, :, 0, :], WUs[:, :, 1, :]
                sfx_ = str(par)
                ln = lanes[L]
                nm = lambda x: "%s_%d" % (x, L)
                bL = 3 + L
                KTOK, VTOK = KTOKs[par], VTOKs[par]
                ktn, vtn, knn = "KTOK%d" % par, "VTOK%d" % par, "KNb%d" % par
                KNbp = KNb[par]
                RGA = BKf(bL)[:, 0:128]
                RGB = BKf(bL)[:, 128:256]
                RGAB = BKf(bL)[:, 0:256].rearrange("p (j n) -> p j n", j=2)
                RGC = BKf(bL)[:, 256:384]
                RGCb = BKb(bL)[:, 512:640]
                samp = c >= NCH
                C = 32 if samp else 128
                cs = ccols(c)
                if samp:
                    mTRI, mSFX, mNEGS, mNEGT = [SM[:, i, :] for i in range(4)]
                    mnm = "SM"
                else:
                    mTRI, mSFX, mNEGS, mNEGT = [CM[:, i, :] for i in range(4)]
                    mnm = "CM"
                OF = ONESF[0:C, 0:C]
                NOF = NONESF[0:C, 0:C]
                gcol = GTOK[0:C, c, h:h + 1]
                bcol = BTOK[0:C, c, h:h + 1]
                nbcol = NBTOK[0:C, c, h:h + 1]
                Gm, Es, ET, Nm, NTt, RT, KBG, VB = [ln[k][0:C, 0:C] if k not in ("KBG", "VB") else ln[k][0:C, :]
                                                    for k in ("Gm", "Es", "ET", "Nm", "NTt", "RT", "KBG", "VB")]
                BEG, GSEL = ln["BEG"][0:C, :], ln["GSEL"][0:C, :]
                IDf = CM[0:C, 5, 0:C]
                P.op("dve", lambda e: e.tensor_scalar(out=Gm, in0=mTRI, scalar1=gcol, scalar2=None, op0=ALU.mult),
                     [bf(mnm), bf("GTOK")], [bf(nm("Gm"))])
                if samp:
                    P.op("dve", lambda e: e.tensor_scalar(out=GSEL, in0=SEQSEL[:], scalar1=gcol, scalar2=None, op0=ALU.mult),
                         [bf("SEQSEL"), bf("GTOK")], [bf(nm("GSEL"))])
                yield
                D_ = RGA[0:C, 0:C]
                DT_ = RGB[0:C, 0:C]
                sm_ = BKf(bL)[:, 384:384 + 2 + NSQ]
                fns = [lambda e: e.matmul(D_, lhsT=Gm, rhs=OF, start=True, stop=False),
                       lambda e: e.matmul(D_, lhsT=NOF, rhs=Gm, start=False, stop=False),
                       lambda e: e.matmul(D_, lhsT=IDf, rhs=mNEGS, start=False, stop=True)]
                if not so:
                    fns += [lambda e: e.matmul(DT_, lhsT=OF, rhs=Gm, start=True, stop=False),
                            lambda e: e.matmul(DT_, lhsT=Gm, rhs=NOF, start=False, stop=False),
                            lambda e: e.matmul(DT_, lhsT=IDf, rhs=mNEGT, start=False, stop=True)]
                fns += [lambda e: e.matmul(sm_[0:C, 0:1], lhsT=mTRI, rhs=gcol, start=True, stop=True),
                        lambda e: e.matmul(sm_[0:C, 1:2], lhsT=mSFX, rhs=gcol, start=True, stop=True)]
                if samp:
                    fns.append(lambda e: e.matmul(sm_[:, 2:2 + NSQ], lhsT=ONESF[0:C, :], rhs=GSEL, start=True, stop=True))
                else:
                    fns.append(lambda e: e.matmul(sm_[:, 2:3], lhsT=ONESF[0:C, :], rhs=gcol, start=True, stop=True))
                P.group("pe", fns, [bf(nm("Gm")), bf(nm("GSEL")), bf(mnm), bf("GTOK")], [bk(bL)])
                yield
                P.op("act", lambda e: e.activation(out=Es, in_=D_, func=AF.Exp), [bk(bL)], [bf(nm("Es"))])
                yield
                if not so:
                    P.op("act", lambda e: e.activation(out=ET, in_=DT_, func=AF.Exp), [bk(bL)], [bf(nm("ET"))])
                    yield
                if samp:
                    P.op("act", lambda e: e.activation(out=SMX[0:C, c, 0:2], in_=sm_[0:C, 0:2], func=AF.Exp), [bk(bL)], [bf("SMX" + sfx_)])
                    P.op("act", lambda e: e.activation(out=EGLS[:, (c - NCH) * NSQ:(c - NCH + 1) * NSQ], in_=sm_[:, 2:2 + NSQ], func=AF.Exp),
                         [bk(bL)], [bf("EGLS" + sfx_)])
                else:
                    P.op("act", lambda e: e.activation(out=SMX[:, c, :], in_=sm_[:, 0:3], func=AF.Exp), [bk(bL)], [bf("SMX" + sfx_)])
                yield
                P.op("dve", lambda e: e.tensor_tensor(out=BEG, in0=SMX[0:C, c, 0:1], in1=bcol, op=ALU.mult),
                     [bf("SMX" + sfx_), bf("BTOK")], [bf(nm("BEG"))])
                P.op("pe", lambda e: e.matmul(RGC[0:C, 0:C], lhsT=KNbp[:, cs], rhs=KNbp[:, cs], start=True, stop=True),
                     [bf(knn)], [bk(bL)])
                yield
                P.op("dve", lambda e: e.scalar_tensor_tensor(out=Nm, in0=RGC[0:C, 0:C], scalar=nbcol, in1=Es, op0=ALU.mult, op1=ALU.mult),
                     [bk(bL), bf("NBTOK"), bf(nm("Es"))], [bf(nm("Nm"))])
                yield
                P.op("pe", lambda e: e.transpose(out=RGCb[0:C, 0:C], in_=Nm, identity=IDB[0:C, 0:C]), [bf(nm("Nm")), bf("IDB")], [bk(bL)])
                yield
                P.op("act", lambda e: e.activation(out=NTt, in_=RGCb[0:C, 0:C], func=AF.Copy), [bk(bL)], [bf(nm("NTt"))])
                P.op("dve", lambda e: e.tensor_tensor(out=RT, in0=RGCb[0:C, 0:C], in1=IDB[0:C, 0:C], op=ALU.add),
                     [bk(bL), bf("IDB")], [bf(nm("RT"))])
                yield
                Pc, PTc, pcn, ptn = Nm, NTt, nm("Nm"), nm("NTt")
                nl = 1 if samp else 6
                for l in range(nl):
                    last = l == nl - 1
                    PPl = ln["PP"][l % 2]
                    Pn, PTn = PPl[0:C, 0, 0:C], PPl[0:C, 1, 0:C]
                    pnn = ptnn = nm("PP%d" % (l % 2))
                    fns = [lambda e, Pc=Pc, PTc=PTc: e.matmul(RGA[0:C, 0:C], lhsT=PTc, rhs=Pc, start=True, stop=True)]
                    if not last:
                        fns.append(lambda e, Pc=Pc, PTc=PTc: e.matmul(RGB[0:C, 0:C], lhsT=Pc, rhs=PTc, start=True, stop=True))
                    P.group("pe", fns, [bf(pcn), bf(ptn)], [bk(bL)])
                    yield
                    if last:
                        P.op("act", lambda e, Pn=Pn: e.activation(out=Pn, in_=RGA[0:C, 0:C], func=AF.Copy), [bk(bL)], [bf(pnn)])
                    else:
                        P.op("act", lambda e, PPl=PPl: e.activation(out=PPl[:, :, :], in_=RGAB,
                                                                    func=AF.Copy), [bk(bL)], [bf(pnn)])
                    yield
                    P.op("pe", lambda e, Pn=Pn: e.matmul(RGC[0:C, 0:C], lhsT=Pn, rhs=RT, start=True, stop=True),
                         [bf(pnn), bf(nm("RT"))], [bk(bL)])
                    yield
                    P.op("dve", lambda e: e.tensor_tensor(out=RT, in0=RGC[0:C, 0:C], in1=RT, op=ALU.add),
                         [bk(bL), bf(nm("RT"))], [bf(nm("RT"))])
                    yield
                    Pc, PTc, pcn, ptn = Pn, PTn, pnn, ptnn
                P.op("dve", lambda e: e.tensor_scalar(out=KBG, in0=KTOK[0:C, c, :], scalar1=BEG[:, 0:1], scalar2=None, op0=ALU.mult),
                     [bf(ktn), bf(nm("BEG"))], [bf(nm("KBG"))])
                P.op("dve", lambda e: e.tensor_scalar(out=VB, in0=VTOK[0:C, c, :], scalar1=bcol, scalar2=None, op0=ALU.mult),
                     [bf(vtn), bf("BTOK")], [bf(nm("VB"))])
                yield
                P.op("dve", lambda e: e.tensor_scalar(out=KPs[0:C, c, :], in0=KTOK[0:C, c, :], scalar1=SMX[0:C, c, 1:2], scalar2=None, op0=ALU.mult),
                     [bf(ktn), bf("SMX" + sfx_)], [bf("KPs" + sfx_)])
                fns = [lambda e: e.matmul(RGA[:, 0:C], lhsT=KBG, rhs=RT, start=True, stop=True),
                       lambda e: e.matmul(RGB[0:C, :], lhsT=RT, rhs=VB, start=True, stop=True)]
                P.group("pe", fns, [bf(nm("KBG")), bf(nm("VB")), bf(nm("RT"))], [bk(bL)])
                if not so:
                    P.op("pe", lambda e: e.matmul(RGC[0:C, 0:C], lhsT=KNbp[:, cs], rhs=QNb[par3][:, cs], start=True, stop=True),
                         [bf(knn), bf("QNb%d" % par3)], [bk(bL)])
                yield
                if samp:
                    P.op("act", lambda e: e.activation(out=WTs[:, c, 0:C], in_=RGA[:, 0:C], func=AF.Copy), [bk(bL)], [bf("WTs" + sfx_)])
                    P.op("act", lambda e: e.activation(out=Us[0:C, c, :], in_=RGB[0:C, :], func=AF.Copy), [bk(bL)], [bf("Us" + sfx_)])
                else:
                    P.op("act", lambda e: e.activation(out=WUs[:, c, :, :], in_=RGAB, func=AF.Copy),
                         [bk(bL)], [bf("WTs" + sfx_), bf("Us" + sfx_)])
                if not so:
                    P.op("dve", lambda e: e.tensor_tensor(out=QKTs[0:C, c, 0:C], in0=RGC[0:C, 0:C], in1=ET, op=ALU.mult),
                         [bk(bL), bf(nm("ET"))], [bf("QKTs" + sfx_)])
                yield

            def scan(h):
                par = h % 3
                par2 = h % 2
                WUs, QKTs, KPs, SMX, EGLS = WUs2[par2], QKTs2[par2], KPs2[par2], SMX2[par2], EGLS2[par2]
                WTs, Us = WUs[:, :, 0, :], WUs[:, :, 1, :]
                sfx_ = str(par2)
                S = SST[:, h, :]
                B7 = BKf(7)
                P.op("act", lambda e: e.activation(out=Sb, in_=S, func=AF.Copy), [bf("SST")], [bf("Sb")])
                yield
                pend = None

                def outpath(c, C, cs):
                    P.op("act", lambda e: e.activation(out=ON[0:C, :], in_=Oo[0:C, :], func=AF.Square, accum_out=SSQ[0:C, 0:1]),
                         [bf("O")], [bf("ON"), bf("SSQ")])
                    P.op("act", lambda e: e.activation(out=SSQ[0:C, 1:2], in_=SSQ[0:C, 0:1], func=AF.Sqrt, bias=EPSB[0:C, 0:1], scale=1.0 / 128),
                         [bf("SSQ"), bf("EPSB")], [bf("SSQ")])
                    yield
                    P.op("dve", lambda e: e.reciprocal(out=RS[0:C, :], in_=SSQ[0:C, 1:2]), [bf("SSQ")], [bf("RS")])
                    P.op("dve", lambda e: e.tensor_scalar(out=ON[0:C, :], in0=Oo[0:C, :], scalar1=RS[0:C, 0:1], scalar2=None, op0=ALU.mult),
                         [bf("O"), bf("RS")], [bf("ON")])
                    yield
                    P.op("pe", lambda e: e.transpose(out=B7[:, 0:C], in_=ON[0:C, :], identity=CM[0:C, 5, 0:C]), [bf("ON"), bf("CM")], [bk(7)])
                    yield
                    P.op("dve", lambda e: e.scalar_tensor_tensor(out=RB_m[:, 8 + h, cs], in0=B7[:, 0:C], scalar=DNN[:, 0:1],
                                                                 in1=ZS[par][:, cs], op0=ALU.mult, op1=ALU.mult),
                         [bk(7), bf("DNN"), bf("ZS%d" % par)], [bf("YMIX")])
                    yield

                for c in range(nck):
                    samp = c >= NCH
                    sbi = c - NCH
                    C = 32 if samp else 128
                    cs = ccols(c)
                    if not samp:
                        fns = [lambda e, c=c: e.matmul(B7[:, 0:128], lhsT=WTs[:, c, :], rhs=Sb, start=True, stop=True)]
                        if not so:
                            fns.append(lambda e, cs=cs: e.matmul(B7[:, 128:256], lhsT=QNb[par][:, cs], rhs=Sb, start=True, stop=True))
                        P.group("pe", fns, [bf("WTs" + sfx_), bf("QNb%d" % par), bf("Sb")], [bk(7)])
                    else:
                        P.dma("sp", S0h[:], s0[sbi * NSQ:(sbi + 1) * NSQ, h].rearrange("s p n -> p s n"), [], bf("S0h"))
                        P.op("act", lambda e: e.activation(out=S0b[:], in_=S0h[:], func=AF.Copy), [bf("S0h")], [bf("S0b")])
                        P.op("dve", lambda e, c=c: e.tensor_tensor(out=WTpad[:, :, :], in0=WTs[:, c:c + 1, 0:32].to_broadcast([128, NSQ, 32]),
                                                                   in1=SEQMB[:, :, :], op=ALU.mult),
                             [bf("WTs" + sfx_), bf("SEQMB")], [bf("WTpad")])
                        yield
                        P.op("dve", lambda e, cs=cs: e.tensor_tensor(out=QNpad[:, :, :], in0=QNb[par][:, cs].unsqueeze(1).to_broadcast([128, NSQ, 32]),
                                                                     in1=SEQMB[:, :, :], op=ALU.mult),
                             [bf("QNb%d" % par), bf("SEQMB")], [bf("QNpad")])
                        yield
                        P.op("dve", lambda e, c=c: e.tensor_tensor(out=KPpad[0:32, :, :], in0=KPs[0:32, c:c + 1, :].to_broadcast([32, NSQ, 128]),
                                                                   in1=SEQSEL[:, :].unsqueeze(2).to_broadcast([32, NSQ, 128]), op=ALU.mult),
                             [bf("KPs" + sfx_), bf("SEQSEL")], [bf("KPpad")])
                        yield
                        fns = [lambda e, s_=s_: e.matmul(B7[0:32, 0:128], lhsT=WTpad[:, s_, :], rhs=S0b[:, s_, :], start=(s_ == 0), stop=(s_ == NSQ - 1))
                               for s_ in range(NSQ)]
                        fns += [lambda e, s_=s_: e.matmul(B7[0:32, 128:256], lhsT=QNpad[:, s_, :], rhs=S0b[:, s_, :], start=(s_ == 0), stop=(s_ == NSQ - 1))
                                for s_ in range(NSQ)]
                        P.group("pe", fns, [bf("WTpad"), bf("QNpad"), bf("S0b")], [bk(7)])
                    yield
                    P.op("dve", lambda e, c=c, C=C: e.tensor_tensor(out=VNb[0:C, :], in0=Us[0:C, c, :], in1=B7[0:C, 0:128], op=ALU.subtract),
                         [bf("Us" + sfx_), bk(7)], [bf("VNb")])
                    yield
                    if not samp:
                        fns = [lambda e, c=c: e.matmul(B7[:, 256:384], lhsT=KPs[:, c, :], rhs=VNb[:, :], start=True, stop=True)]
                        if not so:
                            fns.append(lambda e, c=c: e.matmul(B7[:, 384:512], lhsT=QKTs[:, c, :], rhs=VNb[:, :], start=True, stop=True))
                        P.group("pe", fns, [bf("KPs" + sfx_), bf("QKTs" + sfx_), bf("VNb")], [bk(7)])
                        yield
                        P.op("dve", lambda e, c=c: e.scalar_tensor_tensor(out=S, in0=S, scalar=SMX[:, c, 2:3], in1=B7[:, 256:384],
                                                                          op0=ALU.mult, op1=ALU.add),
                             [bf("SST"), bf("SMX" + sfx_), bk(7)], [bf("SST")])
                        yield
                        P.op("act", lambda e: e.activation(out=Sb, in_=S, func=AF.Copy), [bf("SST")], [bf("Sb")])
                        yield
                        if so:
                            continue
                    else:
                        P.op("pe", lambda e, c=c: e.matmul(B7[0:32, 384:512], lhsT=QKTs[0:32, c, 0:32], rhs=VNb[0:32, :], start=True, stop=True),
                             [bf("QKTs" + sfx_), bf("VNb")], [bk(7)])
                        yield
                    P.op("act", lambda e, C=C: e.activation(out=T2[0:C, :], in_=B7[0:C, 384:512], func=AF.Copy), [bk(7)], [bf("T2")])
                    yield
                    if pend is not None:
                        yield from pend
                    P.op("dve", lambda e, c=c, C=C: e.scalar_tensor_tensor(out=Oo[0:C, :], in0=B7[0:C, 128:256], scalar=SMX[0:C, c, 0:1],
                                                                           in1=T2[0:C, :], op0=ALU.mult, op1=ALU.add),
                         [bk(7), bf("SMX" + sfx_), bf("T2")], [bf("O")])
                    yield
                    pend = outpath(c, C, cs)
                    if samp:
                        for s_ in range(NSQ):
                            P.op("pe", lambda e, s_=s_: e.matmul(B7[:, 256:384], lhsT=KPpad[0:32, s_, :], rhs=VNb[0:32, :], start=True, stop=True),
                                 [bf("KPpad"), bf("VNb")], [bk(7)])
                            P.op("dve", lambda e, s_=s_, sbi=sbi: e.scalar_tensor_tensor(out=S0h[:, s_, :], in0=S0h[:, s_, :],
                                                                                         scalar=EGLS[:, sbi * NSQ + s_:sbi * NSQ + s_ + 1],
                                                                                         in1=B7[:, 256:384], op0=ALU.mult, op1=ALU.add),
                                 [bf("S0h"), bf("EGLS" + sfx_), bk(7)], [bf("S0h")])
                            yield
                        P.dma("sp", o_dl_s[sbi * NSQ:(sbi + 1) * NSQ, h].rearrange("s p n -> p s n"), S0h[:], [bf("S0h")], bf("ODLS"))
                if pend is not None:
                    yield from pend

            def poolB(h):
                yield from pool_gen([(lambda L, c=c: prep(h, c, L)) for c in (list(range(NCH, nck)) + list(range(NCH)))], 4, _TUNE[3])

            for r in range(10):
                gl = []
                if r - 2 >= 0:
                    gl.append((scan(r - 2), _TUNE[0]))
                if 0 <= r - 1 < 8:
                    gl.append((poolB(r - 1), _TUNE[1]))
                if r < 8:
                    gl.append((stageP(r), _TUNE[2]))
                run_weighted(gl)

        def checkpoint(label, src3=None, nk=KC, parts=128):
            if stop != label:
                return
            P.barrier()
            src3 = RA_x if src3 is None else src3
            for k in range(nk):
                P.dma("sp", dbg[k * 128:k * 128 + parts, :], src3[0:parts, k, :], [], bf("DBG"))
            P.barrier()
            raise _Stop()

        def checkpoint2(label, aps):
            if stop != label:
                return
            P.barrier()
            for i, ap in enumerate(aps):
                n = ap.shape[-1]
                if ap.dtype != F32:
                    raise ValueError("fp32 only")
                P.dma("sp", dbg[i * 128:i * 128 + ap.shape[0], 0:n], ap, [], bf("DBG"))
            P.barrier()
            raise _Stop()

        try:
          for g in range(NG):
              for q4 in range(4):
                  P.dma("sp", RA_x[:, q4 * 4:(q4 + 1) * 4, :],
                        xT[g, q4 * 512:(q4 + 1) * 512, :].rearrange("(k p) n -> p k n", p=128), [], bf("SRC"))
              checkpoint("load")
              prenorm(0)
              P.barrier()
              checkpoint("hb0", RB_h.bitcast(F32) if False else None)
              ffn(wgu1, wd1)
              checkpoint("y1", RB_y)
              post_residual(RB_y, 1, 0.5, xT[g], g, next_stats=True)
              P.barrier()
              checkpoint("x1")
              prenorm(2, have_stats=True)
              P.barrier()
              so = g == 0
              mixer(g, so)
              P.barrier()
              if so:
                  for ap, nm_ in ((SST[:], "SST"), (HISTA[:], "HISTA"), (HISTQ[:], "HISTQ")):
                      P.op("dve", lambda e, ap=ap: e.tensor_scalar(out=ap, in0=ap, scalar1=CARRY[:, 0:1], scalar2=None, op0=ALU.mult),
                           [bf(nm_), bf("CARRY")], [bf(nm_)])
                  P.barrier()
                  continue
              checkpoint("ymix", RB_y[:, 8:16, :], nk=8)
              t3 = WTiles([wout[m].rearrange("(k p) n -> p k n", p=128) for m in range(KC)], 2048, 4,
                          lambda ap: ap.rearrange("p (k n) -> p k n", k=KC))
              for m in range(KC):
                  w, wbuf = t3.get(m)
                  pa = m % 2
                  fns = [lambda e, t=t, k=k, w=w, pa=pa: e.matmul(PTt[pa][:, t, 0:TGW], lhsT=w[:, k, :],
                                                                 rhs=RB_m[:, k, t * TGW:(t + 1) * TGW],
                                                                 start=(k == 0), stop=(k == KC - 1))
                         for t in range(3) for k in range(KC)]
                  P.group("pe", fns, [wbuf, bf("YMIX")], [*ptb(pa)])
                  t3.done(m)
                  P.op("act", lambda e, m=m, pa=pa: e.activation(out=v3(RA_x[:, m, :]), in_=pt3(pa), func=AF.Copy),
                       [*ptb(pa)], [bf("SRC")])
              P.barrier()
              post_residual(RA_x, 3, 1.0, xs_d, g, next_stats=True)
              P.barrier()
              checkpoint("x2")
              prenorm(4, have_stats=True)
              P.barrier()
              ffn(wgu2, wd2)
              post_residual(RB_y, 5, 0.5, xs_d, g, next_stats=True)
              P.barrier()
              checkpoint("x3")
              prenorm(6, have_stats=True)
              P.barrier()
              PB = RB_m[:, 0:2, :]
              P.dma("pool", PB, pT.rearrange("(k p) n -> p k n", p=128), [], bf("PB"))
              t4 = WTiles([wpg[m].rearrange("(k p) n -> p k n", p=128) for m in range(KC)], 2048, 2,
                          lambda ap: ap.rearrange("p (k n) -> p k n", k=KC))
              WPP = WRb[:, 4096:4096 + KC * 256].rearrange("p (m k n) -> p m k n", m=KC, k=2)
              P.dma("pool", WPP, wpp.rearrange("m (k p) n -> p m k n", p=128), [], bf("WPP"))
              for m in range(KC):
                  w, wbuf = t4.get(m)
                  proj_tile(w, wbuf, 0)
                  t4.done(m)
                  fns = [lambda e, t=t, k=k, m=m: e.matmul(PTt[1][:, t, 0:TGW], lhsT=WPP[:, m, k, :],
                                                          rhs=PB[:, k, t * TGW:(t + 1) * TGW], start=(k == 0), stop=(k == 1))
                         for t in range(3) for k in range(2)]
                  P.group("pe", fns, [bf("WPP"), bf("PB")], [*ptb(1)])
                  P.op("act", lambda e: e.activation(out=v3(TMP1[:]), in_=pt3(0), func=AF.Sigmoid), [*ptb(0)], [bf("TMP1")])
                  P.op("dve", lambda e, m=m: e.tensor_tensor(out=v3(RA_x[:, m, :]), in0=v3(TMP1[:]), in1=pt3(1), op=ALU.mult),
                       [bf("TMP1"), *ptb(1)], [bf("SRC")])
              P.barrier()
              post_residual(RA_x, 7, 1.0, xs_d, g, last=True)
              P.barrier()

        except _Stop:
            pass
        P.dma("sp", o_ca_p.rearrange("c p t -> p c t"), HISTA[:], [bf("HISTA")], bf("O1"))
        P.dma("sp", o_cq_p.rearrange("c p t -> p c t"), HISTQ[:], [bf("HISTQ")], bf("O2"))
        P.dma("sp", o_dl_p.rearrange("h p n -> p h n"), SST[:], [bf("SST")], bf("O3"))
        P.barrier()
        P.wait_all("sp", [bf("O1"), bf("O2"), bf("O3"), bf("DBG"), bf("OCAS"), bf("OCQS"), bf("ODLS")] + [bf("YOUT%d" % m) for m in range(KC)])
        P.emit()
    return nc


_CACHE = {}
_TUNE = [2, 2, 1, 6]
_DEBUG = {}


def _masks():
    def mk(C, blk):
        idx = np.arange(C)
        same = (idx[:, None] // blk) == (idx[None, :] // blk)
        tri = ((idx[:, None] <= idx[None, :]) & same).astype(np.float32)
        sfx = ((idx[:, None] > idx[None, :]) & same).astype(np.float32)
        negi = np.where((idx[None, :] < idx[:, None]) & same, 0.0, NEG).astype(np.float32)
        negt = np.where((idx[:, None] <= idx[None, :]) & same, 0.0, NEG).astype(np.float32)
        strict = ((idx[None, :] < idx[:, None]) & same).astype(np.float32)
        return tri, sfx, negi, negt, strict
    c = list(mk(128, 128)) + [np.eye(128, dtype=np.float32)]
    sm = list(mk(32, 4))
    seqsel = (np.arange(32)[:, None] // 4 == np.arange(NSQ)[None, :]).astype(np.float32)
    seqmb = np.broadcast_to(seqsel.T[None], (128, NSQ, 32)).astype(np.float32).copy()
    return np.stack(c), np.stack(sm), seqsel, seqmb


def kernel(x_prompt, x_sample, state_conv_a, state_conv_qkv, state_delta, p_prompt, p_sample,
           f1_pre, f1_post, f1_wg, f1_wu, f1_wd,
           mix_pre, mix_post, w_in, conv_a_w, conv_qkv_w, a_log, dt_bias, dn_norm, w_out,
           f2_pre, f2_post, f2_wg, f2_wu, f2_wd,
           ple_pre, ple_post, w_ple_gate, w_ple_proj):
    f = lambda a: np.ascontiguousarray(np.asarray(a, dtype=np.float32))
    x_prompt, x_sample, p_prompt, p_sample = f(x_prompt), f(x_sample), f(p_prompt)[0], f(p_sample)[0]
    sca, scq, sdl = f(state_conv_a)[0], f(state_conv_qkv)[0], f(state_delta)[0]

    def pack_gu(wg, wu):
        wg, wu = f(wg)[0], f(wu)[0]
        return f(np.concatenate([wg.reshape(D, JC, 128), wu.reshape(D, JC, 128)], axis=2).transpose(1, 0, 2))

    def pack_d(wd):
        wd = f(wd)[0]
        t = wd.reshape(4, 11, 128, KC, 128)
        return f(t.transpose(3, 0, 2, 1, 4).reshape(KC, 4, 128, 11 * 128))

    def coltiles(w, n=128):
        w = f(w)
        return f(w.reshape(w.shape[0], w.shape[1] // n, n).transpose(1, 0, 2))

    wi = f(w_in)[0]
    Bc, Cc, Hc, Qc, Kc, Vc, Zc = [coltiles(wi[:, o:o + 1024]) for o in range(0, 7168, 1024)]
    tiles = []
    for i in range(8):
        tiles += [Cc[i], Hc[i], Bc[i]]
    for h in range(8):
        tiles += [Qc[h], Kc[h], Vc[h], Zc[h]]
    win = f(np.stack(tiles))
    cm, sm, seqsel, seqmb = _masks()
    vecs = [f1_pre, f1_post, mix_pre, mix_post, f2_pre, f2_post, ple_pre, ple_post]
    gains = f(np.stack([f(v)[0].reshape(KC, 128).T for v in vecs], axis=1))
    shared = dict(
        wgu1=pack_gu(f1_wg, f1_wu), wd1=pack_d(f1_wd), wgu2=pack_gu(f2_wg, f2_wu), wd2=pack_d(f2_wd),
        win=win, wa=f(wi[:, 7168:7176]), wb=f(wi[:, 7176:7184]),
        wout=coltiles(f(w_out)[0]), wpg=coltiles(f(w_ple_gate)[0]), wpp=coltiles(f(w_ple_proj)[0]),
        gains=gains,
        cw_a=f(f(conv_a_w)[0].reshape(3, 8, 128).transpose(2, 1, 0)),
        cw_q=f(f(conv_qkv_w)[0].reshape(4, 24, 128).transpose(2, 1, 0)),
        alog=f(a_log)[0], dtb=f(dt_bias)[0], dnn=f(f(dn_norm)[0].reshape(128, 1)),
        cmask=cm, smask=sm, seqsel=seqsel, seqmb=seqmb,
    )
    in_maps = []
    for c in range(8):
        seq, half = c % 4, c // 4
        sq = slice(c * NSEQ, (c + 1) * NSEQ)
        z65 = np.zeros((NS + 1, D), np.float32)
        x0 = x_prompt[seq, 0:NP] if half == 1 else np.zeros((NP, D), np.float32)
        x1 = x_prompt[seq, half * NP:(half + 1) * NP]
        xs = [np.concatenate([x0, z65], axis=0).T,
              np.concatenate([x1, x_sample[sq].reshape(NS, D), np.zeros((1, D), np.float32)], axis=0).T]
        pp = np.concatenate([p_prompt[seq, half * NP:(half + 1) * NP], p_sample[sq].reshape(NS, 256),
                             np.zeros((1, 256), np.float32)], axis=0).T
        m = dict(shared)
        m.update(xT=f(np.stack(xs)), pT=f(pp),
                 hist_a=f(sca[sq].reshape(NSEQ, 2, 8, 128).transpose(2, 3, 0, 1)),
                 hist_q=f(scq[sq].reshape(NSEQ, 3, 24, 128).transpose(2, 3, 0, 1)),
                 s0=f(sdl[sq]), carry=np.full((128, 1), float(half), np.float32))
        in_maps.append(m)

    if _DEBUG.get("return_maps"):
        return in_maps
    if "nc" not in _CACHE:
        _CACHE["nc"] = build_program()
    res = run_bass_kernel_spmd(_CACHE["nc"], in_maps, core_ids=list(range(8)))
    R = res.results

    y_prompt = np.zeros((4, NG * NP, D), np.float32)
    y_sample = np.zeros((128, 4, D), np.float32)
    ca_p = np.zeros((1, 4, 2, 1024), np.float32)
    cq_p = np.zeros((1, 4, 3, 3072), np.float32)
    dl_p = np.zeros((1, 4, 8, 128, 128), np.float32)
    ca_s = np.zeros((1, 128, 2, 1024), np.float32)
    cq_s = np.zeros((1, 128, 3, 3072), np.float32)
    dl_s = np.zeros((1, 128, 8, 128, 128), np.float32)
    for c in range(8):
        r = R[c]
        seq, half = c % 4, c // 4
        sq = slice(c * NSEQ, (c + 1) * NSEQ)
        yt = np.asarray(r["yT"])
        y_prompt[seq, half * NP:(half + 1) * NP] = yt[:, 0:NP].T
        y_sample[sq] = yt[:, NP:NP + NS].T.reshape(NSEQ, 4, D)
        ca_s[0, sq] = np.asarray(r["o_ca_s"]).transpose(2, 3, 0, 1).reshape(NSEQ, 2, 1024)
        cq_s[0, sq] = np.asarray(r["o_cq_s"]).transpose(2, 3, 0, 1).reshape(NSEQ, 3, 3072)
        dl_s[0, sq] = np.asarray(r["o_dl_s"])
        if half == 1:
            ca_p[0, seq] = np.asarray(r["o_ca_p"]).transpose(2, 0, 1).reshape(2, 1024)
            cq_p[0, seq] = np.asarray(r["o_cq_p"]).transpose(2, 0, 1).reshape(3, 3072)
            dl_p[0, seq] = np.asarray(r["o_dl_p"])
    return (y_prompt, y_sample, ca_p, cq_p, dl_p, ca_s, cq_s, dl_s)
```

```python
import numpy as np
from contextlib import ExitStack
import concourse.bass as bass
import concourse.mybir as mybir
from concourse.bass_utils import run_bass_kernel_spmd

F32 = mybir.dt.float32
BF16 = mybir.dt.bfloat16
ALU = mybir.AluOpType
AF = mybir.ActivationFunctionType

D = 2048
KC = 16
DFF = 5632
JC = 44
NG = 2
NP = 1024
NSQ = 8
NSB = 2
NSEQ = NSQ * NSB
NS = NSEQ * 4
NT = NP + NS + 1
TGW = NT // 3
NCH = NP // 128
NCK = NCH + NSB
EPS = 1e-6
NEG = -1.0e5


class Buf:
    __slots__ = ("last_w", "reads", "sem", "semval", "excl")

    def __init__(self, excl=False):
        self.excl = excl
        self.last_w = None
        self.reads = []
        self.sem = None
        self.semval = 0


class Prog:
    ENGS = ("pe", "act", "dve", "pool", "sp")

    def __init__(self, nc):
        self.nc = nc
        self.ops = {e: [] for e in self.ENGS}
        self.sems = {}
        self.cnt = {e: 0 for e in self.ENGS}
        self.seen = {e: {} for e in self.ENGS}
        for e in self.ENGS:
            self.sems[e] = nc.alloc_semaphore("s_" + e)
        self.ndsem = 0
        self.dma_events = {}

    def _deps(self, eng, reads, writes):
        need = {}
        for b in reads:
            if b.last_w is not None:
                k, v = b.last_w
                if need.get(k, 0) < v:
                    need[k] = v
        for b in writes:
            if b.last_w is not None:
                k, v = b.last_w
                if need.get(k, 0) < v:
                    need[k] = v
            for (k, v) in b.reads:
                if need.get(k, 0) < v:
                    need[k] = v
        return self._prune(eng, need)

    def _prune(self, eng, need):
        waits = []
        seen = self.seen[eng]
        for k, v in need.items():
            if eng == "pe" and k == "pe":
                continue
            if seen.get(k, 0) < v:
                seen[k] = v
                waits.append((k, v))
        return waits

    def _mark(self, ev, reads, writes):
        for b in reads:
            b.reads.append(ev)
        for b in writes:
            b.last_w = ev
            b.reads = []

    @staticmethod
    def _split(reads, writes):
        ex = [b for b in reads if b.excl]
        if ex:
            return [b for b in reads if not b.excl], list(writes) + ex
        return reads, writes

    def op(self, eng, fn, reads=(), writes=()):
        reads, writes = self._split(reads, writes)
        waits = self._deps(eng, reads, writes)
        self.cnt[eng] += 1
        ev = (eng, self.cnt[eng])
        self._mark(ev, reads, writes)
        self.ops[eng].append((waits, fn, ev))
        return ev

    def group(self, eng, fns, reads=(), writes=()):
        n = len(fns)
        if n == 1:
            return self.op(eng, fns[0], reads, writes)
        reads, writes = self._split(reads, writes)
        waits = self._deps(eng, reads, writes)
        self.ops[eng].append((waits, fns[0], None))
        for fn in fns[1:-1]:
            self.ops[eng].append(((), fn, None))
        self.cnt[eng] += 1
        ev = (eng, self.cnt[eng])
        self._mark(ev, reads, writes)
        self.ops[eng].append(((), fns[-1], ev))
        return ev

    def dma(self, eng, out_ap, in_ap, reads, wbuf):
        writes = (wbuf,)
        waits = self._deps(eng, reads, writes)
        if wbuf.sem is None:
            key = "d%d" % self.ndsem
            self.ndsem += 1
            self.sems[key] = self.nc.alloc_semaphore(key)
            wbuf.sem = key
        wbuf.semval += 16
        ev = (wbuf.sem, wbuf.semval)
        self._mark(ev, reads, writes)
        if eng == "sp":
            self.dma_events[wbuf.sem] = wbuf.semval

        def fn(e, out_ap=out_ap, in_ap=in_ap):
            return e.dma_start(out=out_ap, in_=in_ap)
        self.ops[eng].append((waits, fn, ev))
        return ev

    def barrier(self):
        need = {e: self.cnt[e] for e in ("pe", "act", "dve") if self.cnt[e] > 0}
        need.update(self.dma_events)
        for e in ("pe", "act", "dve", "sp", "pool"):
            n2 = dict(need)
            if e == "pe":
                n2.pop("pe", None)
            waits = self._prune(e, n2)
            if waits:
                self.ops[e].append((waits, None, None))

    def wait_all(self, eng, bufs):
        waits = self._deps(eng, bufs, bufs)
        self.ops[eng].append((waits, None, None))

    def emit(self):
        nc = self.nc
        sems = self.sems
        ops = self.ops

        def run(e, name):
            for waits, fn, ev in ops[name]:
                for (k, v) in waits:
                    e.wait_ge(sems[k], v)
                if fn is None:
                    continue
                ins = fn(e)
                if ev is not None:
                    k, v = ev
                    ins.then_inc(sems[k], 1 if k == name else 16)

        with nc.Block() as block:
            @block.tensor
            def _(e):
                run(e, "pe")

            @block.scalar
            def _(e):
                run(e, "act")

            @block.vector
            def _(e):
                run(e, "dve")

            @block.gpsimd
            def _(e):
                run(e, "pool")

            @block.sync
            def _(e):
                run(e, "sp")


class _Stop(Exception):
    pass


def build_program(stop=None):
    nc = bass.Bass("TRN2", target_bir_lowering=False)
    dt_in = lambda name, shape: nc.dram_tensor(name, list(shape), F32, kind="ExternalInput").ap()
    dt_out = lambda name, shape: nc.dram_tensor(name, list(shape), F32, kind="ExternalOutput").ap()

    xT = dt_in("xT", (NG, D, NT))
    pT = dt_in("pT", (256, NT))
    hist_a = dt_in("hist_a", (8, 128, NSEQ, 2))
    hist_q = dt_in("hist_q", (24, 128, NSEQ, 3))
    s0 = dt_in("s0", (NSEQ, 8, 128, 128))
    carry = dt_in("carry", (128, 1))
    wgu1 = dt_in("wgu1", (JC, D, 256))
    wd1 = dt_in("wd1", (KC, 4, 128, 11 * 128))
    wgu2 = dt_in("wgu2", (JC, D, 256))
    wd2 = dt_in("wd2", (KC, 4, 128, 11 * 128))
    win = dt_in("win", (56, D, 128))
    wa = dt_in("wa", (D, 8))
    wb = dt_in("wb", (D, 8))
    wout = dt_in("wout", (KC, D, 128))
    wpg = dt_in("wpg", (KC, D, 128))
    wpp = dt_in("wpp", (KC, 256, 128))
    gains = dt_in("gains", (128, 8, KC))
    cw_a = dt_in("cw_a", (128, 8, 3))
    cw_q = dt_in("cw_q", (128, 24, 4))
    alog = dt_in("alog", (8,))
    dtb = dt_in("dtb", (8,))
    dnn = dt_in("dnn", (128, 1))
    cmask = dt_in("cmask", (6, 128, 128))
    smask = dt_in("smask", (5, 32, 32))
    seqsel = dt_in("seqsel", (32, NSQ))
    seqmb = dt_in("seqmb", (128, NSQ, 32))

    yT = dt_out("yT", (D, NT))
    o_ca_p = dt_out("o_ca_p", (8, 128, 2))
    o_cq_p = dt_out("o_cq_p", (24, 128, 3))
    o_dl_p = dt_out("o_dl_p", (8, 128, 128))
    o_ca_s = dt_out("o_ca_s", (8, 128, NSEQ, 2))
    o_cq_s = dt_out("o_cq_s", (24, 128, NSEQ, 3))
    o_dl_s = dt_out("o_dl_s", (NSEQ, 8, 128, 128))
    xs_d = nc.dram_tensor("xs_scratch", [D, NT], F32).ap()
    dbg = dt_out("dbg", (D, NT)) if stop is not None else None

    P = Prog(nc)
    es = ExitStack()
    with es:
        sb = lambda n, s, d=F32: es.enter_context(nc.sbuf_tensor(n, list(s), d))
        RA = sb("RA", (128, JC * NT // 2))
        RB = sb("RB", (128, KC * NT))
        WR = sb("WR", (128, 4224), F32)
        RA_x = RA[:, 0:KC * NT].rearrange("p (k n) -> p k n", k=KC)
        RA_act = RA[:].bitcast(BF16).rearrange("p (j n) -> p j n", j=JC)
        RB_y = RB[:].rearrange("p (k n) -> p k n", k=KC)
        RB_bf = RB[:].bitcast(BF16)
        RB_h = RB_bf[:, 0:KC * NT].rearrange("p (k n) -> p k n", k=KC)
        RB_m = RB_bf[:, KC * NT:2 * KC * NT].rearrange("p (k n) -> p k n", k=KC)
        WRb = WR[:].bitcast(BF16)
        RSTD = sb("RSTD", (128, NT))
        TMP1 = sb("TMP1", (128, NT))
        SQ = [sb("SQ%d" % i, (128, NT), BF16) for i in range(2)]
        CARRY = sb("CARRY", (128, 1))
        GAINS = sb("GAINS", (128, 8, KC))
        CWA = sb("CWA", (128, 8, 3))
        CWQ = sb("CWQ", (128, 24, 4))
        ONESB = sb("ONESB", (128, 128), BF16)
        ONES1 = sb("ONES1", (128, 128), BF16)
        ONESF = sb("ONESF", (128, 128))
        NONESF = sb("NONESF", (128, 128))
        IDB = sb("IDB", (128, 128), BF16)
        CM = sb("CM", (128, 6, 128))
        SM = sb("SM", (32, 5, 32))
        SEQSEL = sb("SEQSEL", (32, NSQ))
        SEQMB = sb("SEQMB", (128, NSQ, 32))
        DNN = sb("DNN", (128, 1))
        ALOG = sb("ALOG", (8, 1))
        DTB = sb("DTB", (8, 1))
        NEXPA = sb("NEXPA", (8, 1))
        HISTA = sb("HISTA", (128, 8, 2))
        HISTQ = sb("HISTQ", (128, 24, 3))
        SST = sb("SST", (128, 8, 128))
        ALLPS = es.enter_context(nc.psum_tensor("ALLPS", [128, 8, 512], F32))
        PTt = [ALLPS[:, 0:3, :], ALLPS[:, 3:6, :]]
        PM = [ALLPS[:, 6, :], ALLPS[:, 7, :]]

        B = {}

        def bf(name):
            if name not in B:
                B[name] = Buf()
            return B[name]

        def bk(i):
            name = "BK%d" % i
            if name not in B:
                B[name] = Buf(excl=True)
            return B[name]

        def ptb(a):
            return [bk(3 * a), bk(3 * a + 1), bk(3 * a + 2)]

        def pt3(a, m=128):
            return PTt[a][0:m, :, 0:TGW]

        def v3(ap2d):
            return ap2d.rearrange("p (t n) -> p t n", t=3)

        P.dma("sp", GAINS[:], gains, [], bf("GAINS"))
        P.dma("sp", CWA[:], cw_a, [], bf("CWA"))
        P.dma("sp", CWQ[:], cw_q, [], bf("CWQ"))
        P.dma("sp", CM[:], cmask.rearrange("m p n -> p m n"), [], bf("CM"))
        P.dma("sp", SM[:], smask.rearrange("m p n -> p m n"), [], bf("SM"))
        P.dma("sp", SEQSEL[:], seqsel, [], bf("SEQSEL"))
        P.dma("sp", SEQMB[:], seqmb, [], bf("SEQMB"))
        P.dma("sp", DNN[:], dnn, [], bf("DNN"))
        P.dma("sp", CARRY[:], carry, [], bf("CARRY"))
        P.dma("sp", ALOG[:], alog.rearrange("(h o) -> h o", o=1), [], bf("ALOG"))
        P.dma("sp", DTB[:], dtb.rearrange("(h o) -> h o", o=1), [], bf("DTB"))
        P.op("dve", lambda e: e.memset(ONESB[:], 1.0 / D), [], [bf("ONESB")])
        P.op("dve", lambda e: e.memset(ONES1[:], 1.0), [], [bf("ONES1")])
        P.op("dve", lambda e: e.memset(ONESF[:], 1.0), [], [bf("ONESF")])
        P.op("dve", lambda e: e.memset(NONESF[:], -1.0), [], [bf("NONESF")])
        P.op("dve", lambda e: e.tensor_copy(out=IDB[:], in_=CM[:, 5, :]), [bf("CM")], [bf("IDB")])
        P.op("dve", lambda e: e.memset(HISTA[:], 0.0), [], [bf("HISTA")])
        P.op("dve", lambda e: e.memset(HISTQ[:], 0.0), [], [bf("HISTQ")])
        P.op("dve", lambda e: e.memset(SST[:], 0.0), [], [bf("SST")])
        P.op("act", lambda e: e.activation(out=NEXPA[:], in_=ALOG[:], func=AF.Exp), [bf("ALOG")], [bf("NEXPA")])
        P.op("dve", lambda e: e.tensor_scalar(out=NEXPA[:], in0=NEXPA[:], scalar1=-1.0, scalar2=None, op0=ALU.mult),
             [bf("NEXPA")], [bf("NEXPA")])
        TRI, SFX, NEGI, NEGT, STRICT, IDENT = [CM[:, i, :] for i in range(6)]

        wstate = {"n": 0}

        class WTiles:
            def __init__(self, srcs, slot_elems, nslots, view):
                self.srcs = srcs
                self.view = view
                self.nslots = nslots
                self.slot_elems = slot_elems
                self.issued = 0
                wstate["n"] += 1
                self.tag = "W%d_" % wstate["n"]
                self.prefetch(nslots)

            def slot(self, i):
                s = i % self.nslots
                return self.view(WRb[:, s * self.slot_elems:(s + 1) * self.slot_elems]), bf(self.tag + str(s))

            def prefetch(self, upto):
                while self.issued < min(upto, len(self.srcs)):
                    ap, b = self.slot(self.issued)
                    P.dma("pool", ap, self.srcs[self.issued], [], b)
                    self.issued += 1

            def get(self, i):
                assert self.issued > i
                return self.slot(i)

            def done(self, i):
                self.prefetch(i + self.nslots + 1)

        def stats(src3, nk, ones, pa):
            for k in range(nk):
                q = SQ[k % 2]
                P.op("act", lambda e, k=k, q=q: e.activation(out=q[:], in_=src3[:, k, :], func=AF.Square),
                     [bf("SRC")], [bf("SQ%d" % (k % 2))])
                fns = [lambda e, t=t, k=k, q=q: e.matmul(PTt[pa][:, t, 0:TGW], lhsT=ones[:], rhs=q[:, t * TGW:(t + 1) * TGW],
                                                         start=(k == 0), stop=(k == nk - 1)) for t in range(3)]
                P.group("pe", fns, [bf("SQ%d" % (k % 2)), bf("ONES")], [*ptb(pa)])

        def rstd_from(pa, scale=1.0):
            P.op("act", lambda e: e.activation(out=v3(TMP1[:]), in_=pt3(pa), func=AF.Sqrt, bias=EPSB[:, 0:1], scale=scale),
                 [*ptb(pa), bf("EPSB")], [bf("TMP1")])
            P.op("dve", lambda e: e.reciprocal(out=RSTD[:], in_=TMP1[:]), [bf("TMP1")], [bf("RSTD")])

        EPSB = sb("EPSB", (128, 1))
        P.op("dve", lambda e: e.memset(EPSB[:], EPS), [], [bf("EPSB")])
        P.barrier()

        def prenorm(gi, have_stats=False):
            if have_stats:
                rstd_from(1)
            else:
                stats(RA_x, KC, ONESB, 0)
                rstd_from(0)
            for k in range(KC):
                P.op("dve", lambda e, k=k: e.scalar_tensor_tensor(out=RB_h[:, k, :], in0=RA_x[:, k, :],
                                                                  scalar=GAINS[:, gi, k:k + 1], in1=RSTD[:],
                                                                  op0=ALU.mult, op1=ALU.mult),
                     [bf("SRC"), bf("RSTD"), bf("GAINS")], [bf("HB")])

        def post_residual(y3, gi, coef, res_src, g, last=False, next_stats=False):
            stats(y3, KC, ONESB, 0)
            rstd_from(0)
            NXS = 5
            XS2 = [RA[:, (KC + i) * NT:(KC + i + 1) * NT] for i in range(NXS)]

            def load(m):
                P.dma("sp", XS2[m % NXS], res_src[m * 128:(m + 1) * 128, :], [bf("XS%d" % m)], bf("XST%d" % (m % NXS)))
            for m in range(NXS - 1):
                load(m)
            for m in range(KC):
                xs = XS2[m % NXS]
                if m + NXS - 1 < KC:
                    load(m + NXS - 1)
                P.op("dve", lambda e, m=m: e.scalar_tensor_tensor(out=y3[:, m, :], in0=y3[:, m, :],
                                                                  scalar=GAINS[:, gi, m:m + 1], in1=RSTD[:],
                                                                  op0=ALU.mult, op1=ALU.mult),
                     [bf("SRC"), bf("RSTD"), bf("GAINS")], [bf("SRC")])
                P.op("dve", lambda e, m=m, xs=xs: e.scalar_tensor_tensor(out=RA_x[:, m, :], in0=y3[:, m, :], scalar=coef,
                                                                         in1=xs, op0=ALU.mult, op1=ALU.add),
                     [bf("SRC"), bf("XST%d" % (m % NXS))], [bf("XN%d" % m)])
                if next_stats:
                    q = SQ[m % 2]
                    P.op("act", lambda e, m=m, q=q: e.activation(out=q[:], in_=RA_x[:, m, :], func=AF.Square),
                         [bf("XN%d" % m)], [bf("SQ%d" % (m % 2))])
                    fns = [lambda e, t=t, m=m, q=q: e.matmul(PTt[1][:, t, 0:TGW], lhsT=ONESB[:], rhs=q[:, t * TGW:(t + 1) * TGW],
                                                             start=(m == 0), stop=(m == KC - 1)) for t in range(3)]
                    P.group("pe", fns, [bf("SQ%d" % (m % 2))], [*ptb(1)])
                dst = yT[m * 128:(m + 1) * 128, :] if last else xs_d[m * 128:(m + 1) * 128, :]
                P.dma("sp", dst, RA_x[:, m, :], [bf("XN%d" % m)], bf("YOUT%d" % (m % 4)) if last else bf("XS%d" % m))

        def ffn(wgu, wd):
            t1 = WTiles([wgu[j].rearrange("(k p) n -> p k n", p=128) for j in range(JC)], 4096, 2,
                        lambda ap: ap.rearrange("p (k n) -> p k n", k=KC))
            for j in range(JC):
                w, wbuf = t1.get(j)
                for which in range(2):
                    fns = [lambda e, t=t, k=k, w=w, which=which: e.matmul(
                        PTt[which][:, t, 0:TGW], lhsT=w[:, k, which * 128:(which + 1) * 128],
                        rhs=RB_h[:, k, t * TGW:(t + 1) * TGW], start=(k == 0), stop=(k == KC - 1))
                        for t in range(3) for k in range(KC)]
                    P.group("pe", fns, [wbuf, bf("HB")], [*ptb(which)])
                t1.done(j)
                P.op("act", lambda e: e.activation(out=v3(TMP1[:]), in_=pt3(0), func=AF.Silu), [*ptb(0)], [bf("TMP1")])
                P.op("dve", lambda e, j=j: e.tensor_tensor(out=v3(RA_act[:, j, :]), in0=v3(TMP1[:]), in1=pt3(1), op=ALU.mult),
                     [bf("TMP1"), *ptb(1)], [bf("ACT")])
            P.barrier()
            t2 = WTiles([wd[m, hf] for m in range(KC) for hf in range(4)], 1408, 6,
                        lambda ap: ap.rearrange("p (j n) -> p j n", j=11))
            for m in range(KC):
                wl = [t2.get(4 * m + hf) for hf in range(4)]
                pa = m % 2
                for hf in range(4):
                    fns = [lambda e, t=t, hf=hf, jj=jj, pa=pa, w=wl[hf][0]: e.matmul(
                        PTt[pa][:, t, 0:TGW], lhsT=w[:, jj, :], rhs=RA_act[:, hf * 11 + jj, t * TGW:(t + 1) * TGW],
                        start=(hf == 0 and jj == 0), stop=(hf == 3 and jj == 10))
                        for t in range(3) for jj in range(11)]
                    P.group("pe", fns, [wl[hf][1], bf("ACT")], [*ptb(pa)])
                    t2.done(4 * m + hf)
                P.op("act", lambda e, m=m, pa=pa: e.activation(out=v3(RB_y[:, m, :]), in_=pt3(pa), func=AF.Copy),
                     [*ptb(pa)], [bf("SRC")])
            P.barrier()

        def proj_tile(w, wbuf, pa, m=128):
            fns = [lambda e, t=t, k=k: e.matmul(PTt[pa][0:m, t, 0:TGW], lhsT=w[:, k, 0:m],
                                                rhs=RB_h[:, k, t * TGW:(t + 1) * TGW], start=(k == 0), stop=(k == KC - 1))
                   for t in range(3) for k in range(KC)]
            P.group("pe", fns, [wbuf, bf("HB")], [*ptb(pa)])


        WAs = sb("WAs", (128, KC, 8), BF16)
        WBs = sb("WBs", (128, KC, 8), BF16)
        SSQ = sb("SSQ", (128, 2))
        ZS2 = sb("ZS2", (128, NT))

        def PMq(a, q):
            return ALLPS[:, a * 4 + q, 0:128]

        def pmb(a, q):
            return bk(a * 4 + q)

        def run_lockstep(gens):
            gens = list(gens)
            while gens:
                for g_ in list(gens):
                    try:
                        next(g_)
                    except StopIteration:
                        gens.remove(g_)

        def run_pool(tasks, nlanes, stagger, extra=()):
            tasks = list(tasks)
            active = {}
            extra = list(extra)
            rnd = 0
            started = 0
            while tasks or active or extra:
                for L in range(nlanes):
                    if L not in active and tasks and rnd >= started * stagger:
                        active[L] = tasks.pop(0)(L)
                        started += 1
                for L in list(active):
                    try:
                        next(active[L])
                    except StopIteration:
                        del active[L]
                for g_ in list(extra):
                    try:
                        next(g_)
                    except StopIteration:
                        extra.remove(g_)
                rnd += 1

        def pool_gen(tasks, nlanes, stagger):
            tasks = list(tasks)
            active = {}
            rnd = 0
            started = 0
            while tasks or active:
                for L in range(nlanes):
                    if L not in active and tasks and rnd >= started * stagger:
                        active[L] = tasks.pop(0)(L)
                        started += 1
                for L in list(active):
                    try:
                        next(active[L])
                    except StopIteration:
                        del active[L]
                rnd += 1
                yield

        def run_weighted(gens_w):
            gens_w = [[g_, w_] for g_, w_ in gens_w]
            while gens_w:
                for item in list(gens_w):
                    for _ in range(item[1]):
                        try:
                            next(item[0])
                        except StopIteration:
                            gens_w.remove(item)
                            break

        def BKf(i):
            return ALLPS[:, i, :]

        def BKb(i):
            return ALLPS[:, i, :].bitcast(BF16)

        def mixer(g, so):
            off = [0]

            def ra(n, name=None):
                a = RA[:, off[0]:off[0] + n]
                off[0] += n
                assert off[0] <= JC * NT // 2, off[0]
                return a

            def rab(n, name=None):
                return ra((n + 1) // 2).bitcast(BF16)[:, 0:n]
            CIP = ra(1032)
            CIS = ra(NSEQ * 7).rearrange("p (s t) -> p s t", s=NSEQ)
            FL = ra(NT)
            T1 = ra(NT)
            QNb = [rab(NT), rab(NT), rab(NT)]
            KNb = [rab(NT), rab(NT)]; VSb = rab(NT)
            ZS = [ra(NT), ra(NT), ZS2[:]]
            GFM = FL[0:8, :]; BFM = T1[0:8, :]
            GTOK = ra(NCK * 8).rearrange("p (c h) -> p c h", c=NCK)
            BTOK = ra(NCK * 8).rearrange("p (c h) -> p c h", c=NCK)
            NBTOK = ra(NCK * 8).rearrange("p (c h) -> p c h", c=NCK)
            KTOKs = [rab(NCK * 128).rearrange("p (c n) -> p c n", c=NCK) for _ in range(2)]
            VTOKs = [rab(NCK * 128).rearrange("p (c n) -> p c n", c=NCK) for _ in range(2)]
            WUs2 = [rab(NCK * 256).rearrange("p (c j n) -> p c j n", c=NCK, j=2) for _ in range(2)]
            QKTs2 = [rab(NCK * 128).rearrange("p (c n) -> p c n", c=NCK) for _ in range(2)]
            KPs2 = [rab(NCK * 128).rearrange("p (c n) -> p c n", c=NCK) for _ in range(2)]
            SMX2 = [ra(NCK * 3).rearrange("p (c j) -> p c j", c=NCK) for _ in range(2)]
            EGLS2 = [ra(NSB * NSQ), ra(NSB * NSQ)]
            RS = ra(1)
            nck = NCH if so else NCK

            def ccols(c):
                return slice(c * 128, (c + 1) * 128) if c < NCH else slice(NP + 32 * (c - NCH), NP + 32 * (c - NCH + 1))
            lanes = []
            for L in range(4):
                lanes.append(dict(Gm=ra(128), Es=ra(128), ET=ra(128), Nm=rab(128), NTt=rab(128),
                                  PP=[rab(256).rearrange("p (j n) -> p j n", j=2), rab(256).rearrange("p (j n) -> p j n", j=2)],
                                  RT=rab(128), KBG=rab(128), VB=rab(128), BEG=ra(1), GSEL=ra(NSQ)))
            VNb = rab(128); T2 = ra(128); Oo = ra(128); ON = ra(128); Sb = rab(128)
            WTpad = rab(NSQ * 32).rearrange("p (s i) -> p s i", s=NSQ)
            QNpad = rab(NSQ * 32).rearrange("p (s i) -> p s i", s=NSQ)
            KPpad = rab(NSQ * 128).rearrange("p (s n) -> p s n", s=NSQ)
            S0h = ra(NSQ * 128).rearrange("p (s n) -> p s n", s=NSQ)
            S0b = rab(NSQ * 128).rearrange("p (s n) -> p s n", s=NSQ)

            if so:
                tl = [3 * i + j for i in range(8) for j in (0, 1)] + [24 + 4 * h + j for h in range(8) for j in (0, 1, 2)]
            else:
                tl = list(range(56))
            tpos = {t: i for i, t in enumerate(tl)}
            wt0 = WTiles([win[t].rearrange("(k p) n -> p k n", p=128) for t in tl], 2048, 4,
                         lambda ap: ap.rearrange("p (k n) -> p k n", k=KC))

            class _WT:
                def get(self, t):
                    return wt0.get(tpos[t])

                def done(self, t):
                    wt0.done(tpos[t])
            wt = _WT()
            P.dma("pool", WAs[:], wa.rearrange("(k p) n -> p k n", p=128), [], bf("WAs"))
            P.dma("pool", WBs[:], wb.rearrange("(k p) n -> p k n", p=128), [], bf("WBs"))
            T1s = T1[:, NP:NP + NS].rearrange("p (s t) -> p s t", s=NSEQ)
            FLs = FL[:, NP:NP + NS].rearrange("p (s t) -> p s t", s=NSEQ)

            def conv(wts, ci, ntap):
                for j in range(ntap):
                    wj = wts[:, ci, j:j + 1]
                    if j == 0:
                        P.op("dve", lambda e, wj=wj: e.tensor_scalar(out=T1[:, 0:NP], in0=CIP[:, 0:NP], scalar1=wj,
                                                                    scalar2=None, op0=ALU.mult), [bf("CIN")], [bf("T1")])
                        if not so:
                            P.op("dve", lambda e, wj=wj: e.tensor_scalar(out=T1s, in0=CIS[:, :, 0:4], scalar1=wj,
                                                                        scalar2=None, op0=ALU.mult), [bf("CIN")], [bf("T1")])
                    else:
                        P.op("dve", lambda e, wj=wj, j=j: e.scalar_tensor_tensor(
                            out=T1[:, 0:NP], in0=CIP[:, j:j + NP], scalar=wj, in1=T1[:, 0:NP], op0=ALU.mult, op1=ALU.add),
                            [bf("CIN"), bf("T1")], [bf("T1")])
                        if not so:
                            P.op("dve", lambda e, wj=wj, j=j: e.scalar_tensor_tensor(
                                out=T1s, in0=CIS[:, :, j:j + 4], scalar=wj, in1=T1s, op0=ALU.mult, op1=ALU.add),
                                [bf("CIN"), bf("T1")], [bf("T1")])
                    yield

            def hist_proj(t, ncol, pa):
                w, wb_ = wt.get(t)
                fns = [lambda e, k=k, w=w: e.matmul(PTt[pa][:, 0, 0:ncol], lhsT=w[:, k, :], rhs=RB_h[:, k, NP - ncol:NP],
                                                    start=(k == 0), stop=(k == KC - 1)) for k in range(KC)]
                P.group("pe", fns, [wb_, bf("HB")], [*ptb(pa)])
                wt.done(t)

            for i in range(8 if so else 0):
                hist_proj(3 * i, 2, 0)
                P.op("act", lambda e: e.activation(out=TMP1[:, 0:2], in_=PTt[0][:, 0, 0:2], func=AF.Copy), [*ptb(0)], [bf("TMP1")])
                hist_proj(3 * i + 1, 2, 1)
                P.op("dve", lambda e, i=i: e.tensor_tensor(out=HISTA[:, i, :], in0=TMP1[:, 0:2], in1=PTt[1][:, 0, 0:2], op=ALU.mult),
                     [bf("TMP1"), *ptb(1)], [bf("HISTA")])
            for i in range(0 if so else 8):
                P.op("dve", lambda e, i=i: e.tensor_copy(out=CIP[:, 0:2], in_=HISTA[:, i, :]), [bf("HISTA")], [bf("CIN")])
                P.dma("sp", CIS[:, :, 0:2], hist_a[i], [], bf("CIN"))
                w, wb_ = wt.get(3 * i)
                proj_tile(w, wb_, 0)
                wt.done(3 * i)
                P.op("act", lambda e: e.activation(out=v3(TMP1[:]), in_=pt3(0), func=AF.Copy), [*ptb(0)], [bf("TMP1")])
                w, wb_ = wt.get(3 * i + 1)
                proj_tile(w, wb_, 1)
                wt.done(3 * i + 1)
                P.op("dve", lambda e: e.tensor_tensor(out=v3(FL[:]), in0=v3(TMP1[:]), in1=pt3(1), op=ALU.mult),
                     [bf("TMP1"), *ptb(1)], [bf("FL")])
                P.op("act", lambda e: e.activation(out=CIP[:, 2:2 + NP], in_=FL[:, 0:NP], func=AF.Copy), [bf("FL")], [bf("CIN")])
                P.op("dve", lambda e: e.tensor_copy(out=CIS[:, :, 2:6], in_=FLs), [bf("FL")], [bf("CIN")])
                for _ in conv(CWA, i, 3):
                    pass
                P.op("dve", lambda e, i=i: e.tensor_copy(out=HISTA[:, i, :], in_=CIP[:, NP:NP + 2]), [bf("CIN")], [bf("HISTA")])
                P.dma("sp", o_ca_s[i], CIS[:, :, 4:6], [bf("CIN")], bf("OCAS"))
                w, wb_ = wt.get(3 * i + 2)
                proj_tile(w, wb_, 0)
                wt.done(3 * i + 2)
                P.op("act", lambda e: e.activation(out=v3(RSTD[:]), in_=pt3(0), func=AF.Copy), [*ptb(0)], [bf("RSTD")])
                P.op("dve", lambda e, i=i: e.tensor_tensor(out=RB_m[:, i, :], in0=T1[:], in1=RSTD[:], op=ALU.mult),
                     [bf("T1"), bf("RSTD")], [bf("YMIX")])
            checkpoint("mixA", RB_y[:, 8:16, :], nk=8)

            for pa, wsm, nm in ((0, WAs, "WAs"), (1, WBs, "WBs")):
                fns = [lambda e, t=t, k=k, pa=pa, wsm=wsm: e.matmul(PTt[pa][0:8, t, 0:TGW], lhsT=wsm[:, k, :],
                                                                    rhs=RB_h[:, k, t * TGW:(t + 1) * TGW],
                                                                    start=(k == 0), stop=(k == KC - 1))
                       for t in range(3) for k in range(KC)]
                P.group("pe", fns, [bf(nm), bf("HB")], [*ptb(pa)])
            P.op("act", lambda e: e.activation(out=v3(GFM), in_=pt3(0, 8), func=AF.Exp, bias=DTB[:, 0:1]),
                 [*ptb(0), bf("DTB")], [bf("FL")])
            P.op("act", lambda e: e.activation(out=GFM, in_=GFM, func=AF.Ln, bias=ONESF[0:8, 0:1]),
                 [bf("FL"), bf("ONESF")], [bf("FL")])
            P.op("dve", lambda e: e.tensor_scalar(out=GFM, in0=GFM, scalar1=NEXPA[:, 0:1], scalar2=None,
                                                  op0=ALU.mult), [bf("FL"), bf("NEXPA")], [bf("FL")])
            P.op("act", lambda e: e.activation(out=v3(BFM), in_=pt3(1, 8), func=AF.Sigmoid), [*ptb(1)], [bf("T1")])
            for a, src, nm in ((0, GFM, "FL"), (1, BFM, "T1")):
                fns = [lambda e, c=c, a=a, src=src: e.transpose(out=PM[a][:, c * 8:(c + 1) * 8],
                                                                 in_=src[:, c * 128:(c + 1) * 128], identity=CM[0:8, 5, 0:8])
                       for c in range(NCH)]
                for b_ in range(0 if so else NSB):
                    fns.append(lambda e, a=a, src=src, b_=b_: e.transpose(out=PM[a][0:32, 64 + 8 * b_:72 + 8 * b_],
                                                                          in_=src[:, NP + 32 * b_:NP + 32 * b_ + 32],
                                                                          identity=CM[0:8, 5, 0:8]))
                P.group("pe", fns, [bf(nm), bf("CM")], [bk(6 + a)])
            for a, dst, nm in ((0, GTOK, "GTOK"), (1, BTOK, "BTOK")):
                P.op("dve", lambda e, a=a, dst=dst: e.tensor_copy(out=dst[:, 0:8, :],
                                                                  in_=PM[a][:, 0:64].rearrange("p (c h) -> p c h", c=8)),
                     [bk(6 + a)], [bf(nm)])
                if not so:
                    P.op("dve", lambda e, a=a, dst=dst: e.tensor_copy(out=dst[0:32, 8:8 + NSB, :],
                                                                      in_=PM[a][0:32, 64:64 + 8 * NSB].rearrange("p (c h) -> p c h", c=NSB)),
                         [bk(6 + a)], [bf(nm)])
            P.op("dve", lambda e: e.tensor_scalar(out=NBTOK[:, 0:8, :], in0=BTOK[:, 0:8, :], scalar1=-1.0, scalar2=None,
                                                  op0=ALU.mult), [bf("BTOK")], [bf("NBTOK")])
            if not so:
                P.op("dve", lambda e: e.tensor_scalar(out=NBTOK[0:32, 8:8 + NSB, :], in0=BTOK[0:32, 8:8 + NSB, :], scalar1=-1.0, scalar2=None,
                                                      op0=ALU.mult), [bf("BTOK")], [bf("NBTOK")])
            checkpoint("gb", RB_y[:, 8:16, :], nk=8)

            def stageP(h):
                par = h % 2
                par3 = h % 3
                for qi in range(3):
                    cch = qi * 8 + h
                    if so and qi == 0:
                        hist_proj(24 + 4 * h, 3, 0)
                        P.op("dve", lambda e, cch=cch: e.tensor_copy(out=HISTQ[:, cch, :], in_=PTt[0][:, 0, 0:3]), [*ptb(0)], [bf("HISTQ")])
                        yield
                        continue
                    P.op("dve", lambda e, cch=cch: e.tensor_copy(out=CIP[:, 0:3], in_=HISTQ[:, cch, :]), [bf("HISTQ")], [bf("CIN")])
                    if not so:
                        P.dma("sp", CIS[:, :, 0:3], hist_q[cch], [], bf("CIN"))
                    w, wb_ = wt.get(24 + 4 * h + qi)
                    pa = 0
                    proj_tile(w, wb_, pa)
                    wt.done(24 + 4 * h + qi)
                    yield
                    P.op("act", lambda e, pa=pa: e.activation(out=v3(FL[:]), in_=pt3(pa), func=AF.Copy), [*ptb(pa)], [bf("FL")])
                    yield
                    P.op("act", lambda e: e.activation(out=CIP[:, 3:3 + NP], in_=FL[:, 0:NP], func=AF.Copy), [bf("FL")], [bf("CIN")])
                    if not so:
                        P.op("dve", lambda e: e.tensor_copy(out=CIS[:, :, 3:7], in_=FLs), [bf("FL")], [bf("CIN")])
                    yield
                    yield from conv(CWQ, cch, 4)
                    P.op("dve", lambda e, cch=cch: e.tensor_copy(out=HISTQ[:, cch, :], in_=CIP[:, NP:NP + 3]), [bf("CIN")], [bf("HISTQ")])
                    if not so:
                        P.dma("sp", o_cq_s[cch], CIS[:, :, 4:7], [bf("CIN")], bf("OCQS"))
                    yield
                    if qi == 2:
                        P.op("act", lambda e: e.activation(out=VSb, in_=T1[:], func=AF.Silu), [bf("T1")], [bf("VSb")])
                        yield
                        continue
                    P.op("act", lambda e: e.activation(out=FL[:], in_=T1[:], func=AF.Silu), [bf("T1")], [bf("FL")])
                    yield
                    P.op("act", lambda e: e.activation(out=SQ[0][:], in_=FL[:], func=AF.Square), [bf("FL")], [bf("SQ0")])
                    yield
                    fns = [lambda e, t=t, pa=pa: e.matmul(PTt[pa][:, t, 0:TGW], lhsT=ONES1[:], rhs=SQ[0][:, t * TGW:(t + 1) * TGW],
                                                          start=True, stop=True) for t in range(3)]
                    P.group("pe", fns, [bf("SQ0")], [*ptb(pa)])
                    yield
                    rstd_from(pa)
                    yield
                    if qi == 0:
                        P.op("dve", lambda e, par3=par3: e.scalar_tensor_tensor(out=QNb[par3], in0=FL[:], scalar=128.0 ** -0.5, in1=RSTD[:],
                                                                             op0=ALU.mult, op1=ALU.mult),
                             [bf("FL"), bf("RSTD")], [bf("QNb%d" % par3)])
                    else:
                        P.op("dve", lambda e, par=par: e.tensor_tensor(out=KNb[par], in0=FL[:], in1=RSTD[:], op=ALU.mult),
                             [bf("FL"), bf("RSTD")], [bf("KNb%d" % par)])
                    yield
                if not so:
                    w, wb_ = wt.get(24 + 4 * h + 3)
                    proj_tile(w, wb_, 0)
                    wt.done(24 + 4 * h + 3)
                    yield
                    P.op("act", lambda e, par3=par3: e.activation(out=v3(ZS[par3][:]), in_=pt3(0), func=AF.Silu), [*ptb(0)], [bf("ZS%d" % par3)])
                    yield
                n = 0
                for src, snm, dst, dnm in ((KNb[par], "KNb%d" % par, KTOKs[par], "KTOK%d" % par), (VSb, "VSb", VTOKs[par], "VTOK%d" % par)):
                    for c in range(nck):
                        C = 128 if c < NCH else 32
                        cs = ccols(c)
                        bi = n % 3
                        n += 1
                        P.op("pe", lambda e, src=src, cs=cs, bi=bi, C=C: e.transpose(out=BKb(bi)[0:C, 0:128], in_=src[:, cs], identity=IDB[:]),
                             [bf(snm), bf("IDB")], [bk(bi)])
                        if n % 2:
                            P.op("act", lambda e, dst=dst, c=c, bi=bi, C=C: e.activation(out=dst[0:C, c, :], in_=BKb(bi)[0:C, 0:128], func=AF.Copy),
                                 [bk(bi)], [bf(dnm)])
                        else:
                            P.op("dve", lambda e, dst=dst, c=c, bi=bi, C=C: e.tensor_copy(out=dst[0:C, c, :], in_=BKb(bi)[0:C, 0:128]),
                                 [bk(bi)], [bf(dnm)])
                        yield

            def prep(h, c, L):
                par = h % 2
                par3 = h % 3
                WUs, QKTs, KPs, SMX, EGLS = WUs2[par], QKTs2[par], KPs2[par], SMX2[par], EGLS2[par]
                WTs, Us = WUs[:, :, 0, :], WUs[:, :, 1, :]
                sfx_ = str(par)
                ln = lanes[L]
                nm = lambda x: "%s_%d" % (x, L)
                bL = 3 + L
                KTOK, VTOK = KTOKs[par], VTOKs[par]
                ktn, vtn, knn = "KTOK%d" % par, "VTOK%d" % par, "KNb%d" % par
                KNbp = KNb[par]
                RGA = BKf(bL)[:, 0:128]
                RGB = BKf(bL)[:, 128:256]
                RGAB = BKf(bL)[:, 0:256].rearrange("p (j n) -> p j n", j=2)
                RGC = BKf(bL)[:, 256:384]
                RGCb = BKb(bL)[:, 512:640]
                samp = c >= NCH
                C = 32 if samp else 128
                cs = ccols(c)
                if samp:
                    mTRI, mSFX, mNEGS, mNEGT = [SM[:, i, :] for i in range(4)]
                    mnm = "SM"
                else:
                    mTRI, mSFX, mNEGS, mNEGT = [CM[:, i, :] for i in range(4)]
                    mnm = "CM"
                OF = ONESF[0:C, 0:C]
                NOF = NONESF[0:C, 0:C]
                gcol = GTOK[0:C, c, h:h + 1]
                bcol = BTOK[0:C, c, h:h + 1]
                nbcol = NBTOK[0:C, c, h:h + 1]
                Gm, Es, ET, Nm, NTt, RT, KBG, VB = [ln[k][0:C, 0:C] if k not in ("KBG", "VB") else ln[k][0:C, :]
                                                    for k in ("Gm", "Es", "ET", "Nm", "NTt", "RT", "KBG", "VB")]
                BEG, GSEL = ln["BEG"][0:C, :], ln["GSEL"][0:C, :]
                IDf = CM[0:C, 5, 0:C]
                P.op("dve", lambda e: e.tensor_scalar(out=Gm, in0=mTRI, scalar1=gcol, scalar2=None, op0=ALU.mult),
                     [bf(mnm), bf("GTOK")], [bf(nm("Gm"))])
                if samp:
                    P.op("dve", lambda e: e.tensor_scalar(out=GSEL, in0=SEQSEL[:], scalar1=gcol, scalar2=None, op0=ALU.mult),
                         [bf("SEQSEL"), bf("GTOK")], [bf(nm("GSEL"))])
                yield
                D_ = RGA[0:C, 0:C]
                DT_ = RGB[0:C, 0:C]
                sm_ = BKf(bL)[:, 384:384 + 2 + NSQ]
                fns = [lambda e: e.matmul(D_, lhsT=Gm, rhs=OF, start=True, stop=False),
                       lambda e: e.matmul(D_, lhsT=NOF, rhs=Gm, start=False, stop=False),
                       lambda e: e.matmul(D_, lhsT=IDf, rhs=mNEGS, start=False, stop=True)]
                if not so:
                    fns += [lambda e: e.matmul(DT_, lhsT=OF, rhs=Gm, start=True, stop=False),
                            lambda e: e.matmul(DT_, lhsT=Gm, rhs=NOF, start=False, stop=False),
                            lambda e: e.matmul(DT_, lhsT=IDf, rhs=mNEGT, start=False, stop=True)]
                fns += [lambda e: e.matmul(sm_[0:C, 0:1], lhsT=mTRI, rhs=gcol, start=True, stop=True),
                        lambda e: e.matmul(sm_[0:C, 1:2], lhsT=mSFX, rhs=gcol, start=True, stop=True)]
                if samp:
                    fns.append(lambda e: e.matmul(sm_[:, 2:2 + NSQ], lhsT=ONESF[0:C, :], rhs=GSEL, start=True, stop=True))
                else:
                    fns.append(lambda e: e.matmul(sm_[:, 2:3], lhsT=ONESF[0:C, :], rhs=gcol, start=True, stop=True))
                P.group("pe", fns, [bf(nm("Gm")), bf(nm("GSEL")), bf(mnm), bf("GTOK")], [bk(bL)])
                yield
                P.op("act", lambda e: e.activation(out=Es, in_=D_, func=AF.Exp), [bk(bL)], [bf(nm("Es"))])
                yield
                if not so:
                    P.op("act", lambda e: e.activation(out=ET, in_=DT_, func=AF.Exp), [bk(bL)], [bf(nm("ET"))])
                    yield
                if samp:
                    P.op("act", lambda e: e.activation(out=SMX[0:C, c, 0:2], in_=sm_[0:C, 0:2], func=AF.Exp), [bk(bL)], [bf("SMX" + sfx_)])
                    P.op("act", lambda e: e.activation(out=EGLS[:, (c - NCH) * NSQ:(c - NCH + 1) * NSQ], in_=sm_[:, 2:2 + NSQ], func=AF.Exp),
                         [bk(bL)], [bf("EGLS" + sfx_)])
                else:
                    P.op("act", lambda e: e.activation(out=SMX[:, c, :], in_=sm_[:, 0:3], func=AF.Exp), [bk(bL)], [bf("SMX" + sfx_)])
                yield
                P.op("dve", lambda e: e.tensor_tensor(out=BEG, in0=SMX[0:C, c, 0:1], in1=bcol, op=ALU.mult),
                     [bf("SMX" + sfx_), bf("BTOK")], [bf(nm("BEG"))])
                P.op("pe", lambda e: e.matmul(RGC[0:C, 0:C], lhsT=KNbp[:, cs], rhs=KNbp[:, cs], start=True, stop=True),
                     [bf(knn)], [bk(bL)])
                yield
                P.op("dve", lambda e: e.scalar_tensor_tensor(out=Nm, in0=RGC[0:C, 0:C], scalar=nbcol, in1=Es, op0=ALU.mult, op1=ALU.mult),
                     [bk(bL), bf("NBTOK"), bf(nm("Es"))], [bf(nm("Nm"))])
                yield
                P.op("pe", lambda e: e.transpose(out=RGCb[0:C, 0:C], in_=Nm, identity=IDB[0:C, 0:C]), [bf(nm("Nm")), bf("IDB")], [bk(bL)])
                yield
                P.op("act", lambda e: e.activation(out=NTt, in_=RGCb[0:C, 0:C], func=AF.Copy), [bk(bL)], [bf(nm("NTt"))])
                P.op("dve", lambda e: e.tensor_tensor(out=RT, in0=RGCb[0:C, 0:C], in1=IDB[0:C, 0:C], op=ALU.add),
                     [bk(bL), bf("IDB")], [bf(nm("RT"))])
                yield
                Pc, PTc, pcn, ptn = Nm, NTt, nm("Nm"), nm("NTt")
                nl = 1 if samp else 6
                for l in range(nl):
                    last = l == nl - 1
                    PPl = ln["PP"][l % 2]
                    Pn, PTn = PPl[0:C, 0, 0:C], PPl[0:C, 1, 0:C]
                    pnn = ptnn = nm("PP%d" % (l % 2))
                    fns = [lambda e, Pc=Pc, PTc=PTc: e.matmul(RGA[0:C, 0:C], lhsT=PTc, rhs=Pc, start=True, stop=True)]
                    if not last:
                        fns.append(lambda e, Pc=Pc, PTc=PTc: e.matmul(RGB[0:C, 0:C], lhsT=Pc, rhs=PTc, start=True, stop=True))
                    P.group("pe", fns, [bf(pcn), bf(ptn)], [bk(bL)])
                    yield
                    if last:
                        P.op("act", lambda e, Pn=Pn: e.activation(out=Pn, in_=RGA[0:C, 0:C], func=AF.Copy), [bk(bL)], [bf(pnn)])
                    else:
                        P.op("act", lambda e, PPl=PPl: e.activation(out=PPl[:, :, :], in_=RGAB,
                                                                    func=AF.Copy), [bk(bL)], [bf(pnn)])
                    yield
                    P.op("pe", lambda e, Pn=Pn: e.matmul(RGC[0:C, 0:C], lhsT=Pn, rhs=RT, start=True, stop=True),
                         [bf(pnn), bf(nm("RT"))], [bk(bL)])
                    yield
                    P.op("dve", lambda e: e.tensor_tensor(out=RT, in0=RGC[0:C, 0:C], in1=RT, op=ALU.add),
                         [bk(bL), bf(nm("RT"))], [bf(nm("RT"))])
                    yield
                    Pc, PTc, pcn, ptn = Pn, PTn, pnn, ptnn
                P.op("dve", lambda e: e.tensor_scalar(out=KBG, in0=KTOK[0:C, c, :], scalar1=BEG[:, 0:1], scalar2=None, op0=ALU.mult),
                     [bf(ktn), bf(nm("BEG"))], [bf(nm("KBG"))])
                P.op("dve", lambda e: e.tensor_scalar(out=VB, in0=VTOK[0:C, c, :], scalar1=bcol, scalar2=None, op0=ALU.mult),
                     [bf(vtn), bf("BTOK")], [bf(nm("VB"))])
                yield
                P.op("dve", lambda e: e.tensor_scalar(out=KPs[0:C, c, :], in0=KTOK[0:C, c, :], scalar1=SMX[0:C, c, 1:2], scalar2=None, op0=ALU.mult),
                     [bf(ktn), bf("SMX" + sfx_)], [bf("KPs" + sfx_)])
                fns = [lambda e: e.matmul(RGA[:, 0:C], lhsT=KBG, rhs=RT, start=True, stop=True),
                       lambda e: e.matmul(RGB[0:C, :], lhsT=RT, rhs=VB, start=True, stop=True)]
                P.group("pe", fns, [bf(nm("KBG")), bf(nm("VB")), bf(nm("RT"))], [bk(bL)])
                if not so:
                    P.op("pe", lambda e: e.matmul(RGC[0:C, 0:C], lhsT=KNbp[:, cs], rhs=QNb[par3][:, cs], start=True, stop=True),
                         [bf(knn), bf("QNb%d" % par3)], [bk(bL)])
                yield
                if samp:
                    P.op("act", lambda e: e.activation(out=WTs[:, c, 0:C], in_=RGA[:, 0:C], func=AF.Copy), [bk(bL)], [bf("WTs" + sfx_)])
                    P.op("act", lambda e: e.activation(out=Us[0:C, c, :], in_=RGB[0:C, :], func=AF.Copy), [bk(bL)], [bf("Us" + sfx_)])
                else:
                    P.op("act", lambda e: e.activation(out=WUs[:, c, :, :], in_=RGAB, func=AF.Copy),
                         [bk(bL)], [bf("WTs" + sfx_), bf("Us" + sfx_)])
                if not so:
                    P.op("dve", lambda e: e.tensor_tensor(out=QKTs[0:C, c, 0:C], in0=RGC[0:C, 0:C], in1=ET, op=ALU.mult),
                         [bk(bL), bf(nm("ET"))], [bf("QKTs" + sfx_)])
                yield

            def scan(h):
                par = h % 3
                par2 = h % 2
                WUs, QKTs, KPs, SMX, EGLS = WUs2[par2], QKTs2[par2], KPs2[par2], SMX2[par2], EGLS2[par2]
                WTs, Us = WUs[:, :, 0, :], WUs[:, :, 1, :]
                sfx_ = str(par2)
                S = SST[:, h, :]
                B7 = BKf(7)
                P.op("act", lambda e: e.activation(out=Sb, in_=S, func=AF.Copy), [bf("SST")], [bf("Sb")])
                yield
                pend = None

                def outpath(c, C, cs):
                    P.op("act", lambda e: e.activation(out=ON[0:C, :], in_=Oo[0:C, :], func=AF.Square, accum_out=SSQ[0:C, 0:1]),
                         [bf("O")], [bf("ON"), bf("SSQ")])
                    P.op("act", lambda e: e.activation(out=SSQ[0:C, 1:2], in_=SSQ[0:C, 0:1], func=AF.Sqrt, bias=EPSB[0:C, 0:1], scale=1.0 / 128),
                         [bf("SSQ"), bf("EPSB")], [bf("SSQ")])
                    yield
                    P.op("dve", lambda e: e.reciprocal(out=RS[0:C, :], in_=SSQ[0:C, 1:2]), [bf("SSQ")], [bf("RS")])
                    P.op("dve", lambda e: e.tensor_scalar(out=ON[0:C, :], in0=Oo[0:C, :], scalar1=RS[0:C, 0:1], scalar2=None, op0=ALU.mult),
                         [bf("O"), bf("RS")], [bf("ON")])
                    yield
                    P.op("pe", lambda e: e.transpose(out=B7[:, 0:C], in_=ON[0:C, :], identity=CM[0:C, 5, 0:C]), [bf("ON"), bf("CM")], [bk(7)])
                    yield
                    P.op("dve", lambda e: e.scalar_tensor_tensor(out=RB_m[:, 8 + h, cs], in0=B7[:, 0:C], scalar=DNN[:, 0:1],
                                                                 in1=ZS[par][:, cs], op0=ALU.mult, op1=ALU.mult),
                         [bk(7), bf("DNN"), bf("ZS%d" % par)], [bf("YMIX")])
                    yield

                for c in range(nck):
                    samp = c >= NCH
                    sbi = c - NCH
                    C = 32 if samp else 128
                    cs = ccols(c)
                    if not samp:
                        fns = [lambda e, c=c: e.matmul(B7[:, 0:128], lhsT=WTs[:, c, :], rhs=Sb, start=True, stop=True)]
                        if not so:
                            fns.append(lambda e, cs=cs: e.matmul(B7[:, 128:256], lhsT=QNb[par][:, cs], rhs=Sb, start=True, stop=True))
                        P.group("pe", fns, [bf("WTs" + sfx_), bf("QNb%d" % par), bf("Sb")], [bk(7)])
                    else:
                        P.dma("sp", S0h[:], s0[sbi * NSQ:(sbi + 1) * NSQ, h].rearrange("s p n -> p s n"), [], bf("S0h"))
                        P.op("act", lambda e: e.activation(out=S0b[:], in_=S0h[:], func=AF.Copy), [bf("S0h")], [bf("S0b")])
                        P.op("dve", lambda e, c=c: e.tensor_tensor(out=WTpad[:, :, :], in0=WTs[:, c:c + 1, 0:32].to_broadcast([128, NSQ, 32]),
                                                                   in1=SEQMB[:, :, :], op=ALU.mult),
                             [bf("WTs" + sfx_), bf("SEQMB")], [bf("WTpad")])
                        yield
                        P.op("dve", lambda e, cs=cs: e.tensor_tensor(out=QNpad[:, :, :], in0=QNb[par][:, cs].unsqueeze(1).to_broadcast([128, NSQ, 32]),
                                                                     in1=SEQMB[:, :, :], op=ALU.mult),
                             [bf("QNb%d" % par), bf("SEQMB")], [bf("QNpad")])
                        yield
                        P.op("dve", lambda e, c=c: e.tensor_tensor(out=KPpad[0:32, :, :], in0=KPs[0:32, c:c + 1, :].to_broadcast([32, NSQ, 128]),
                                                                   in1=SEQSEL[:, :].unsqueeze(2).to_broadcast([32, NSQ, 128]), op=ALU.mult),
                             [bf("KPs" + sfx_), bf("SEQSEL")], [bf("KPpad")])
                        yield
                        fns = [lambda e, s_=s_: e.matmul(B7[0:32, 0:128], lhsT=WTpad[:, s_, :], rhs=S0b[:, s_, :], start=(s_ == 0), stop=(s_ == NSQ - 1))
                               for s_ in range(NSQ)]
                        fns += [lambda e, s_=s_: e.matmul(B7[0:32, 128:256], lhsT=QNpad[:, s_, :], rhs=S0b[:, s_, :], start=(s_ == 0), stop=(s_ == NSQ - 1))
                                for s_ in range(NSQ)]
                        P.group("pe", fns, [bf("WTpad"), bf("QNpad"), bf("S0b")], [bk(7)])
                    yield
                    P.op("dve", lambda e, c=c, C=C: e.tensor_tensor(out=VNb[0:C, :], in0=Us[0:C, c, :], in1=B7[0:C, 0:128], op=ALU.subtract),
                         [bf("Us" + sfx_), bk(7)], [bf("VNb")])
                    yield
                    if not samp:
                        fns = [lambda e, c=c: e.matmul(B7[:, 256:384], lhsT=KPs[:, c, :], rhs=VNb[:, :], start=True, stop=True)]
                        if not so:
                            fns.append(lambda e, c=c: e.matmul(B7[:, 384:512], lhsT=QKTs[:, c, :], rhs=VNb[:, :], start=True, stop=True))
                        P.group("pe", fns, [bf("KPs" + sfx_), bf("QKTs" + sfx_), bf("VNb")], [bk(7)])
                        yield
                        P.op("dve", lambda e, c=c: e.scalar_tensor_tensor(out=S, in0=S, scalar=SMX[:, c, 2:3], in1=B7[:, 256:384],
                                                                          op0=ALU.mult, op1=ALU.add),
                             [bf("SST"), bf("SMX" + sfx_), bk(7)], [bf("SST")])
                        yield
                        P.op("act", lambda e: e.activation(out=Sb, in_=S, func=AF.Copy), [bf("SST")], [bf("Sb")])
                        yield
                        if so:
                            continue
                    else:
                        P.op("pe", lambda e, c=c: e.matmul(B7[0:32, 384:512], lhsT=QKTs[0:32, c, 0:32], rhs=VNb[0:32, :], start=True, stop=True),
                             [bf("QKTs" + sfx_), bf("VNb")], [bk(7)])
                        yield
                    P.op("act", lambda e, C=C: e.activation(out=T2[0:C, :], in_=B7[0:C, 384:512], func=AF.Copy), [bk(7)], [bf("T2")])
                    yield
                    if pend is not None:
                        yield from pend
                    P.op("dve", lambda e, c=c, C=C: e.scalar_tensor_tensor(out=Oo[0:C, :], in0=B7[0:C, 128:256], scalar=SMX[0:C, c, 0:1],
                                                                           in1=T2[0:C, :], op0=ALU.mult, op1=ALU.add),
                         [bk(7), bf("SMX" + sfx_), bf("T2")], [bf("O")])
                    yield
                    pend = outpath(c, C, cs)
                    if samp:
                        for s_ in range(NSQ):
                            P.op("pe", lambda e, s_=s_: e.matmul(B7[:, 256:384], lhsT=KPpad[0:32, s_, :], rhs=VNb[0:32, :], start=True, stop=True),
                                 [bf("KPpad"), bf("VNb")], [bk(7)])
                            P.op("dve", lambda e, s_=s_, sbi=sbi: e.scalar_tensor_tensor(out=S0h[:, s_, :], in0=S0h[:, s_, :],
                                                                                         scalar=EGLS[:, sbi * NSQ + s_:sbi * NSQ + s_ + 1],
                                                                                         in1=B7[:, 256:384], op0=ALU.mult, op1=ALU.add),
                                 [bf("S0h"), bf("EGLS" + sfx_), bk(7)], [bf("S0h")])
                            yield
                        P.dma("sp", o_dl_s[sbi * NSQ:(sbi + 1) * NSQ, h].rearrange("s p n -> p s n"), S0h[:], [bf("S0h")], bf("ODLS"))
                if pend is not None:
                    yield from pend

            def poolB(h):
                yield from pool_gen([(lambda L, c=c: prep(h, c, L)) for c in (list(range(NCH, nck)) + list(range(NCH)))], 4, _TUNE[3])

            for r in range(10):
                gl = []
                if r - 2 >= 0:
                    gl.append((scan(r - 2), _TUNE[0]))
                if 0 <= r - 1 < 8:
                    gl.append((poolB(r - 1), _TUNE[1]))
                if r < 8:
                    gl.append((stageP(r), _TUNE[2]))
                run_weighted(gl)

        def checkpoint(label, src3=None, nk=KC, parts=128):
            if stop != label:
                return
            P.barrier()
            src3 = RA_x if src3 is None else src3
            for k in range(nk):
                P.dma("sp", dbg[k * 128:k * 128 + parts, :], src3[0:parts, k, :], [], bf("DBG"))
            P.barrier()
            raise _Stop()

        def checkpoint2(label, aps):
            if stop != label:
                return
            P.barrier()
            for i, ap in enumerate(aps):
                n = ap.shape[-1]
                if ap.dtype != F32:
                    raise ValueError("fp32 only")
                P.dma("sp", dbg[i * 128:i * 128 + ap.shape[0], 0:n], ap, [], bf("DBG"))
            P.barrier()
            raise _Stop()

        try:
          for g in range(NG):
              for q4 in range(4):
                  P.dma("sp", RA_x[:, q4 * 4:(q4 + 1) * 4, :],
                        xT[g, q4 * 512:(q4 + 1) * 512, :].rearrange("(k p) n -> p k n", p=128), [], bf("SRC"))
              checkpoint("load")
              prenorm(0)
              P.barrier()
              checkpoint("hb0", RB_h.bitcast(F32) if False else None)
              ffn(wgu1, wd1)
              checkpoint("y1", RB_y)
              post_residual(RB_y, 1, 0.5, xT[g], g, next_stats=True)
              P.barrier()
              checkpoint("x1")
              prenorm(2, have_stats=True)
              P.barrier()
              so = g == 0
              mixer(g, so)
              P.barrier()
              if so:
                  for ap, nm_ in ((SST[:], "SST"), (HISTA[:], "HISTA"), (HISTQ[:], "HISTQ")):
                      P.op("dve", lambda e, ap=ap: e.tensor_scalar(out=ap, in0=ap, scalar1=CARRY[:, 0:1], scalar2=None, op0=ALU.mult),
                           [bf(nm_), bf("CARRY")], [bf(nm_)])
                  P.barrier()
                  continue
              checkpoint("ymix", RB_y[:, 8:16, :], nk=8)
              t3 = WTiles([wout[m].rearrange("(k p) n -> p k n", p=128) for m in range(KC)], 2048, 4,
                          lambda ap: ap.rearrange("p (k n) -> p k n", k=KC))
              for m in range(KC):
                  w, wbuf = t3.get(m)
                  pa = m % 2
                  fns = [lambda e, t=t, k=k, w=w, pa=pa: e.matmul(PTt[pa][:, t, 0:TGW], lhsT=w[:, k, :],
                                                                 rhs=RB_m[:, k, t * TGW:(t + 1) * TGW],
                                                                 start=(k == 0), stop=(k == KC - 1))
                         for t in range(3) for k in range(KC)]
                  P.group("pe", fns, [wbuf, bf("YMIX")], [*ptb(pa)])
                  t3.done(m)
                  P.op("act", lambda e, m=m, pa=pa: e.activation(out=v3(RA_x[:, m, :]), in_=pt3(pa), func=AF.Copy),
                       [*ptb(pa)], [bf("SRC")])
              P.barrier()
              post_residual(RA_x, 3, 1.0, xs_d, g, next_stats=True)
              P.barrier()
              checkpoint("x2")
              prenorm(4, have_stats=True)
              P.barrier()
              ffn(wgu2, wd2)
              post_residual(RB_y, 5, 0.5, xs_d, g, next_stats=True)
              P.barrier()
              checkpoint("x3")
              prenorm(6, have_stats=True)
              P.barrier()
              PB = RB_m[:, 0:2, :]
              P.dma("pool", PB, pT.rearrange("(k p) n -> p k n", p=128), [], bf("PB"))
              t4 = WTiles([wpg[m].rearrange("(k p) n -> p k n", p=128) for m in range(KC)], 2048, 2,
                          lambda ap: ap.rearrange("p (k n) -> p k n", k=KC))
              WPP = WRb[:, 4096:4096 + KC * 256].rearrange("p (m k n) -> p m k n", m=KC, k=2)
              P.dma("pool", WPP, wpp.rearrange("m (k p) n -> p m k n", p=128), [], bf("WPP"))
              for m in range(KC):
                  w, wbuf = t4.get(m)
                  proj_tile(w, wbuf, 0)
                  t4.done(m)
                  fns = [lambda e, t=t, k=k, m=m: e.matmul(PTt[1][:, t, 0:TGW], lhsT=WPP[:, m, k, :],
                                                          rhs=PB[:, k, t * TGW:(t + 1) * TGW], start=(k == 0), stop=(k == 1))
                         for t in range(3) for k in range(2)]
                  P.group("pe", fns, [bf("WPP"), bf("PB")], [*ptb(1)])
                  P.op("act", lambda e: e.activation(out=v3(TMP1[:]), in_=pt3(0), func=AF.Sigmoid), [*ptb(0)], [bf("TMP1")])
                  P.op("dve", lambda e, m=m: e.tensor_tensor(out=v3(RA_x[:, m, :]), in0=v3(TMP1[:]), in1=pt3(1), op=ALU.mult),
                       [bf("TMP1"), *ptb(1)], [bf("SRC")])
              P.barrier()
              post_residual(RA_x, 7, 1.0, xs_d, g, last=True)
              P.barrier()

        except _Stop:
            pass
        P.dma("sp", o_ca_p.rearrange("c p t -> p c t"), HISTA[:], [bf("HISTA")], bf("O1"))
        P.dma("sp", o_cq_p.rearrange("c p t -> p c t"), HISTQ[:], [bf("HISTQ")], bf("O2"))
        P.dma("sp", o_dl_p.rearrange("h p n -> p h n"), SST[:], [bf("SST")], bf("O3"))
        P.barrier()
        P.wait_all("sp", [bf("O1"), bf("O2"), bf("O3"), bf("DBG"), bf("OCAS"), bf("OCQS"), bf("ODLS")] + [bf("YOUT%d" % m) for m in range(4)])
        P.emit()
    return nc


_CACHE = {}
_TUNE = [2, 2, 1, 6]
_DEBUG = {}


def _masks():
    def mk(C, blk):
        idx = np.arange(C)
        same = (idx[:, None] // blk) == (idx[None, :] // blk)
        tri = ((idx[:, None] <= idx[None, :]) & same).astype(np.float32)
        sfx = ((idx[:, None] > idx[None, :]) & same).astype(np.float32)
        negi = np.where((idx[None, :] < idx[:, None]) & same, 0.0, NEG).astype(np.float32)
        negt = np.where((idx[:, None] <= idx[None, :]) & same, 0.0, NEG).astype(np.float32)
        strict = ((idx[None, :] < idx[:, None]) & same).astype(np.float32)
        return tri, sfx, negi, negt, strict
    c = list(mk(128, 128)) + [np.eye(128, dtype=np.float32)]
    sm = list(mk(32, 4))
    seqsel = (np.arange(32)[:, None] // 4 == np.arange(NSQ)[None, :]).astype(np.float32)
    seqmb = np.broadcast_to(seqsel.T[None], (128, NSQ, 32)).astype(np.float32).copy()
    return np.stack(c), np.stack(sm), seqsel, seqmb


def kernel(x_prompt, x_sample, state_conv_a, state_conv_qkv, state_delta, p_prompt, p_sample,
           f1_pre, f1_post, f1_wg, f1_wu, f1_wd,
           mix_pre, mix_post, w_in, conv_a_w, conv_qkv_w, a_log, dt_bias, dn_norm, w_out,
           f2_pre, f2_post, f2_wg, f2_wu, f2_wd,
           ple_pre, ple_post, w_ple_gate, w_ple_proj):
    f = lambda a: np.ascontiguousarray(np.asarray(a, dtype=np.float32))
    x_prompt, x_sample, p_prompt, p_sample = f(x_prompt), f(x_sample), f(p_prompt)[0], f(p_sample)[0]
    sca, scq, sdl = f(state_conv_a)[0], f(state_conv_qkv)[0], f(state_delta)[0]

    def pack_gu(wg, wu):
        wg, wu = f(wg)[0], f(wu)[0]
        return f(np.concatenate([wg.reshape(D, JC, 128), wu.reshape(D, JC, 128)], axis=2).transpose(1, 0, 2))

    def pack_d(wd):
        wd = f(wd)[0]
        t = wd.reshape(4, 11, 128, KC, 128)
        return f(t.transpose(3, 0, 2, 1, 4).reshape(KC, 4, 128, 11 * 128))

    def coltiles(w, n=128):
        w = f(w)
        return f(w.reshape(w.shape[0], w.shape[1] // n, n).transpose(1, 0, 2))

    wi = f(w_in)[0]
    Bc, Cc, Hc, Qc, Kc, Vc, Zc = [coltiles(wi[:, o:o + 1024]) for o in range(0, 7168, 1024)]
    tiles = []
    for i in range(8):
        tiles += [Cc[i], Hc[i], Bc[i]]
    for h in range(8):
        tiles += [Qc[h], Kc[h], Vc[h], Zc[h]]
    win = f(np.stack(tiles))
    cm, sm, seqsel, seqmb = _masks()
    vecs = [f1_pre, f1_post, mix_pre, mix_post, f2_pre, f2_post, ple_pre, ple_post]
    gains = f(np.stack([f(v)[0].reshape(KC, 128).T for v in vecs], axis=1))
    shared = dict(
        wgu1=pack_gu(f1_wg, f1_wu), wd1=pack_d(f1_wd), wgu2=pack_gu(f2_wg, f2_wu), wd2=pack_d(f2_wd),
        win=win, wa=f(wi[:, 7168:7176]), wb=f(wi[:, 7176:7184]),
        wout=coltiles(f(w_out)[0]), wpg=coltiles(f(w_ple_gate)[0]), wpp=coltiles(f(w_ple_proj)[0]),
        gains=gains,
        cw_a=f(f(conv_a_w)[0].reshape(3, 8, 128).transpose(2, 1, 0)),
        cw_q=f(f(conv_qkv_w)[0].reshape(4, 24, 128).transpose(2, 1, 0)),
        alog=f(a_log)[0], dtb=f(dt_bias)[0], dnn=f(f(dn_norm)[0].reshape(128, 1)),
        cmask=cm, smask=sm, seqsel=seqsel, seqmb=seqmb,
    )
    in_maps = []
    for c in range(8):
        seq, half = c % 4, c // 4
        sq = slice(c * NSEQ, (c + 1) * NSEQ)
        z65 = np.zeros((NS + 1, D), np.float32)
        x0 = x_prompt[seq, 0:NP] if half == 1 else np.zeros((NP, D), np.float32)
        x1 = x_prompt[seq, half * NP:(half + 1) * NP]
        xs = [np.concatenate([x0, z65], axis=0).T,
              np.concatenate([x1, x_sample[sq].reshape(NS, D), np.zeros((1, D), np.float32)], axis=0).T]
        pp = np.concatenate([p_prompt[seq, half * NP:(half + 1) * NP], p_sample[sq].reshape(NS, 256),
                             np.zeros((1, 256), np.float32)], axis=0).T
        m = dict(shared)
        m.update(xT=f(np.stack(xs)), pT=f(pp),
                 hist_a=f(sca[sq].reshape(NSEQ, 2, 8, 128).transpose(2, 3, 0, 1)),
                 hist_q=f(scq[sq].reshape(NSEQ, 3, 24, 128).transpose(2, 3, 0, 1)),
                 s0=f(sdl[sq]), carry=np.full((128, 1), float(half), np.float32))
        in_maps.append(m)

    if _DEBUG.get("return_maps"):
        return in_maps
    if "nc" not in _CACHE:
        _CACHE["nc"] = build_program()
    res = run_bass_kernel_spmd(_CACHE["nc"], in_maps, core_ids=list(range(8)))
    R = res.results

    y_prompt = np.zeros((4, NG * NP, D), np.float32)
    y_sample = np.zeros((128, 4, D), np.float32)
    ca_p = np.zeros((1, 4, 2, 1024), np.float32)
    cq_p = np.zeros((1, 4, 3, 3072), np.float32)
    dl_p = np.zeros((1, 4, 8, 128, 128), np.float32)
    ca_s = np.zeros((1, 128, 2, 1024), np.float32)
    cq_s = np.zeros((1, 128, 3, 3072), np.float32)
    dl_s = np.zeros((1, 128, 8, 128, 128), np.float32)
    for c in range(8):
        r = R[c]
        seq, half = c % 4, c // 4
        sq = slice(c * NSEQ, (c + 1) * NSEQ)
        yt = np.asarray(r["yT"])
        y_prompt[seq, half * NP:(half + 1) * NP] = yt[:, 0:NP].T
        y_sample[sq] = yt[:, NP:NP + NS].T.reshape(NSEQ, 4, D)
        ca_s[0, sq] = np.asarray(r["o_ca_s"]).transpose(2, 3, 0, 1).reshape(NSEQ, 2, 1024)
        cq_s[0, sq] = np.asarray(r["o_cq_s"]).transpose(2, 3, 0, 1).reshape(NSEQ, 3, 3072)
        dl_s[0, sq] = np.asarray(r["o_dl_s"])
        if half == 1:
            ca_p[0, seq] = np.asarray(r["o_ca_p"]).transpose(2, 0, 1).reshape(2, 1024)
            cq_p[0, seq] = np.asarray(r["o_cq_p"]).transpose(2, 0, 1).reshape(3, 3072)
            dl_p[0, seq] = np.asarray(r["o_dl_p"])
    return (y_prompt, y_sample, ca_p, cq_p, dl_p, ca_s, cq_s, dl_s)
```

```python
import numpy as np
from contextlib import ExitStack
import concourse.bass as bass
import concourse.mybir as mybir
from concourse.bass_utils import run_bass_kernel_spmd

F32 = mybir.dt.float32
BF16 = mybir.dt.bfloat16
ALU = mybir.AluOpType
AF = mybir.ActivationFunctionType

D = 2048
KC = 16
DFF = 5632
JC = 44
NG = 2
NP = 1024
NSQ = 8
NSB = 2
NSEQ = NSQ * NSB
NS = NSEQ * 4
NT = NP + NS + 1
TGW = NT // 3
NCH = NP // 128
NCK = NCH + NSB
EPS = 1e-6
NEG = -1.0e5


class Buf:
    __slots__ = ("last_w", "reads", "sem", "semval", "excl")

    def __init__(self, excl=False):
        self.excl = excl
        self.last_w = None
        self.reads = []
        self.sem = None
        self.semval = 0


class Prog:
    ENGS = ("pe", "act", "dve", "pool", "sp")

    def __init__(self, nc):
        self.nc = nc
        self.ops = {e: [] for e in self.ENGS}
        self.sems = {}
        self.cnt = {e: 0 for e in self.ENGS}
        self.seen = {e: {} for e in self.ENGS}
        for e in self.ENGS:
            self.sems[e] = nc.alloc_semaphore("s_" + e)
        self.ndsem = 0
        self.dma_events = {}

    def _deps(self, eng, reads, writes):
        need = {}
        for b in reads:
            if b.last_w is not None:
                k, v = b.last_w
                if need.get(k, 0) < v:
                    need[k] = v
        for b in writes:
            if b.last_w is not None:
                k, v = b.last_w
                if need.get(k, 0) < v:
                    need[k] = v
            for (k, v) in b.reads:
                if need.get(k, 0) < v:
                    need[k] = v
        return self._prune(eng, need)

    def _prune(self, eng, need):
        waits = []
        seen = self.seen[eng]
        for k, v in need.items():
            if eng == "pe" and k == "pe":
                continue
            if seen.get(k, 0) < v:
                seen[k] = v
                waits.append((k, v))
        return waits

    def _mark(self, ev, reads, writes):
        for b in reads:
            b.reads.append(ev)
        for b in writes:
            b.last_w = ev
            b.reads = []

    @staticmethod
    def _split(reads, writes):
        ex = [b for b in reads if b.excl]
        if ex:
            return [b for b in reads if not b.excl], list(writes) + ex
        return reads, writes

    def op(self, eng, fn, reads=(), writes=()):
        reads, writes = self._split(reads, writes)
        waits = self._deps(eng, reads, writes)
        self.cnt[eng] += 1
        ev = (eng, self.cnt[eng])
        self._mark(ev, reads, writes)
        self.ops[eng].append((waits, fn, ev))
        return ev

    def group(self, eng, fns, reads=(), writes=()):
        n = len(fns)
        if n == 1:
            return self.op(eng, fns[0], reads, writes)
        reads, writes = self._split(reads, writes)
        waits = self._deps(eng, reads, writes)
        self.ops[eng].append((waits, fns[0], None))
        for fn in fns[1:-1]:
            self.ops[eng].append(((), fn, None))
        self.cnt[eng] += 1
        ev = (eng, self.cnt[eng])
        self._mark(ev, reads, writes)
        self.ops[eng].append(((), fns[-1], ev))
        return ev

    def dma(self, eng, out_ap, in_ap, reads, wbuf):
        writes = (wbuf,)
        waits = self._deps(eng, reads, writes)
        if wbuf.sem is None:
            key = "d%d" % self.ndsem
            self.ndsem += 1
            self.sems[key] = self.nc.alloc_semaphore(key)
            wbuf.sem = key
        wbuf.semval += 16
        ev = (wbuf.sem, wbuf.semval)
        self._mark(ev, reads, writes)
        if eng == "sp":
            self.dma_events[wbuf.sem] = wbuf.semval

        def fn(e, out_ap=out_ap, in_ap=in_ap):
            return e.dma_start(out=out_ap, in_=in_ap)
        self.ops[eng].append((waits, fn, ev))
        return ev

    def barrier(self):
        need = {e: self.cnt[e] for e in ("pe", "act", "dve") if self.cnt[e] > 0}
        need.update(self.dma_events)
        for e in ("pe", "act", "dve", "sp", "pool"):
            n2 = dict(need)
            if e == "pe":
                n2.pop("pe", None)
            waits = self._prune(e, n2)
            if waits:
                self.ops[e].append((waits, None, None))

    def wait_all(self, eng, bufs):
        waits = self._deps(eng, bufs, bufs)
        self.ops[eng].append((waits, None, None))

    def emit(self):
        nc = self.nc
        sems = self.sems
        ops = self.ops

        def run(e, name):
            for waits, fn, ev in ops[name]:
                for (k, v) in waits:
                    e.wait_ge(sems[k], v)
                if fn is None:
                    continue
                ins = fn(e)
                if ev is not None:
                    k, v = ev
                    ins.then_inc(sems[k], 1 if k == name else 16)

        with nc.Block() as block:
            @block.tensor
            def _(e):
                run(e, "pe")

            @block.scalar
            def _(e):
                run(e, "act")

            @block.vector
            def _(e):
                run(e, "dve")

            @block.gpsimd
            def _(e):
                run(e, "pool")

            @block.sync
            def _(e):
                run(e, "sp")


class _Stop(Exception):
    pass


def build_program(stop=None):
    nc = bass.Bass("TRN2", target_bir_lowering=False)
    dt_in = lambda name, shape: nc.dram_tensor(name, list(shape), F32, kind="ExternalInput").ap()
    dt_out = lambda name, shape: nc.dram_tensor(name, list(shape), F32, kind="ExternalOutput").ap()

    xT = dt_in("xT", (NG, D, NT))
    pT = dt_in("pT", (256, NT))
    hist_a = dt_in("hist_a", (8, 128, NSEQ, 2))
    hist_q = dt_in("hist_q", (24, 128, NSEQ, 3))
    s0 = dt_in("s0", (NSEQ, 8, 128, 128))
    carry = dt_in("carry", (128, 1))
    wgu1 = dt_in("wgu1", (JC, D, 256))
    wd1 = dt_in("wd1", (KC, 4, 128, 11 * 128))
    wgu2 = dt_in("wgu2", (JC, D, 256))
    wd2 = dt_in("wd2", (KC, 4, 128, 11 * 128))
    win = dt_in("win", (56, D, 128))
    wa = dt_in("wa", (D, 8))
    wb = dt_in("wb", (D, 8))
    wout = dt_in("wout", (KC, D, 128))
    wpg = dt_in("wpg", (KC, D, 128))
    wpp = dt_in("wpp", (KC, 256, 128))
    gains = dt_in("gains", (128, 8, KC))
    cw_a = dt_in("cw_a", (128, 8, 3))
    cw_q = dt_in("cw_q", (128, 24, 4))
    alog = dt_in("alog", (8,))
    dtb = dt_in("dtb", (8,))
    dnn = dt_in("dnn", (128, 1))
    cmask = dt_in("cmask", (6, 128, 128))
    smask = dt_in("smask", (5, 32, 32))
    seqsel = dt_in("seqsel", (32, NSQ))
    seqmb = dt_in("seqmb", (128, NSQ, 32))

    yT = dt_out("yT", (D, NT))
    o_ca_p = dt_out("o_ca_p", (8, 128, 2))
    o_cq_p = dt_out("o_cq_p", (24, 128, 3))
    o_dl_p = dt_out("o_dl_p", (8, 128, 128))
    o_ca_s = dt_out("o_ca_s", (8, 128, NSEQ, 2))
    o_cq_s = dt_out("o_cq_s", (24, 128, NSEQ, 3))
    o_dl_s = dt_out("o_dl_s", (NSEQ, 8, 128, 128))
    xs_d = nc.dram_tensor("xs_scratch", [D, NT], F32).ap()
    dbg = dt_out("dbg", (D, NT)) if stop is not None else None

    P = Prog(nc)
    es = ExitStack()
    with es:
        sb = lambda n, s, d=F32: es.enter_context(nc.sbuf_tensor(n, list(s), d))
        RA = sb("RA", (128, JC * NT // 2))
        RB = sb("RB", (128, KC * NT))
        WR = sb("WR", (128, 4224), F32)
        RA_x = RA[:, 0:KC * NT].rearrange("p (k n) -> p k n", k=KC)
        RA_act = RA[:].bitcast(BF16).rearrange("p (j n) -> p j n", j=JC)
        RB_y = RB[:].rearrange("p (k n) -> p k n", k=KC)
        RB_bf = RB[:].bitcast(BF16)
        RB_h = RB_bf[:, 0:KC * NT].rearrange("p (k n) -> p k n", k=KC)
        RB_m = RB_bf[:, KC * NT:2 * KC * NT].rearrange("p (k n) -> p k n", k=KC)
        WRb = WR[:].bitcast(BF16)
        RSTD = sb("RSTD", (128, NT))
        TMP1 = sb("TMP1", (128, NT))
        SQ = [sb("SQ%d" % i, (128, NT), BF16) for i in range(2)]
        CARRY = sb("CARRY", (128, 1))
        GAINS = sb("GAINS", (128, 8, KC))
        CWA = sb("CWA", (128, 8, 3))
        CWQ = sb("CWQ", (128, 24, 4))
        ONESB = sb("ONESB", (128, 128), BF16)
        ONES1 = sb("ONES1", (128, 128), BF16)
        ONESF = sb("ONESF", (128, 128))
        NONESF = sb("NONESF", (128, 128))
        IDB = sb("IDB", (128, 128), BF16)
        CM = sb("CM", (128, 6, 128))
        SM = sb("SM", (32, 5, 32))
        SEQSEL = sb("SEQSEL", (32, NSQ))
        SEQMB = sb("SEQMB", (128, NSQ, 32))
        DNN = sb("DNN", (128, 1))
        ALOG = sb("ALOG", (8, 1))
        DTB = sb("DTB", (8, 1))
        NEXPA = sb("NEXPA", (8, 1))
        HISTA = sb("HISTA", (128, 8, 2))
        HISTQ = sb("HISTQ", (128, 24, 3))
        SST = sb("SST", (128, 8, 128))
        ALLPS = es.enter_context(nc.psum_tensor("ALLPS", [128, 8, 512], F32))
        PTt = [ALLPS[:, 0:3, :], ALLPS[:, 3:6, :]]
        PM = [ALLPS[:, 6, :], ALLPS[:, 7, :]]

        B = {}

        def bf(name):
            if name not in B:
                B[name] = Buf()
            return B[name]

        def bk(i):
            name = "BK%d" % i
            if name not in B:
                B[name] = Buf(excl=True)
            return B[name]

        def ptb(a):
            return [bk(3 * a), bk(3 * a + 1), bk(3 * a + 2)]

        def pt3(a, m=128):
            return PTt[a][0:m, :, 0:TGW]

        def v3(ap2d):
            return ap2d.rearrange("p (t n) -> p t n", t=3)

        P.dma("sp", GAINS[:], gains, [], bf("GAINS"))
        P.dma("sp", CWA[:], cw_a, [], bf("CWA"))
        P.dma("sp", CWQ[:], cw_q, [], bf("CWQ"))
        P.dma("sp", CM[:], cmask.rearrange("m p n -> p m n"), [], bf("CM"))
        P.dma("sp", SM[:], smask.rearrange("m p n -> p m n"), [], bf("SM"))
        P.dma("sp", SEQSEL[:], seqsel, [], bf("SEQSEL"))
        P.dma("sp", SEQMB[:], seqmb, [], bf("SEQMB"))
        P.dma("sp", DNN[:], dnn, [], bf("DNN"))
        P.dma("sp", CARRY[:], carry, [], bf("CARRY"))
        P.dma("sp", ALOG[:], alog.rearrange("(h o) -> h o", o=1), [], bf("ALOG"))
        P.dma("sp", DTB[:], dtb.rearrange("(h o) -> h o", o=1), [], bf("DTB"))
        P.op("dve", lambda e: e.memset(ONESB[:], 1.0 / D), [], [bf("ONESB")])
        P.op("dve", lambda e: e.memset(ONES1[:], 1.0), [], [bf("ONES1")])
        P.op("dve", lambda e: e.memset(ONESF[:], 1.0), [], [bf("ONESF")])
        P.op("dve", lambda e: e.memset(NONESF[:], -1.0), [], [bf("NONESF")])
        P.op("dve", lambda e: e.tensor_copy(out=IDB[:], in_=CM[:, 5, :]), [bf("CM")], [bf("IDB")])
        P.op("dve", lambda e: e.memset(HISTA[:], 0.0), [], [bf("HISTA")])
        P.op("dve", lambda e: e.memset(HISTQ[:], 0.0), [], [bf("HISTQ")])
        P.op("dve", lambda e: e.memset(SST[:], 0.0), [], [bf("SST")])
        P.op("act", lambda e: e.activation(out=NEXPA[:], in_=ALOG[:], func=AF.Exp), [bf("ALOG")], [bf("NEXPA")])
        P.op("dve", lambda e: e.tensor_scalar(out=NEXPA[:], in0=NEXPA[:], scalar1=-1.0, scalar2=None, op0=ALU.mult),
             [bf("NEXPA")], [bf("NEXPA")])
        TRI, SFX, NEGI, NEGT, STRICT, IDENT = [CM[:, i, :] for i in range(6)]

        wstate = {"n": 0}

        class WTiles:
            def __init__(self, srcs, slot_elems, nslots, view):
                self.srcs = srcs
                self.view = view
                self.nslots = nslots
                self.slot_elems = slot_elems
                self.issued = 0
                wstate["n"] += 1
                self.tag = "W%d_" % wstate["n"]
                self.prefetch(nslots)

            def slot(self, i):
                s = i % self.nslots
                return self.view(WRb[:, s * self.slot_elems:(s + 1) * self.slot_elems]), bf(self.tag + str(s))

            def prefetch(self, upto):
                while self.issued < min(upto, len(self.srcs)):
                    ap, b = self.slot(self.issued)
                    P.dma("pool", ap, self.srcs[self.issued], [], b)
                    self.issued += 1

            def get(self, i):
                assert self.issued > i
                return self.slot(i)

            def done(self, i):
                self.prefetch(i + self.nslots + 1)

        def stats(src3, nk, ones, pa):
            for k in range(nk):
                q = SQ[k % 2]
                P.op("act", lambda e, k=k, q=q: e.activation(out=q[:], in_=src3[:, k, :], func=AF.Square),
                     [bf("SRC")], [bf("SQ%d" % (k % 2))])
                fns = [lambda e, t=t, k=k, q=q: e.matmul(PTt[pa][:, t, 0:TGW], lhsT=ones[:], rhs=q[:, t * TGW:(t + 1) * TGW],
                                                         start=(k == 0), stop=(k == nk - 1)) for t in range(3)]
                P.group("pe", fns, [bf("SQ%d" % (k % 2)), bf("ONES")], [*ptb(pa)])

        def rstd_from(pa, scale=1.0):
            P.op("act", lambda e: e.activation(out=v3(TMP1[:]), in_=pt3(pa), func=AF.Sqrt, bias=EPSB[:, 0:1], scale=scale),
                 [*ptb(pa), bf("EPSB")], [bf("TMP1")])
            P.op("dve", lambda e: e.reciprocal(out=RSTD[:], in_=TMP1[:]), [bf("TMP1")], [bf("RSTD")])

        EPSB = sb("EPSB", (128, 1))
        P.op("dve", lambda e: e.memset(EPSB[:], EPS), [], [bf("EPSB")])
        P.barrier()

        def prenorm(gi, have_stats=False):
            if have_stats:
                rstd_from(1)
            else:
                stats(RA_x, KC, ONESB, 0)
                rstd_from(0)
            for k in range(KC):
                P.op("dve", lambda e, k=k: e.scalar_tensor_tensor(out=RB_h[:, k, :], in0=RA_x[:, k, :],
                                                                  scalar=GAINS[:, gi, k:k + 1], in1=RSTD[:],
                                                                  op0=ALU.mult, op1=ALU.mult),
                     [bf("SRC"), bf("RSTD"), bf("GAINS")], [bf("HB")])

        def post_residual(y3, gi, coef, res_src, g, last=False, next_stats=False):
            stats(y3, KC, ONESB, 0)
            rstd_from(0)
            NXS = 5
            XS2 = [RA[:, (KC + i) * NT:(KC + i + 1) * NT] for i in range(NXS)]

            def load(m):
                P.dma("sp", XS2[m % NXS], res_src[m * 128:(m + 1) * 128, :], [bf("XS%d" % m)], bf("XST%d" % (m % NXS)))
            for m in range(NXS - 1):
                load(m)
            for m in range(KC):
                xs = XS2[m % NXS]
                if m + NXS - 1 < KC:
                    load(m + NXS - 1)
                P.op("dve", lambda e, m=m: e.scalar_tensor_tensor(out=y3[:, m, :], in0=y3[:, m, :],
                                                                  scalar=GAINS[:, gi, m:m + 1], in1=RSTD[:],
                                                                  op0=ALU.mult, op1=ALU.mult),
                     [bf("SRC"), bf("RSTD"), bf("GAINS")], [bf("SRC")])
                P.op("dve", lambda e, m=m, xs=xs: e.scalar_tensor_tensor(out=RA_x[:, m, :], in0=y3[:, m, :], scalar=coef,
                                                                         in1=xs, op0=ALU.mult, op1=ALU.add),
                     [bf("SRC"), bf("XST%d" % (m % NXS))], [bf("XN%d" % m)])
                if next_stats:
                    q = SQ[m % 2]
                    P.op("act", lambda e, m=m, q=q: e.activation(out=q[:], in_=RA_x[:, m, :], func=AF.Square),
                         [bf("XN%d" % m)], [bf("SQ%d" % (m % 2))])
                    fns = [lambda e, t=t, m=m, q=q: e.matmul(PTt[1][:, t, 0:TGW], lhsT=ONESB[:], rhs=q[:, t * TGW:(t + 1) * TGW],
                                                             start=(m == 0), stop=(m == KC - 1)) for t in range(3)]
                    P.group("pe", fns, [bf("SQ%d" % (m % 2))], [*ptb(1)])
                dst = yT[m * 128:(m + 1) * 128, :] if last else xs_d[m * 128:(m + 1) * 128, :]
                P.dma("sp", dst, RA_x[:, m, :], [bf("XN%d" % m)], bf("YOUT%d" % (m % 4)) if last else bf("XS%d" % m))

        def ffn_tiles(wgu):
            return WTiles([wgu[j].rearrange("(k p) n -> p k n", p=128) for j in range(JC)], 4096, 2,
                          lambda ap: ap.rearrange("p (k n) -> p k n", k=KC))

        def ffn(wgu, wd, t1):
            for j in range(JC):
                w, wbuf = t1.get(j)
                for which in range(2):
                    fns = [lambda e, t=t, k=k, w=w, which=which: e.matmul(
                        PTt[which][:, t, 0:TGW], lhsT=w[:, k, which * 128:(which + 1) * 128],
                        rhs=RB_h[:, k, t * TGW:(t + 1) * TGW], start=(k == 0), stop=(k == KC - 1))
                        for t in range(3) for k in range(KC)]
                    P.group("pe", fns, [wbuf, bf("HB")], [*ptb(which)])
                t1.done(j)
                P.op("act", lambda e: e.activation(out=v3(TMP1[:]), in_=pt3(0), func=AF.Silu), [*ptb(0)], [bf("TMP1")])
                P.op("dve", lambda e, j=j: e.tensor_tensor(out=v3(RA_act[:, j, :]), in0=v3(TMP1[:]), in1=pt3(1), op=ALU.mult),
                     [bf("TMP1"), *ptb(1)], [bf("ACT")])
            P.barrier()
            t2 = WTiles([wd[m, hf] for m in range(KC) for hf in range(4)], 1408, 6,
                        lambda ap: ap.rearrange("p (j n) -> p j n", j=11))
            for m in range(KC):
                wl = [t2.get(4 * m + hf) for hf in range(4)]
                pa = m % 2
                for hf in range(4):
                    fns = [lambda e, t=t, hf=hf, jj=jj, pa=pa, w=wl[hf][0]: e.matmul(
                        PTt[pa][:, t, 0:TGW], lhsT=w[:, jj, :], rhs=RA_act[:, hf * 11 + jj, t * TGW:(t + 1) * TGW],
                        start=(hf == 0 and jj == 0), stop=(hf == 3 and jj == 10))
                        for t in range(3) for jj in range(11)]
                    P.group("pe", fns, [wl[hf][1], bf("ACT")], [*ptb(pa)])
                    t2.done(4 * m + hf)
                P.op("act", lambda e, m=m, pa=pa: e.activation(out=v3(RB_y[:, m, :]), in_=pt3(pa), func=AF.Copy),
                     [*ptb(pa)], [bf("SRC")])
            P.barrier()

        def proj_tile(w, wbuf, pa, m=128):
            fns = [lambda e, t=t, k=k: e.matmul(PTt[pa][0:m, t, 0:TGW], lhsT=w[:, k, 0:m],
                                                rhs=RB_h[:, k, t * TGW:(t + 1) * TGW], start=(k == 0), stop=(k == KC - 1))
                   for t in range(3) for k in range(KC)]
            P.group("pe", fns, [wbuf, bf("HB")], [*ptb(pa)])


        WAs = sb("WAs", (128, KC, 8), BF16)
        WBs = sb("WBs", (128, KC, 8), BF16)
        SSQ = sb("SSQ", (128, 2))
        ZS2 = sb("ZS2", (128, NT))

        def PMq(a, q):
            return ALLPS[:, a * 4 + q, 0:128]

        def pmb(a, q):
            return bk(a * 4 + q)

        def run_lockstep(gens):
            gens = list(gens)
            while gens:
                for g_ in list(gens):
                    try:
                        next(g_)
                    except StopIteration:
                        gens.remove(g_)

        def run_pool(tasks, nlanes, stagger, extra=()):
            tasks = list(tasks)
            active = {}
            extra = list(extra)
            rnd = 0
            started = 0
            while tasks or active or extra:
                for L in range(nlanes):
                    if L not in active and tasks and rnd >= started * stagger:
                        active[L] = tasks.pop(0)(L)
                        started += 1
                for L in list(active):
                    try:
                        next(active[L])
                    except StopIteration:
                        del active[L]
                for g_ in list(extra):
                    try:
                        next(g_)
                    except StopIteration:
                        extra.remove(g_)
                rnd += 1

        def pool_gen(tasks, nlanes, stagger):
            tasks = list(tasks)
            active = {}
            rnd = 0
            started = 0
            while tasks or active:
                for L in range(nlanes):
                    if L not in active and tasks and rnd >= started * stagger:
                        active[L] = tasks.pop(0)(L)
                        started += 1
                for L in list(active):
                    try:
                        next(active[L])
                    except StopIteration:
                        del active[L]
                rnd += 1
                yield

        def run_weighted(gens_w):
            gens_w = [[g_, w_] for g_, w_ in gens_w]
            while gens_w:
                for item in list(gens_w):
                    for _ in range(item[1]):
                        try:
                            next(item[0])
                        except StopIteration:
                            gens_w.remove(item)
                            break

        def BKf(i):
            return ALLPS[:, i, :]

        def BKb(i):
            return ALLPS[:, i, :].bitcast(BF16)

        def mixer(g, so):
            off = [0]

            def ra(n, name=None):
                a = RA[:, off[0]:off[0] + n]
                off[0] += n
                assert off[0] <= JC * NT // 2, off[0]
                return a

            def rab(n, name=None):
                return ra((n + 1) // 2).bitcast(BF16)[:, 0:n]
            CIP = ra(1032)
            CIS = ra(NSEQ * 7).rearrange("p (s t) -> p s t", s=NSEQ)
            FL = ra(NT)
            T1 = ra(NT)
            QNb = [rab(NT), rab(NT), rab(NT)]
            KNb = [rab(NT), rab(NT)]; VSb = rab(NT)
            ZS = [ra(NT), ra(NT), ZS2[:]]
            GFM = FL[0:8, :]; BFM = T1[0:8, :]
            GTOK = ra(NCK * 8).rearrange("p (c h) -> p c h", c=NCK)
            BTOK = ra(NCK * 8).rearrange("p (c h) -> p c h", c=NCK)
            NBTOK = ra(NCK * 8).rearrange("p (c h) -> p c h", c=NCK)
            KTOKs = [rab(NCK * 128).rearrange("p (c n) -> p c n", c=NCK) for _ in range(2)]
            VTOKs = [rab(NCK * 128).rearrange("p (c n) -> p c n", c=NCK) for _ in range(2)]
            WUs2 = [rab(NCK * 256).rearrange("p (c j n) -> p c j n", c=NCK, j=2) for _ in range(2)]
            QKTs2 = [rab(NCK * 128).rearrange("p (c n) -> p c n", c=NCK) for _ in range(2)]
            KPs2 = [rab(NCK * 128).rearrange("p (c n) -> p c n", c=NCK) for _ in range(2)]
            SMX2 = [ra(NCK * 3).rearrange("p (c j) -> p c j", c=NCK) for _ in range(2)]
            EGLS2 = [ra(NSB * NSQ), ra(NSB * NSQ)]
            RS = ra(1)
            nck = NCH if so else NCK

            def ccols(c):
                return slice(c * 128, (c + 1) * 128) if c < NCH else slice(NP + 32 * (c - NCH), NP + 32 * (c - NCH + 1))
            lanes = []
            for L in range(4):
                lanes.append(dict(Gm=ra(128), Es=ra(128), ET=ra(128), Nm=rab(128), NTt=rab(128),
                                  PP=[rab(256).rearrange("p (j n) -> p j n", j=2), rab(256).rearrange("p (j n) -> p j n", j=2)],
                                  RT=rab(128), KBG=rab(128), VB=rab(128), BEG=ra(1), GSEL=ra(NSQ)))
            VNb = rab(128); T2 = ra(128); Oo = ra(128); ON = ra(128); Sb = rab(128)
            WTpad = rab(NSQ * 32).rearrange("p (s i) -> p s i", s=NSQ)
            QNpad = rab(NSQ * 32).rearrange("p (s i) -> p s i", s=NSQ)
            KPpad = rab(NSQ * 128).rearrange("p (s n) -> p s n", s=NSQ)
            S0h = ra(NSQ * 128).rearrange("p (s n) -> p s n", s=NSQ)
            S0b = rab(NSQ * 128).rearrange("p (s n) -> p s n", s=NSQ)

            if so:
                tl = [3 * i + j for i in range(8) for j in (0, 1)] + [24 + 4 * h + j for h in range(8) for j in (0, 1, 2)]
            else:
                tl = list(range(56))
            tpos = {t: i for i, t in enumerate(tl)}
            wt0 = WTiles([win[t].rearrange("(k p) n -> p k n", p=128) for t in tl], 2048, 4,
                         lambda ap: ap.rearrange("p (k n) -> p k n", k=KC))

            class _WT:
                def get(self, t):
                    return wt0.get(tpos[t])

                def done(self, t):
                    wt0.done(tpos[t])
            wt = _WT()
            P.dma("pool", WAs[:], wa.rearrange("(k p) n -> p k n", p=128), [], bf("WAs"))
            P.dma("pool", WBs[:], wb.rearrange("(k p) n -> p k n", p=128), [], bf("WBs"))
            T1s = T1[:, NP:NP + NS].rearrange("p (s t) -> p s t", s=NSEQ)
            FLs = FL[:, NP:NP + NS].rearrange("p (s t) -> p s t", s=NSEQ)

            def conv(wts, ci, ntap):
                for j in range(ntap):
                    wj = wts[:, ci, j:j + 1]
                    if j == 0:
                        P.op("dve", lambda e, wj=wj: e.tensor_scalar(out=T1[:, 0:NP], in0=CIP[:, 0:NP], scalar1=wj,
                                                                    scalar2=None, op0=ALU.mult), [bf("CIN")], [bf("T1")])
                        if not so:
                            P.op("dve", lambda e, wj=wj: e.tensor_scalar(out=T1s, in0=CIS[:, :, 0:4], scalar1=wj,
                                                                        scalar2=None, op0=ALU.mult), [bf("CIN")], [bf("T1")])
                    else:
                        P.op("dve", lambda e, wj=wj, j=j: e.scalar_tensor_tensor(
                            out=T1[:, 0:NP], in0=CIP[:, j:j + NP], scalar=wj, in1=T1[:, 0:NP], op0=ALU.mult, op1=ALU.add),
                            [bf("CIN"), bf("T1")], [bf("T1")])
                        if not so:
                            P.op("dve", lambda e, wj=wj, j=j: e.scalar_tensor_tensor(
                                out=T1s, in0=CIS[:, :, j:j + 4], scalar=wj, in1=T1s, op0=ALU.mult, op1=ALU.add),
                                [bf("CIN"), bf("T1")], [bf("T1")])
                    yield

            def hist_proj(t, ncol, pa):
                w, wb_ = wt.get(t)
                fns = [lambda e, k=k, w=w: e.matmul(PTt[pa][:, 0, 0:ncol], lhsT=w[:, k, :], rhs=RB_h[:, k, NP - ncol:NP],
                                                    start=(k == 0), stop=(k == KC - 1)) for k in range(KC)]
                P.group("pe", fns, [wb_, bf("HB")], [*ptb(pa)])
                wt.done(t)

            for i in range(8 if so else 0):
                hist_proj(3 * i, 2, 0)
                P.op("act", lambda e: e.activation(out=TMP1[:, 0:2], in_=PTt[0][:, 0, 0:2], func=AF.Copy), [*ptb(0)], [bf("TMP1")])
                hist_proj(3 * i + 1, 2, 1)
                P.op("dve", lambda e, i=i: e.tensor_tensor(out=HISTA[:, i, :], in0=TMP1[:, 0:2], in1=PTt[1][:, 0, 0:2], op=ALU.mult),
                     [bf("TMP1"), *ptb(1)], [bf("HISTA")])
            for i in range(0 if so else 8):
                P.op("dve", lambda e, i=i: e.tensor_copy(out=CIP[:, 0:2], in_=HISTA[:, i, :]), [bf("HISTA")], [bf("CIN")])
                P.dma("sp", CIS[:, :, 0:2], hist_a[i], [], bf("CIN"))
                w, wb_ = wt.get(3 * i)
                proj_tile(w, wb_, 0)
                wt.done(3 * i)
                P.op("act", lambda e: e.activation(out=v3(TMP1[:]), in_=pt3(0), func=AF.Copy), [*ptb(0)], [bf("TMP1")])
                w, wb_ = wt.get(3 * i + 1)
                proj_tile(w, wb_, 1)
                wt.done(3 * i + 1)
                P.op("dve", lambda e: e.tensor_tensor(out=v3(FL[:]), in0=v3(TMP1[:]), in1=pt3(1), op=ALU.mult),
                     [bf("TMP1"), *ptb(1)], [bf("FL")])
                P.op("act", lambda e: e.activation(out=CIP[:, 2:2 + NP], in_=FL[:, 0:NP], func=AF.Copy), [bf("FL")], [bf("CIN")])
                P.op("dve", lambda e: e.tensor_copy(out=CIS[:, :, 2:6], in_=FLs), [bf("FL")], [bf("CIN")])
                for _ in conv(CWA, i, 3):
                    pass
                P.op("dve", lambda e, i=i: e.tensor_copy(out=HISTA[:, i, :], in_=CIP[:, NP:NP + 2]), [bf("CIN")], [bf("HISTA")])
                P.dma("sp", o_ca_s[i], CIS[:, :, 4:6], [bf("CIN")], bf("OCAS"))
                w, wb_ = wt.get(3 * i + 2)
                proj_tile(w, wb_, 0)
                wt.done(3 * i + 2)
                P.op("act", lambda e: e.activation(out=v3(RSTD[:]), in_=pt3(0), func=AF.Copy), [*ptb(0)], [bf("RSTD")])
                P.op("dve", lambda e, i=i: e.tensor_tensor(out=RB_m[:, i, :], in0=T1[:], in1=RSTD[:], op=ALU.mult),
                     [bf("T1"), bf("RSTD")], [bf("YMIX")])
            checkpoint("mixA", RB_y[:, 8:16, :], nk=8)

            for pa, wsm, nm in ((0, WAs, "WAs"), (1, WBs, "WBs")):
                fns = [lambda e, t=t, k=k, pa=pa, wsm=wsm: e.matmul(PTt[pa][0:8, t, 0:TGW], lhsT=wsm[:, k, :],
                                                                    rhs=RB_h[:, k, t * TGW:(t + 1) * TGW],
                                                                    start=(k == 0), stop=(k == KC - 1))
                       for t in range(3) for k in range(KC)]
                P.group("pe", fns, [bf(nm), bf("HB")], [*ptb(pa)])
            P.op("act", lambda e: e.activation(out=v3(GFM), in_=pt3(0, 8), func=AF.Exp, bias=DTB[:, 0:1]),
                 [*ptb(0), bf("DTB")], [bf("FL")])
            P.op("act", lambda e: e.activation(out=GFM, in_=GFM, func=AF.Ln, bias=ONESF[0:8, 0:1]),
                 [bf("FL"), bf("ONESF")], [bf("FL")])
            P.op("dve", lambda e: e.tensor_scalar(out=GFM, in0=GFM, scalar1=NEXPA[:, 0:1], scalar2=None,
                                                  op0=ALU.mult), [bf("FL"), bf("NEXPA")], [bf("FL")])
            P.op("act", lambda e: e.activation(out=v3(BFM), in_=pt3(1, 8), func=AF.Sigmoid), [*ptb(1)], [bf("T1")])
            for a, src, nm in ((0, GFM, "FL"), (1, BFM, "T1")):
                fns = [lambda e, c=c, a=a, src=src: e.transpose(out=PM[a][:, c * 8:(c + 1) * 8],
                                                                 in_=src[:, c * 128:(c + 1) * 128], identity=CM[0:8, 5, 0:8])
                       for c in range(NCH)]
                for b_ in range(0 if so else NSB):
                    fns.append(lambda e, a=a, src=src, b_=b_: e.transpose(out=PM[a][0:32, 64 + 8 * b_:72 + 8 * b_],
                                                                          in_=src[:, NP + 32 * b_:NP + 32 * b_ + 32],
                                                                          identity=CM[0:8, 5, 0:8]))
                P.group("pe", fns, [bf(nm), bf("CM")], [bk(6 + a)])
            for a, dst, nm in ((0, GTOK, "GTOK"), (1, BTOK, "BTOK")):
                P.op("dve", lambda e, a=a, dst=dst: e.tensor_copy(out=dst[:, 0:8, :],
                                                                  in_=PM[a][:, 0:64].rearrange("p (c h) -> p c h", c=8)),
                     [bk(6 + a)], [bf(nm)])
                if not so:
                    P.op("dve", lambda e, a=a, dst=dst: e.tensor_copy(out=dst[0:32, 8:8 + NSB, :],
                                                                      in_=PM[a][0:32, 64:64 + 8 * NSB].rearrange("p (c h) -> p c h", c=NSB)),
                         [bk(6 + a)], [bf(nm)])
            P.op("dve", lambda e: e.tensor_scalar(out=NBTOK[:, 0:8, :], in0=BTOK[:, 0:8, :], scalar1=-1.0, scalar2=None,
                                                  op0=ALU.mult), [bf("BTOK")], [bf("NBTOK")])
            if not so:
                P.op("dve", lambda e: e.tensor_scalar(out=NBTOK[0:32, 8:8 + NSB, :], in0=BTOK[0:32, 8:8 + NSB, :], scalar1=-1.0, scalar2=None,
                                                      op0=ALU.mult), [bf("BTOK")], [bf("NBTOK")])
            checkpoint("gb", RB_y[:, 8:16, :], nk=8)

            def stageP(h):
                par = h % 2
                par3 = h % 3
                for qi in range(3):
                    cch = qi * 8 + h
                    if so and qi == 0:
                        hist_proj(24 + 4 * h, 3, 0)
                        P.op("dve", lambda e, cch=cch: e.tensor_copy(out=HISTQ[:, cch, :], in_=PTt[0][:, 0, 0:3]), [*ptb(0)], [bf("HISTQ")])
                        yield
                        continue
                    P.op("dve", lambda e, cch=cch: e.tensor_copy(out=CIP[:, 0:3], in_=HISTQ[:, cch, :]), [bf("HISTQ")], [bf("CIN")])
                    if not so:
                        P.dma("sp", CIS[:, :, 0:3], hist_q[cch], [], bf("CIN"))
                    w, wb_ = wt.get(24 + 4 * h + qi)
                    pa = 0
                    proj_tile(w, wb_, pa)
                    wt.done(24 + 4 * h + qi)
                    yield
                    P.op("act", lambda e, pa=pa: e.activation(out=v3(FL[:]), in_=pt3(pa), func=AF.Copy), [*ptb(pa)], [bf("FL")])
                    yield
                    P.op("act", lambda e: e.activation(out=CIP[:, 3:3 + NP], in_=FL[:, 0:NP], func=AF.Copy), [bf("FL")], [bf("CIN")])
                    if not so:
                        P.op("dve", lambda e: e.tensor_copy(out=CIS[:, :, 3:7], in_=FLs), [bf("FL")], [bf("CIN")])
                    yield
                    yield from conv(CWQ, cch, 4)
                    P.op("dve", lambda e, cch=cch: e.tensor_copy(out=HISTQ[:, cch, :], in_=CIP[:, NP:NP + 3]), [bf("CIN")], [bf("HISTQ")])
                    if not so:
                        P.dma("sp", o_cq_s[cch], CIS[:, :, 4:7], [bf("CIN")], bf("OCQS"))
                    yield
                    if qi == 2:
                        P.op("act", lambda e: e.activation(out=VSb, in_=T1[:], func=AF.Silu), [bf("T1")], [bf("VSb")])
                        yield
                        continue
                    P.op("act", lambda e: e.activation(out=FL[:], in_=T1[:], func=AF.Silu), [bf("T1")], [bf("FL")])
                    yield
                    P.op("act", lambda e: e.activation(out=SQ[0][:], in_=FL[:], func=AF.Square), [bf("FL")], [bf("SQ0")])
                    yield
                    fns = [lambda e, t=t, pa=pa: e.matmul(PTt[pa][:, t, 0:TGW], lhsT=ONES1[:], rhs=SQ[0][:, t * TGW:(t + 1) * TGW],
                                                          start=True, stop=True) for t in range(3)]
                    P.group("pe", fns, [bf("SQ0")], [*ptb(pa)])
                    yield
                    rstd_from(pa)
                    yield
                    if qi == 0:
                        P.op("dve", lambda e, par3=par3: e.scalar_tensor_tensor(out=QNb[par3], in0=FL[:], scalar=128.0 ** -0.5, in1=RSTD[:],
                                                                             op0=ALU.mult, op1=ALU.mult),
                             [bf("FL"), bf("RSTD")], [bf("QNb%d" % par3)])
                    else:
                        P.op("dve", lambda e, par=par: e.tensor_tensor(out=KNb[par], in0=FL[:], in1=RSTD[:], op=ALU.mult),
                             [bf("FL"), bf("RSTD")], [bf("KNb%d" % par)])
                    yield
                if not so:
                    w, wb_ = wt.get(24 + 4 * h + 3)
                    proj_tile(w, wb_, 0)
                    wt.done(24 + 4 * h + 3)
                    yield
                    P.op("act", lambda e, par3=par3: e.activation(out=v3(ZS[par3][:]), in_=pt3(0), func=AF.Silu), [*ptb(0)], [bf("ZS%d" % par3)])
                    yield
                n = 0
                for src, snm, dst, dnm in ((KNb[par], "KNb%d" % par, KTOKs[par], "KTOK%d" % par), (VSb, "VSb", VTOKs[par], "VTOK%d" % par)):
                    for c in range(nck):
                        C = 128 if c < NCH else 32
                        cs = ccols(c)
                        bi = n % 3
                        n += 1
                        P.op("pe", lambda e, src=src, cs=cs, bi=bi, C=C: e.transpose(out=BKb(bi)[0:C, 0:128], in_=src[:, cs], identity=IDB[:]),
                             [bf(snm), bf("IDB")], [bk(bi)])
                        if n % 2:
                            P.op("act", lambda e, dst=dst, c=c, bi=bi, C=C: e.activation(out=dst[0:C, c, :], in_=BKb(bi)[0:C, 0:128], func=AF.Copy),
                                 [bk(bi)], [bf(dnm)])
                        else:
                            P.op("dve", lambda e, dst=dst, c=c, bi=bi, C=C: e.tensor_copy(out=dst[0:C, c, :], in_=BKb(bi)[0:C, 0:128]),
                                 [bk(bi)], [bf(dnm)])
                        yield

            def prep(h, c, L):
                par = h % 2
                par3 = h % 3
                WUs, QKTs, KPs, SMX, EGLS = WUs2[par], QKTs2[par], KPs2[par], SMX2[par], EGLS2[par]
                WTs, Us = WUs[:, :, 0, :], WUs[:, :, 1, :]
                sfx_ = str(par)
                ln = lanes[L]
                nm = lambda x: "%s_%d" % (x, L)
                bL = 3 + L
                KTOK, VTOK = KTOKs[par], VTOKs[par]
                ktn, vtn, knn = "KTOK%d" % par, "VTOK%d" % par, "KNb%d" % par
                KNbp = KNb[par]
                RGA = BKf(bL)[:, 0:128]
                RGB = BKf(bL)[:, 128:256]
                RGAB = BKf(bL)[:, 0:256].rearrange("p (j n) -> p j n", j=2)
                RGC = BKf(bL)[:, 256:384]
                RGCb = BKb(bL)[:, 512:640]
                samp = c >= NCH
                C = 32 if samp else 128
                cs = ccols(c)
                if samp:
                    mTRI, mSFX, mNEGS, mNEGT = [SM[:, i, :] for i in range(4)]
                    mnm = "SM"
                else:
                    mTRI, mSFX, mNEGS, mNEGT = [CM[:, i, :] for i in range(4)]
                    mnm = "CM"
                OF = ONESF[0:C, 0:C]
                NOF = NONESF[0:C, 0:C]
                gcol = GTOK[0:C, c, h:h + 1]
                bcol = BTOK[0:C, c, h:h + 1]
                nbcol = NBTOK[0:C, c, h:h + 1]
                Gm, Es, ET, Nm, NTt, RT, KBG, VB = [ln[k][0:C, 0:C] if k not in ("KBG", "VB") else ln[k][0:C, :]
                                                    for k in ("Gm", "Es", "ET", "Nm", "NTt", "RT", "KBG", "VB")]
                BEG, GSEL = ln["BEG"][0:C, :], ln["GSEL"][0:C, :]
                IDf = CM[0:C, 5, 0:C]
                P.op("dve", lambda e: e.tensor_scalar(out=Gm, in0=mTRI, scalar1=gcol, scalar2=None, op0=ALU.mult),
                     [bf(mnm), bf("GTOK")], [bf(nm("Gm"))])
                if samp:
                    P.op("dve", lambda e: e.tensor_scalar(out=GSEL, in0=SEQSEL[:], scalar1=gcol, scalar2=None, op0=ALU.mult),
                         [bf("SEQSEL"), bf("GTOK")], [bf(nm("GSEL"))])
                yield
                D_ = RGA[0:C, 0:C]
                DT_ = RGB[0:C, 0:C]
                sm_ = BKf(bL)[:, 384:384 + 2 + NSQ]
                fns = [lambda e: e.matmul(D_, lhsT=Gm, rhs=OF, start=True, stop=False),
                       lambda e: e.matmul(D_, lhsT=NOF, rhs=Gm, start=False, stop=False),
                       lambda e: e.matmul(D_, lhsT=IDf, rhs=mNEGS, start=False, stop=True)]
                if not so:
                    fns += [lambda e: e.matmul(DT_, lhsT=OF, rhs=Gm, start=True, stop=False),
                            lambda e: e.matmul(DT_, lhsT=Gm, rhs=NOF, start=False, stop=False),
                            lambda e: e.matmul(DT_, lhsT=IDf, rhs=mNEGT, start=False, stop=True)]
                fns += [lambda e: e.matmul(sm_[0:C, 0:1], lhsT=mTRI, rhs=gcol, start=True, stop=True),
                        lambda e: e.matmul(sm_[0:C, 1:2], lhsT=mSFX, rhs=gcol, start=True, stop=True)]
                if samp:
                    fns.append(lambda e: e.matmul(sm_[:, 2:2 + NSQ], lhsT=ONESF[0:C, :], rhs=GSEL, start=True, stop=True))
                else:
                    fns.append(lambda e: e.matmul(sm_[:, 2:3], lhsT=ONESF[0:C, :], rhs=gcol, start=True, stop=True))
                P.group("pe", fns, [bf(nm("Gm")), bf(nm("GSEL")), bf(mnm), bf("GTOK")], [bk(bL)])
                yield
                P.op("act", lambda e: e.activation(out=Es, in_=D_, func=AF.Exp), [bk(bL)], [bf(nm("Es"))])
                yield
                if not so:
                    P.op("act", lambda e: e.activation(out=ET, in_=DT_, func=AF.Exp), [bk(bL)], [bf(nm("ET"))])
                    yield
                if samp:
                    P.op("act", lambda e: e.activation(out=SMX[0:C, c, 0:2], in_=sm_[0:C, 0:2], func=AF.Exp), [bk(bL)], [bf("SMX" + sfx_)])
                    P.op("act", lambda e: e.activation(out=EGLS[:, (c - NCH) * NSQ:(c - NCH + 1) * NSQ], in_=sm_[:, 2:2 + NSQ], func=AF.Exp),
                         [bk(bL)], [bf("EGLS" + sfx_)])
                else:
                    P.op("act", lambda e: e.activation(out=SMX[:, c, :], in_=sm_[:, 0:3], func=AF.Exp), [bk(bL)], [bf("SMX" + sfx_)])
                yield
                P.op("dve", lambda e: e.tensor_tensor(out=BEG, in0=SMX[0:C, c, 0:1], in1=bcol, op=ALU.mult),
                     [bf("SMX" + sfx_), bf("BTOK")], [bf(nm("BEG"))])
                P.op("pe", lambda e: e.matmul(RGC[0:C, 0:C], lhsT=KNbp[:, cs], rhs=KNbp[:, cs], start=True, stop=True),
                     [bf(knn)], [bk(bL)])
                yield
                P.op("dve", lambda e: e.scalar_tensor_tensor(out=Nm, in0=RGC[0:C, 0:C], scalar=nbcol, in1=Es, op0=ALU.mult, op1=ALU.mult),
                     [bk(bL), bf("NBTOK"), bf(nm("Es"))], [bf(nm("Nm"))])
                yield
                P.op("pe", lambda e: e.transpose(out=RGCb[0:C, 0:C], in_=Nm, identity=IDB[0:C, 0:C]), [bf(nm("Nm")), bf("IDB")], [bk(bL)])
                yield
                P.op("act", lambda e: e.activation(out=NTt, in_=RGCb[0:C, 0:C], func=AF.Copy), [bk(bL)], [bf(nm("NTt"))])
                P.op("dve", lambda e: e.tensor_tensor(out=RT, in0=RGCb[0:C, 0:C], in1=IDB[0:C, 0:C], op=ALU.add),
                     [bk(bL), bf("IDB")], [bf(nm("RT"))])
                yield
                Pc, PTc, pcn, ptn = Nm, NTt, nm("Nm"), nm("NTt")
                nl = 1 if samp else 6
                for l in range(nl):
                    last = l == nl - 1
                    PPl = ln["PP"][l % 2]
                    Pn, PTn = PPl[0:C, 0, 0:C], PPl[0:C, 1, 0:C]
                    pnn = ptnn = nm("PP%d" % (l % 2))
                    fns = [lambda e, Pc=Pc, PTc=PTc: e.matmul(RGA[0:C, 0:C], lhsT=PTc, rhs=Pc, start=True, stop=True)]
                    if not last:
                        fns.append(lambda e, Pc=Pc, PTc=PTc: e.matmul(RGB[0:C, 0:C], lhsT=Pc, rhs=PTc, start=True, stop=True))
                    P.group("pe", fns, [bf(pcn), bf(ptn)], [bk(bL)])
                    yield
                    if last:
                        P.op("act", lambda e, Pn=Pn: e.activation(out=Pn, in_=RGA[0:C, 0:C], func=AF.Copy), [bk(bL)], [bf(pnn)])
                    else:
                        P.op("act", lambda e, PPl=PPl: e.activation(out=PPl[:, :, :], in_=RGAB,
                                                                    func=AF.Copy), [bk(bL)], [bf(pnn)])
                    yield
                    P.op("pe", lambda e, Pn=Pn: e.matmul(RGC[0:C, 0:C], lhsT=Pn, rhs=RT, start=True, stop=True),
                         [bf(pnn), bf(nm("RT"))], [bk(bL)])
                    yield
                    P.op("dve", lambda e: e.tensor_tensor(out=RT, in0=RGC[0:C, 0:C], in1=RT, op=ALU.add),
                         [bk(bL), bf(nm("RT"))], [bf(nm("RT"))])
                    yield
                    Pc, PTc, pcn, ptn = Pn, PTn, pnn, ptnn
                P.op("dve", lambda e: e.tensor_scalar(out=KBG, in0=KTOK[0:C, c, :], scalar1=BEG[:, 0:1], scalar2=None, op0=ALU.mult),
                     [bf(ktn), bf(nm("BEG"))], [bf(nm("KBG"))])
                P.op("dve", lambda e: e.tensor_scalar(out=VB, in0=VTOK[0:C, c, :], scalar1=bcol, scalar2=None, op0=ALU.mult),
                     [bf(vtn), bf("BTOK")], [bf(nm("VB"))])
                yield
                P.op("dve", lambda e: e.tensor_scalar(out=KPs[0:C, c, :], in0=KTOK[0:C, c, :], scalar1=SMX[0:C, c, 1:2], scalar2=None, op0=ALU.mult),
                     [bf(ktn), bf("SMX" + sfx_)], [bf("KPs" + sfx_)])
                fns = [lambda e: e.matmul(RGA[:, 0:C], lhsT=KBG, rhs=RT, start=True, stop=True),
                       lambda e: e.matmul(RGB[0:C, :], lhsT=RT, rhs=VB, start=True, stop=True)]
                P.group("pe", fns, [bf(nm("KBG")), bf(nm("VB")), bf(nm("RT"))], [bk(bL)])
                if not so:
                    P.op("pe", lambda e: e.matmul(RGC[0:C, 0:C], lhsT=KNbp[:, cs], rhs=QNb[par3][:, cs], start=True, stop=True),
                         [bf(knn), bf("QNb%d" % par3)], [bk(bL)])
                yield
                if samp:
                    P.op("act", lambda e: e.activation(out=WTs[:, c, 0:C], in_=RGA[:, 0:C], func=AF.Copy), [bk(bL)], [bf("WTs" + sfx_)])
                    P.op("act", lambda e: e.activation(out=Us[0:C, c, :], in_=RGB[0:C, :], func=AF.Copy), [bk(bL)], [bf("Us" + sfx_)])
                else:
                    P.op("act", lambda e: e.activation(out=WUs[:, c, :, :], in_=RGAB, func=AF.Copy),
                         [bk(bL)], [bf("WTs" + sfx_), bf("Us" + sfx_)])
                if not so:
                    P.op("dve", lambda e: e.tensor_tensor(out=QKTs[0:C, c, 0:C], in0=RGC[0:C, 0:C], in1=ET, op=ALU.mult),
                         [bk(bL), bf(nm("ET"))], [bf("QKTs" + sfx_)])
                yield

            def scan(h):
                par = h % 3
                par2 = h % 2
                WUs, QKTs, KPs, SMX, EGLS = WUs2[par2], QKTs2[par2], KPs2[par2], SMX2[par2], EGLS2[par2]
                WTs, Us = WUs[:, :, 0, :], WUs[:, :, 1, :]
                sfx_ = str(par2)
                S = SST[:, h, :]
                B7 = BKf(7)
                P.op("act", lambda e: e.activation(out=Sb, in_=S, func=AF.Copy), [bf("SST")], [bf("Sb")])
                yield
                pend = None

                def outpath(c, C, cs):
                    P.op("act", lambda e: e.activation(out=ON[0:C, :], in_=Oo[0:C, :], func=AF.Square, accum_out=SSQ[0:C, 0:1]),
                         [bf("O")], [bf("ON"), bf("SSQ")])
                    P.op("act", lambda e: e.activation(out=SSQ[0:C, 1:2], in_=SSQ[0:C, 0:1], func=AF.Sqrt, bias=EPSB[0:C, 0:1], scale=1.0 / 128),
                         [bf("SSQ"), bf("EPSB")], [bf("SSQ")])
                    yield
                    P.op("dve", lambda e: e.reciprocal(out=RS[0:C, :], in_=SSQ[0:C, 1:2]), [bf("SSQ")], [bf("RS")])
                    P.op("dve", lambda e: e.tensor_scalar(out=ON[0:C, :], in0=Oo[0:C, :], scalar1=RS[0:C, 0:1], scalar2=None, op0=ALU.mult),
                         [bf("O"), bf("RS")], [bf("ON")])
                    yield
                    P.op("pe", lambda e: e.transpose(out=B7[:, 0:C], in_=ON[0:C, :], identity=CM[0:C, 5, 0:C]), [bf("ON"), bf("CM")], [bk(7)])
                    yield
                    P.op("dve", lambda e: e.scalar_tensor_tensor(out=RB_m[:, 8 + h, cs], in0=B7[:, 0:C], scalar=DNN[:, 0:1],
                                                                 in1=ZS[par][:, cs], op0=ALU.mult, op1=ALU.mult),
                         [bk(7), bf("DNN"), bf("ZS%d" % par)], [bf("YMIX")])
                    yield

                for c in range(nck):
                    samp = c >= NCH
                    sbi = c - NCH
                    C = 32 if samp else 128
                    cs = ccols(c)
                    if not samp:
                        fns = [lambda e, c=c: e.matmul(B7[:, 0:128], lhsT=WTs[:, c, :], rhs=Sb, start=True, stop=True)]
                        if not so:
                            fns.append(lambda e, cs=cs: e.matmul(B7[:, 128:256], lhsT=QNb[par][:, cs], rhs=Sb, start=True, stop=True))
                        P.group("pe", fns, [bf("WTs" + sfx_), bf("QNb%d" % par), bf("Sb")], [bk(7)])
                    else:
                        P.dma("sp", S0h[:], s0[sbi * NSQ:(sbi + 1) * NSQ, h].rearrange("s p n -> p s n"), [], bf("S0h"))
                        P.op("act", lambda e: e.activation(out=S0b[:], in_=S0h[:], func=AF.Copy), [bf("S0h")], [bf("S0b")])
                        P.op("dve", lambda e, c=c: e.tensor_tensor(out=WTpad[:, :, :], in0=WTs[:, c:c + 1, 0:32].to_broadcast([128, NSQ, 32]),
                                                                   in1=SEQMB[:, :, :], op=ALU.mult),
                             [bf("WTs" + sfx_), bf("SEQMB")], [bf("WTpad")])
                        yield
                        P.op("dve", lambda e, cs=cs: e.tensor_tensor(out=QNpad[:, :, :], in0=QNb[par][:, cs].unsqueeze(1).to_broadcast([128, NSQ, 32]),
                                                                     in1=SEQMB[:, :, :], op=ALU.mult),
                             [bf("QNb%d" % par), bf("SEQMB")], [bf("QNpad")])
                        yield
                        P.op("dve", lambda e, c=c: e.tensor_tensor(out=KPpad[0:32, :, :], in0=KPs[0:32, c:c + 1, :].to_broadcast([32, NSQ, 128]),
                                                                   in1=SEQSEL[:, :].unsqueeze(2).to_broadcast([32, NSQ, 128]), op=ALU.mult),
                             [bf("KPs" + sfx_), bf("SEQSEL")], [bf("KPpad")])
                        yield
                        fns = [lambda e, s_=s_: e.matmul(B7[0:32, 0:128], lhsT=WTpad[:, s_, :], rhs=S0b[:, s_, :], start=(s_ == 0), stop=(s_ == NSQ - 1))
                               for s_ in range(NSQ)]
                        fns += [lambda e, s_=s_: e.matmul(B7[0:32, 128:256], lhsT=QNpad[:, s_, :], rhs=S0b[:, s_, :], start=(s_ == 0), stop=(s_ == NSQ - 1))
                                for s_ in range(NSQ)]
                        P.group("pe", fns, [bf("WTpad"), bf("QNpad"), bf("S0b")], [bk(7)])
                    yield
                    P.op("dve", lambda e, c=c, C=C: e.tensor_tensor(out=VNb[0:C, :], in0=Us[0:C, c, :], in1=B7[0:C, 0:128], op=ALU.subtract),
                         [bf("Us" + sfx_), bk(7)], [bf("VNb")])
                    yield
                    if not samp:
                        fns = [lambda e, c=c: e.matmul(B7[:, 256:384], lhsT=KPs[:, c, :], rhs=VNb[:, :], start=True, stop=True)]
                        if not so:
                            fns.append(lambda e, c=c: e.matmul(B7[:, 384:512], lhsT=QKTs[:, c, :], rhs=VNb[:, :], start=True, stop=True))
                        P.group("pe", fns, [bf("KPs" + sfx_), bf("QKTs" + sfx_), bf("VNb")], [bk(7)])
                        yield
                        P.op("dve", lambda e, c=c: e.scalar_tensor_tensor(out=S, in0=S, scalar=SMX[:, c, 2:3], in1=B7[:, 256:384],
                                                                          op0=ALU.mult, op1=ALU.add),
                             [bf("SST"), bf("SMX" + sfx_), bk(7)], [bf("SST")])
                        yield
                        P.op("act", lambda e: e.activation(out=Sb, in_=S, func=AF.Copy), [bf("SST")], [bf("Sb")])
                        yield
                        if so:
                            continue
                    else:
                        P.op("pe", lambda e, c=c: e.matmul(B7[0:32, 384:512], lhsT=QKTs[0:32, c, 0:32], rhs=VNb[0:32, :], start=True, stop=True),
                             [bf("QKTs" + sfx_), bf("VNb")], [bk(7)])
                        yield
                    P.op("act", lambda e, C=C: e.activation(out=T2[0:C, :], in_=B7[0:C, 384:512], func=AF.Copy), [bk(7)], [bf("T2")])
                    yield
                    if pend is not None:
                        yield from pend
                    P.op("dve", lambda e, c=c, C=C: e.scalar_tensor_tensor(out=Oo[0:C, :], in0=B7[0:C, 128:256], scalar=SMX[0:C, c, 0:1],
                                                                           in1=T2[0:C, :], op0=ALU.mult, op1=ALU.add),
                         [bk(7), bf("SMX" + sfx_), bf("T2")], [bf("O")])
                    yield
                    pend = outpath(c, C, cs)
                    if samp:
                        for s_ in range(NSQ):
                            P.op("pe", lambda e, s_=s_: e.matmul(B7[:, 256:384], lhsT=KPpad[0:32, s_, :], rhs=VNb[0:32, :], start=True, stop=True),
                                 [bf("KPpad"), bf("VNb")], [bk(7)])
                            P.op("dve", lambda e, s_=s_, sbi=sbi: e.scalar_tensor_tensor(out=S0h[:, s_, :], in0=S0h[:, s_, :],
                                                                                         scalar=EGLS[:, sbi * NSQ + s_:sbi * NSQ + s_ + 1],
                                                                                         in1=B7[:, 256:384], op0=ALU.mult, op1=ALU.add),
                                 [bf("S0h"), bf("EGLS" + sfx_), bk(7)], [bf("S0h")])
                            yield
                        P.dma("sp", o_dl_s[sbi * NSQ:(sbi + 1) * NSQ, h].rearrange("s p n -> p s n"), S0h[:], [bf("S0h")], bf("ODLS"))
                if pend is not None:
                    yield from pend

            def poolB(h):
                yield from pool_gen([(lambda L, c=c: prep(h, c, L)) for c in (list(range(NCH, nck)) + list(range(NCH)))], 4, _TUNE[3])

            for r in range(10):
                gl = []
                if r - 2 >= 0:
                    gl.append((scan(r - 2), _TUNE[0]))
                if 0 <= r - 1 < 8:
                    gl.append((poolB(r - 1), _TUNE[1]))
                if r < 8:
                    gl.append((stageP(r), _TUNE[2]))
                run_weighted(gl)

        def checkpoint(label, src3=None, nk=KC, parts=128):
            if stop != label:
                return
            P.barrier()
            src3 = RA_x if src3 is None else src3
            for k in range(nk):
                P.dma("sp", dbg[k * 128:k * 128 + parts, :], src3[0:parts, k, :], [], bf("DBG"))
            P.barrier()
            raise _Stop()

        def checkpoint2(label, aps):
            if stop != label:
                return
            P.barrier()
            for i, ap in enumerate(aps):
                n = ap.shape[-1]
                if ap.dtype != F32:
                    raise ValueError("fp32 only")
                P.dma("sp", dbg[i * 128:i * 128 + ap.shape[0], 0:n], ap, [], bf("DBG"))
            P.barrier()
            raise _Stop()

        try:
          for g in range(NG):
              for q4 in range(4):
                  P.dma("sp", RA_x[:, q4 * 4:(q4 + 1) * 4, :],
                        xT[g, q4 * 512:(q4 + 1) * 512, :].rearrange("(k p) n -> p k n", p=128), [], bf("SRC"))
              checkpoint("load")
              t1 = ffn_tiles(wgu1)
              prenorm(0)
              P.barrier()
              ffn(wgu1, wd1, t1)
              checkpoint("y1", RB_y)
              post_residual(RB_y, 1, 0.5, xT[g], g, next_stats=True)
              P.barrier()
              checkpoint("x1")
              prenorm(2, have_stats=True)
              P.barrier()
              so = g == 0
              mixer(g, so)
              P.barrier()
              if so:
                  for ap, nm_ in ((SST[:], "SST"), (HISTA[:], "HISTA"), (HISTQ[:], "HISTQ")):
                      P.op("dve", lambda e, ap=ap: e.tensor_scalar(out=ap, in0=ap, scalar1=CARRY[:, 0:1], scalar2=None, op0=ALU.mult),
                           [bf(nm_), bf("CARRY")], [bf(nm_)])
                  P.barrier()
                  continue
              checkpoint("ymix", RB_y[:, 8:16, :], nk=8)
              t3 = WTiles([wout[m].rearrange("(k p) n -> p k n", p=128) for m in range(KC)], 2048, 4,
                          lambda ap: ap.rearrange("p (k n) -> p k n", k=KC))
              for m in range(KC):
                  w, wbuf = t3.get(m)
                  pa = m % 2
                  fns = [lambda e, t=t, k=k, w=w, pa=pa: e.matmul(PTt[pa][:, t, 0:TGW], lhsT=w[:, k, :],
                                                                 rhs=RB_m[:, k, t * TGW:(t + 1) * TGW],
                                                                 start=(k == 0), stop=(k == KC - 1))
                         for t in range(3) for k in range(KC)]
                  P.group("pe", fns, [wbuf, bf("YMIX")], [*ptb(pa)])
                  t3.done(m)
                  P.op("act", lambda e, m=m, pa=pa: e.activation(out=v3(RA_x[:, m, :]), in_=pt3(pa), func=AF.Copy),
                       [*ptb(pa)], [bf("SRC")])
              P.barrier()
              t1 = ffn_tiles(wgu2)
              post_residual(RA_x, 3, 1.0, xs_d, g, next_stats=True)
              P.barrier()
              checkpoint("x2")
              prenorm(4, have_stats=True)
              P.barrier()
              ffn(wgu2, wd2, t1)
              post_residual(RB_y, 5, 0.5, xs_d, g, next_stats=True)
              P.barrier()
              checkpoint("x3")
              prenorm(6, have_stats=True)
              P.barrier()
              PB = RB_m[:, 0:2, :]
              P.dma("pool", PB, pT.rearrange("(k p) n -> p k n", p=128), [], bf("PB"))
              t4 = WTiles([wpg[m].rearrange("(k p) n -> p k n", p=128) for m in range(KC)], 2048, 2,
                          lambda ap: ap.rearrange("p (k n) -> p k n", k=KC))
              WPP = WRb[:, 4096:4096 + KC * 256].rearrange("p (m k n) -> p m k n", m=KC, k=2)
              P.dma("pool", WPP, wpp.rearrange("m (k p) n -> p m k n", p=128), [], bf("WPP"))
              for m in range(KC):
                  w, wbuf = t4.get(m)
                  proj_tile(w, wbuf, 0)
                  t4.done(m)
                  fns = [lambda e, t=t, k=k, m=m: e.matmul(PTt[1][:, t, 0:TGW], lhsT=WPP[:, m, k, :],
                                                          rhs=PB[:, k, t * TGW:(t + 1) * TGW], start=(k == 0), stop=(k == 1))
                         for t in range(3) for k in range(2)]
                  P.group("pe", fns, [bf("WPP"), bf("PB")], [*ptb(1)])
                  P.op("act", lambda e: e.activation(out=v3(TMP1[:]), in_=pt3(0), func=AF.Sigmoid), [*ptb(0)], [bf("TMP1")])
                  P.op("dve", lambda e, m=m: e.tensor_tensor(out=v3(RA_x[:, m, :]), in0=v3(TMP1[:]), in1=pt3(1), op=ALU.mult),
                       [bf("TMP1"), *ptb(1)], [bf("SRC")])
              P.barrier()
              post_residual(RA_x, 7, 1.0, xs_d, g, last=True)
              P.barrier()

        except _Stop:
            pass
        P.dma("sp", o_ca_p.rearrange("c p t -> p c t"), HISTA[:], [bf("HISTA")], bf("O1"))
        P.dma("sp", o_cq_p.rearrange("c p t -> p c t"), HISTQ[:], [bf("HISTQ")], bf("O2"))
        P.dma("sp", o_dl_p.rearrange("h p n -> p h n"), SST[:], [bf("SST")], bf("O3"))
        P.barrier()
        P.wait_all("sp", [bf("O1"), bf("O2"), bf("O3"), bf("DBG"), bf("OCAS"), bf("OCQS"), bf("ODLS")] + [bf("YOUT%d" % m) for m in range(4)])
        P.emit()
    return nc


_CACHE = {}
_TUNE = [2, 2, 1, 6]
_DEBUG = {}


def _masks():
    def mk(C, blk):
        idx = np.arange(C)
        same = (idx[:, None] // blk) == (idx[None, :] // blk)
        tri = ((idx[:, None] <= idx[None, :]) & same).astype(np.float32)
        sfx = ((idx[:, None] > idx[None, :]) & same).astype(np.float32)
        negi = np.where((idx[None, :] < idx[:, None]) & same, 0.0, NEG).astype(np.float32)
        negt = np.where((idx[:, None] <= idx[None, :]) & same, 0.0, NEG).astype(np.float32)
        strict = ((idx[None, :] < idx[:, None]) & same).astype(np.float32)
        return tri, sfx, negi, negt, strict
    c = list(mk(128, 128)) + [np.eye(128, dtype=np.float32)]
    sm = list(mk(32, 4))
    seqsel = (np.arange(32)[:, None] // 4 == np.arange(NSQ)[None, :]).astype(np.float32)
    seqmb = np.broadcast_to(seqsel.T[None], (128, NSQ, 32)).astype(np.float32).copy()
    return np.stack(c), np.stack(sm), seqsel, seqmb


def kernel(x_prompt, x_sample, state_conv_a, state_conv_qkv, state_delta, p_prompt, p_sample,
           f1_pre, f1_post, f1_wg, f1_wu, f1_wd,
           mix_pre, mix_post, w_in, conv_a_w, conv_qkv_w, a_log, dt_bias, dn_norm, w_out,
           f2_pre, f2_post, f2_wg, f2_wu, f2_wd,
           ple_pre, ple_post, w_ple_gate, w_ple_proj):
    f = lambda a: np.ascontiguousarray(np.asarray(a, dtype=np.float32))
    x_prompt, x_sample, p_prompt, p_sample = f(x_prompt), f(x_sample), f(p_prompt)[0], f(p_sample)[0]
    sca, scq, sdl = f(state_conv_a)[0], f(state_conv_qkv)[0], f(state_delta)[0]

    def pack_gu(wg, wu):
        wg, wu = f(wg)[0], f(wu)[0]
        return f(np.concatenate([wg.reshape(D, JC, 128), wu.reshape(D, JC, 128)], axis=2).transpose(1, 0, 2))

    def pack_d(wd):
        wd = f(wd)[0]
        t = wd.reshape(4, 11, 128, KC, 128)
        return f(t.transpose(3, 0, 2, 1, 4).reshape(KC, 4, 128, 11 * 128))

    def coltiles(w, n=128):
        w = f(w)
        return f(w.reshape(w.shape[0], w.shape[1] // n, n).transpose(1, 0, 2))

    wi = f(w_in)[0]
    Bc, Cc, Hc, Qc, Kc, Vc, Zc = [coltiles(wi[:, o:o + 1024]) for o in range(0, 7168, 1024)]
    tiles = []
    for i in range(8):
        tiles += [Cc[i], Hc[i], Bc[i]]
    for h in range(8):
        tiles += [Qc[h], Kc[h], Vc[h], Zc[h]]
    win = f(np.stack(tiles))
    cm, sm, seqsel, seqmb = _masks()
    vecs = [f1_pre, f1_post, mix_pre, mix_post, f2_pre, f2_post, ple_pre, ple_post]
    gains = f(np.stack([f(v)[0].reshape(KC, 128).T for v in vecs], axis=1))
    shared = dict(
        wgu1=pack_gu(f1_wg, f1_wu), wd1=pack_d(f1_wd), wgu2=pack_gu(f2_wg, f2_wu), wd2=pack_d(f2_wd),
        win=win, wa=f(wi[:, 7168:7176]), wb=f(wi[:, 7176:7184]),
        wout=coltiles(f(w_out)[0]), wpg=coltiles(f(w_ple_gate)[0]), wpp=coltiles(f(w_ple_proj)[0]),
        gains=gains,
        cw_a=f(f(conv_a_w)[0].reshape(3, 8, 128).transpose(2, 1, 0)),
        cw_q=f(f(conv_qkv_w)[0].reshape(4, 24, 128).transpose(2, 1, 0)),
        alog=f(a_log)[0], dtb=f(dt_bias)[0], dnn=f(f(dn_norm)[0].reshape(128, 1)),
        cmask=cm, smask=sm, seqsel=seqsel, seqmb=seqmb,
    )
    in_maps = []
    for c in range(8):
        seq, half = c % 4, c // 4
        sq = slice(c * NSEQ, (c + 1) * NSEQ)
        z65 = np.zeros((NS + 1, D), np.float32)
        x0 = x_prompt[seq, 0:NP] if half == 1 else np.zeros((NP, D), np.float32)
        x1 = x_prompt[seq, half * NP:(half + 1) * NP]
        xs = [np.concatenate([x0, z65], axis=0).T,
              np.concatenate([x1, x_sample[sq].reshape(NS, D), np.zeros((1, D), np.float32)], axis=0).T]
        pp = np.concatenate([p_prompt[seq, half * NP:(half + 1) * NP], p_sample[sq].reshape(NS, 256),
                             np.zeros((1, 256), np.float32)], axis=0).T
        m = dict(shared)
        m.update(xT=f(np.stack(xs)), pT=f(pp),
                 hist_a=f(sca[sq].reshape(NSEQ, 2, 8, 128).transpose(2, 3, 0, 1)),
                 hist_q=f(scq[sq].reshape(NSEQ, 3, 24, 128).transpose(2, 3, 0, 1)),
                 s0=f(sdl[sq]), carry=np.full((128, 1), float(half), np.float32))
        in_maps.append(m)

    if _DEBUG.get("return_maps"):
        return in_maps
    if "nc" not in _CACHE:
        _CACHE["nc"] = build_program()
    res = run_bass_kernel_spmd(_CACHE["nc"], in_maps, core_ids=list(range(8)))
    R = res.results

    y_prompt = np.zeros((4, NG * NP, D), np.float32)
    y_sample = np.zeros((128, 4, D), np.float32)
    ca_p = np.zeros((1, 4, 2, 1024), np.float32)
    cq_p = np.zeros((1, 4, 3, 3072), np.float32)
    dl_p = np.zeros((1, 4, 8, 128, 128), np.float32)
    ca_s = np.zeros((1, 128, 2, 1024), np.float32)
    cq_s = np.zeros((1, 128, 3, 3072), np.float32)
    dl_s = np.zeros((1, 128, 8, 128, 128), np.float32)
    for c in range(8):
        r = R[c]
        seq, half = c % 4, c // 4
        sq = slice(c * NSEQ, (c + 1) * NSEQ)
        yt = np.asarray(r["yT"])
        y_prompt[seq, half * NP:(half + 1) * NP] = yt[:, 0:NP].T
        y_sample[sq] = yt[:, NP:NP + NS].T.reshape(NSEQ, 4, D)
        ca_s[0, sq] = np.asarray(r["o_ca_s"]).transpose(2, 3, 0, 1).reshape(NSEQ, 2, 1024)
        cq_s[0, sq] = np.asarray(r["o_cq_s"]).transpose(2, 3, 0, 1).reshape(NSEQ, 3, 3072)
        dl_s[0, sq] = np.asarray(r["o_dl_s"])
        if half == 1:
            ca_p[0, seq] = np.asarray(r["o_ca_p"]).transpose(2, 0, 1).reshape(2, 1024)
            cq_p[0, seq] = np.asarray(r["o_cq_p"]).transpose(2, 0, 1).reshape(3, 3072)
            dl_p[0, seq] = np.asarray(r["o_dl_p"])
    return (y_prompt, y_sample, ca_p, cq_p, dl_p, ca_s, cq_s, dl_s)
```
